# Optimizing a Trainium2 kernel written in Bass

```python
import math
import jax, jax.numpy as jnp
from jax import lax
import numpy as np

D_MODEL = 1024
BATCH = 4
SEQ = 4096
DEPTH = 2

D_MIX = D_MODEL
D_HALF = D_MIX // 2
S5_GROUP = 16
S5_GROUPS = D_HALF // S5_GROUP
S5_STATE = 64
S5_DT_MIN = 0.001
S5_DT_MAX = 0.1
SGU_CHUNK = 128
SGU_HEADS = 4
SGU_HEAD_DIM = D_HALF // SGU_HEADS
POOL_WINDOWS = (2, 4, 8, 16)
POOL_GROUP_DIM = D_HALF // len(POOL_WINDOWS)
HGRN_HEAD_DIM = 128
HGRN_HEADS = D_HALF // HGRN_HEAD_DIM
HGRN_CHUNK = 64
D_FF = 2816
N_EVEN = (DEPTH + 1) // 2
N_ODD = DEPTH // 2
EVEN_IN = 3 * D_HALF
ODD_IN = 5 * D_HALF
EPS = 1e-6

kernel_name = 'hybrid_s5_sgu_pool_hgrn2_macaron'

F32 = jnp.float32


def rmsnorm(x, g):
    xf = x.astype(F32)
    y = xf * lax.rsqrt(jnp.mean(xf * xf, axis=-1, keepdims=True) + EPS)
    return (y * g.astype(F32)).astype(x.dtype)


def swiglu(h, wg, wu, wd):
    return (jax.nn.silu(h @ wg) * (h @ wu)) @ wd


def _complex_affine_combine(earlier, later):
    a1r, a1i, b1r, b1i = earlier
    a2r, a2i, b2r, b2i = later
    return (a2r * a1r - a2i * a1i,
            a2r * a1i + a2i * a1r,
            a2r * b1r - a2i * b1i + b2r,
            a2r * b1i + a2i * b1r + b2i)


def s5_mixer(u, lam_re, lam_im, log_dt, b_re, b_im, c_re, c_im, d_skip, w_glu):
    bsz, seq, _ = u.shape
    uf = u.astype(F32).reshape(bsz, seq, S5_GROUPS, S5_GROUP)
    lr = lam_re.astype(F32)
    li = lam_im.astype(F32)
    dt = jnp.exp(log_dt.astype(F32))[:, None]
    mag = jnp.exp(lr * dt)
    a_re = mag * jnp.cos(li * dt)
    a_im = mag * jnp.sin(li * dt)
    den = lr * lr + li * li
    coef_re = ((a_re - 1.0) * lr + a_im * li) / den
    coef_im = (a_im * lr - (a_re - 1.0) * li) / den
    br = b_re.astype(F32)
    bi = b_im.astype(F32)
    bbar_re = coef_re[..., None] * br - coef_im[..., None] * bi
    bbar_im = coef_re[..., None] * bi + coef_im[..., None] * br
    bu_re = jnp.einsum('blgc,gpc->blgp', uf, bbar_re)
    bu_im = jnp.einsum('blgc,gpc->blgp', uf, bbar_im)
    a_re_b = jnp.broadcast_to(a_re, bu_re.shape)
    a_im_b = jnp.broadcast_to(a_im, bu_im.shape)
    _, _, x_re, x_im = lax.associative_scan(
        _complex_affine_combine, (a_re_b, a_im_b, bu_re, bu_im), axis=1)
    y = (jnp.einsum('blgp,gcp->blgc', x_re, c_re.astype(F32))
         - jnp.einsum('blgp,gcp->blgc', x_im, c_im.astype(F32)))
    y = y.reshape(bsz, seq, D_HALF) + d_skip.astype(F32) * uf.reshape(bsz, seq, D_HALF)
    y = jax.nn.gelu(y)
    y = y * jax.nn.sigmoid(y @ w_glu.astype(F32))
    return y.astype(u.dtype)


def sgu_mixer(zu, zv, norm_g, w_s, b_s):
    bsz, seq, _ = zu.shape
    u = jax.nn.gelu(zu.astype(F32))
    v = jax.nn.gelu(zv.astype(F32))
    mu = jnp.mean(v, axis=-1, keepdims=True)
    var = jnp.mean(jnp.square(v - mu), axis=-1, keepdims=True)
    vn = (v - mu) * lax.rsqrt(var + EPS) * norm_g.astype(F32)
    nch = seq // SGU_CHUNK
    vc = vn.reshape(bsz, nch, SGU_CHUNK, SGU_HEADS, SGU_HEAD_DIM)
    mask = jnp.tril(jnp.ones((SGU_CHUNK, SGU_CHUNK), dtype=bool))
    w = jnp.where(mask, w_s.astype(F32), 0.0)
    s = jnp.einsum('hts,bnshc->bnthc', w, vc) + b_s.astype(F32).T[None, None, :, :, None]
    return (u * s.reshape(bsz, seq, D_HALF)).astype(zu.dtype)


def pool_mixer(z, w_pool, scale):
    bsz, seq, _ = z.shape
    zf = z.astype(F32)
    csum = jnp.pad(jnp.cumsum(zf, axis=1), ((0, 0), (1, 0), (0, 0)))
    pos = jnp.arange(1, seq + 1, dtype=F32)
    outs = []
    for gi, win in enumerate(POOL_WINDOWS):
        sl = slice(gi * POOL_GROUP_DIM, (gi + 1) * POOL_GROUP_DIM)
        cg = csum[:, :, sl]
        lagged = jnp.pad(cg, ((0, 0), (win, 0), (0, 0)))[:, :seq + 1]
        wsum = (cg - lagged)[:, 1:]
        mean = wsum / jnp.minimum(pos, float(win))[None, :, None]
        outs.append((mean - zf[:, :, sl]) @ w_pool[gi].astype(F32))
    y = jnp.concatenate(outs, axis=-1) * scale.astype(F32)
    return y.astype(z.dtype)


def hgrn2_mixer(zq, zf, zi, zg, lb, onorm_g):
    bsz, seq, _ = zq.shape
    lbf = lb.astype(F32)
    q = jax.nn.silu(zq.astype(F32))
    fgate = lbf + (1.0 - lbf) * jax.nn.sigmoid(zf.astype(F32))
    logf = jnp.log(fgate)
    k = 1.0 - fgate
    v = zi.astype(F32)
    nch = seq // HGRN_CHUNK

    def to_chunks(t):
        return t.reshape(bsz, nch, HGRN_CHUNK, HGRN_HEADS, HGRN_HEAD_DIM).transpose(1, 0, 3, 2, 4)

    mask = jnp.tril(jnp.ones((HGRN_CHUNK, HGRN_CHUNK), dtype=bool))

    def step(state, inp):
        qc, kc, vc, lc = inp
        b = jnp.cumsum(lc, axis=2)
        diff = b[:, :, :, None, :] - b[:, :, None, :, :]
        decay = jnp.exp(jnp.where(mask[:, :, None], diff, -jnp.inf))
        scores = jnp.einsum('bhtk,bhtsk,bhsk->bhts', qc, decay, kc)
        o = (jnp.einsum('bhts,bhsv->bhtv', scores, vc)
             + jnp.einsum('bhtk,bhkv->bhtv', qc * jnp.exp(b), state))
        b_last = b[:, :, -1:, :]
        new_state = (jnp.exp(b_last[:, :, 0, :])[..., None] * state
                     + jnp.einsum('bhsk,bhsv->bhkv', kc * jnp.exp(b_last - b), vc))
        return new_state, o

    s0 = jnp.zeros((bsz, HGRN_HEADS, HGRN_HEAD_DIM, HGRN_HEAD_DIM), F32)
    _, o = lax.scan(step, s0, (to_chunks(q), to_chunks(k), to_chunks(v), to_chunks(logf)))
    o = o.transpose(1, 0, 3, 2, 4).reshape(bsz, seq, HGRN_HEADS, HGRN_HEAD_DIM)
    o = o * lax.rsqrt(jnp.mean(o * o, axis=-1, keepdims=True) + EPS) * onorm_g.astype(F32)
    g = jax.nn.silu(zg.astype(F32)).reshape(bsz, seq, HGRN_HEADS, HGRN_HEAD_DIM)
    return (o * g).reshape(bsz, seq, D_HALF).astype(zq.dtype)


def setup_inputs(seed: int = 0) -> dict:
    key = jax.random.key(seed)
    ks = jax.random.split(key, 32)

    def nrm(k, shape, scale=1.0):
        return jax.random.normal(k, shape, F32) * scale

    x = nrm(ks[0], (BATCH, SEQ, D_MODEL))
    norm_g = 1.0 + nrm(ks[1], (DEPTH, 6, D_MODEL), 0.02)
    ffn_wg = nrm(ks[2], (DEPTH, 2, D_MODEL, D_FF), D_MODEL ** -0.5)
    ffn_wu = nrm(ks[3], (DEPTH, 2, D_MODEL, D_FF), D_MODEL ** -0.5)
    ffn_wd = nrm(ks[4], (DEPTH, 2, D_FF, D_MODEL), D_FF ** -0.5)
    even_w_in = nrm(ks[5], (N_EVEN, D_MODEL, EVEN_IN), D_MODEL ** -0.5)
    even_w_out = nrm(ks[6], (N_EVEN, D_MIX, D_MODEL), D_MIX ** -0.5)
    s5_lam_re = -0.5 + nrm(ks[7], (N_EVEN, S5_GROUPS, S5_STATE), 0.01)
    s5_lam_im = (math.pi * jnp.arange(S5_STATE, dtype=F32))[None, None, :] + nrm(ks[8], (N_EVEN, S5_GROUPS, S5_STATE), 0.01)
    s5_log_dt = jax.random.uniform(ks[9], (N_EVEN, S5_GROUPS), F32, minval=math.log(S5_DT_MIN), maxval=math.log(S5_DT_MAX))
    s5_b_re = nrm(ks[10], (N_EVEN, S5_GROUPS, S5_STATE, S5_GROUP), (2 * S5_GROUP) ** -0.5)
    s5_b_im = nrm(ks[11], (N_EVEN, S5_GROUPS, S5_STATE, S5_GROUP), (2 * S5_GROUP) ** -0.5)
    s5_c_re = nrm(ks[12], (N_EVEN, S5_GROUPS, S5_GROUP, S5_STATE), S5_STATE ** -0.5)
    s5_c_im = nrm(ks[13], (N_EVEN, S5_GROUPS, S5_GROUP, S5_STATE), S5_STATE ** -0.5)
    s5_d = nrm(ks[14], (N_EVEN, D_HALF))
    s5_w_glu = nrm(ks[15], (N_EVEN, D_HALF, D_HALF), D_HALF ** -0.5)
    sgu_norm_g = 1.0 + nrm(ks[16], (N_EVEN, D_HALF), 0.02)
    sgu_w = nrm(ks[17], (N_EVEN, SGU_HEADS, SGU_CHUNK, SGU_CHUNK), SGU_CHUNK ** -0.5)
    sgu_b = 1.0 + nrm(ks[18], (N_EVEN, SGU_HEADS, SGU_CHUNK), 0.02)
    odd_w_in = nrm(ks[19], (N_ODD, D_MODEL, ODD_IN), D_MODEL ** -0.5)
    odd_w_out = nrm(ks[20], (N_ODD, D_MIX, D_MODEL), D_MIX ** -0.5)
    pool_w = nrm(ks[21], (N_ODD, len(POOL_WINDOWS), POOL_GROUP_DIM, POOL_GROUP_DIM), POOL_GROUP_DIM ** -0.5)
    pool_scale = 1.0 + nrm(ks[22], (N_ODD, D_HALF), 0.1)
    hgrn_lb = nrm(ks[23], (DEPTH, D_HALF))
    hgrn_onorm_g = 1.0 + nrm(ks[24], (N_ODD, HGRN_HEAD_DIM), 0.02)
    return {'x': x, 'norm_g': norm_g, 'ffn_wg': ffn_wg, 'ffn_wu': ffn_wu, 'ffn_wd': ffn_wd,
            'even_w_in': even_w_in, 'even_w_out': even_w_out,
            's5_lam_re': s5_lam_re, 's5_lam_im': s5_lam_im, 's5_log_dt': s5_log_dt,
            's5_b_re': s5_b_re, 's5_b_im': s5_b_im, 's5_c_re': s5_c_re, 's5_c_im': s5_c_im,
            's5_d': s5_d, 's5_w_glu': s5_w_glu,
            'sgu_norm_g': sgu_norm_g, 'sgu_w': sgu_w, 'sgu_b': sgu_b,
            'odd_w_in': odd_w_in, 'odd_w_out': odd_w_out,
            'pool_w': pool_w, 'pool_scale': pool_scale,
            'hgrn_lb': hgrn_lb, 'hgrn_onorm_g': hgrn_onorm_g}


def reference(x, norm_g, ffn_wg, ffn_wu, ffn_wd, even_w_in, even_w_out,
              s5_lam_re, s5_lam_im, s5_log_dt, s5_b_re, s5_b_im, s5_c_re, s5_c_im,
              s5_d, s5_w_glu, sgu_norm_g, sgu_w, sgu_b, odd_w_in, odd_w_out,
              pool_w, pool_scale, hgrn_lb, hgrn_onorm_g):
    sm = jax.nn.softmax(hgrn_lb.astype(F32), axis=0)
    lb_all = jnp.cumsum(sm, axis=0) - sm[0:1]
    for li in range(DEPTH):
        g = norm_g[li]
        j = li // 2
        h = rmsnorm(x, g[0])
        x = x + 0.5 * rmsnorm(swiglu(h, ffn_wg[li, 0], ffn_wu[li, 0], ffn_wd[li, 0]), g[1])
        h = rmsnorm(x, g[2])
        if li % 2 == 0:
            z = h @ even_w_in[j]
            ya = s5_mixer(z[..., :D_HALF], s5_lam_re[j], s5_lam_im[j], s5_log_dt[j],
                          s5_b_re[j], s5_b_im[j], s5_c_re[j], s5_c_im[j], s5_d[j], s5_w_glu[j])
            yb = sgu_mixer(z[..., D_HALF:2 * D_HALF], z[..., 2 * D_HALF:],
                           sgu_norm_g[j], sgu_w[j], sgu_b[j])
            y = jnp.concatenate([ya, yb], axis=-1) @ even_w_out[j]
        else:
            z = h @ odd_w_in[j]
            yc = pool_mixer(z[..., :D_HALF], pool_w[j], pool_scale[j])
            yd = hgrn2_mixer(z[..., D_HALF:2 * D_HALF], z[..., 2 * D_HALF:3 * D_HALF],
                             z[..., 3 * D_HALF:4 * D_HALF], z[..., 4 * D_HALF:],
                             lb_all[li], hgrn_onorm_g[j])
            y = jnp.concatenate([yc, yd], axis=-1) @ odd_w_out[j]
        x = x + rmsnorm(y, g[3])
        h = rmsnorm(x, g[4])
        x = x + 0.5 * rmsnorm(swiglu(h, ffn_wg[li, 1], ffn_wu[li, 1], ffn_wd[li, 1]), g[5])
    return x
```

```python
import math
from contextlib import ExitStack
import numpy as np
import concourse.bass as bass
import concourse.mybir as mybir
from concourse.bass_utils import run_bass_kernel_spmd

F32 = mybir.dt.float32
BF16 = mybir.dt.bfloat16
I32 = mybir.dt.int32
AF = mybir.ActivationFunctionType
ALU = mybir.AluOpType

D = 1024
KD = 8
DFF = 2816
NF = 22
SEQ = 4096
TL = 512
EPS = 1e-6
TWO_PI = 2.0 * math.pi
DBG_MAP = {}


class Eng:
    def __init__(self, name, h, sem):
        self.name, self.h, self.sem = name, h, sem
        self.count = 0
        self.waited = {}
        self.pending = False


class DSem:
    def __init__(self, sem):
        self.sem = sem
        self.count = 0


class KB:
    def __init__(self, nc, es):
        self.nc, self.es = nc, es
        mk = lambda n: es.enter_context(nc.semaphore(n))
        self.pe = Eng("pe", nc.tensor, mk("s_pe"))
        self.act = Eng("act", nc.scalar, mk("s_act"))
        self.dve = Eng("dve", nc.vector, mk("s_dve"))
        self.pool = Eng("pool", nc.gpsimd, mk("s_pool"))
        self.sp = Eng("sp", nc.sync, mk("s_sp"))
        self.writer = {}
        self.readers = {}
        self.dsems = {}
        self.alias = {}
        self.selfsync_all = False
        self.free_ds = []
        self.prog = {}
        self.trace = []

    def simulate(self):
        pcs = {e: 0 for e in self.prog}
        sems = {}
        progress = True
        while progress:
            progress = False
            for e, lst in self.prog.items():
                while pcs[e] < len(lst):
                    kind, sid, val, tag = lst[pcs[e]]
                    if kind == "wait":
                        if sems.get(sid, 0) >= val:
                            pcs[e] += 1
                            progress = True
                        else:
                            break
                    else:
                        sems[sid] = sems.get(sid, 0) + val
                        pcs[e] += 1
                        progress = True
        stuck = {e: (pcs[e], len(l), l[pcs[e]]) for e, l in self.prog.items() if pcs[e] < len(l)}
        return stuck

    def _ex(self, keys):
        out = []
        for k in keys:
            if k in self.alias:
                out.extend(self.alias[k])
            else:
                out.append(k)
        return out

    def sb(self, name, shape, dt):
        return self.es.enter_context(self.nc.sbuf_tensor(name, shape, dt))

    def ps(self, name, shape, dt=F32):
        return self.es.enter_context(self.nc.psum_tensor(name, shape, dt))

    def dsem(self, key):
        if key not in self.dsems:
            if self.free_ds:
                self.dsems[key] = self.free_ds.pop()
            else:
                self.dsems[key] = DSem(self.es.enter_context(self.nc.semaphore("d_" + key)))
                self.nsem = getattr(self, "nsem", 5) + 1
                assert self.nsem <= 24, "semaphore budget (24) exceeded"
        return self.dsems[key]

    def release(self, keys):
        for k in keys:
            if k in self.dsems:
                self.free_ds.append(self.dsems.pop(k))

    def _deps(self, eng, R, W, selfsync=False):
        need = {}
        selfsync = selfsync or self.selfsync_all
        def add(src, val):
            if src is eng and not (selfsync and eng is not self.pe):
                return
            k = id(src)
            if k not in need or need[k][1] < val:
                need[k] = (src, val)
        for r in R:
            w = self.writer.get(r)
            if w:
                add(*w)
        for w_ in W:
            w = self.writer.get(w_)
            if w:
                add(*w)
            for rd in self.readers.get(w_, ()):
                add(*rd)
        for k, (src, val) in need.items():
            if eng.waited.get(k, 0) < val:
                eng.h.wait_ge(src.sem, val)
                eng.waited[k] = val
                self.prog.setdefault(eng.name, []).append(("wait", id(src), val, getattr(src, "name", "dsem")))

    def _record(self, src, val, R, W):
        for r in R:
            self.readers.setdefault(r, []).append((src, val))
        for w in W:
            self.writer[w] = (src, val)
            self.readers[w] = []

    def op(self, eng, fn, R, W, inc=True, ss=False):
        R, W = self._ex(R), self._ex(W)
        isbank = lambda k: len(k) >= 2 and k[0] == "b" and k[1:].isdigit()
        W = list(W) + [r for r in R if isbank(r)]
        R = [r for r in R if not isbank(r)]
        self._deps(eng, R, W, ss)
        ins = fn()
        if inc:
            ins.then_inc(eng.sem, 1)
            eng.count += 1
            self.prog.setdefault(eng.name, []).append(("inc", id(eng), 1, str(W[:1])))
            self._record(eng, eng.count, R, W)
        else:
            self._record(eng, eng.count + 1, R, W)
        return ins

    def mm(self, out, lhsT, rhs, start, stop, R, W, inc=None):
        if inc is None:
            inc = stop
        return self.op(self.pe, lambda: self.nc.tensor.matmul(out, lhsT, rhs, start=start, stop=stop), R, W, inc=inc)

    def dma(self, q, out, in_, R, W, skey):
        ds = self.dsem(skey)
        R, W = self._ex(R), self._ex(W)
        self._deps(q, R, W)
        q.h.dma_start(out=out, in_=in_).then_inc(ds.sem, 16)
        ds.count += 16
        self.prog.setdefault(q.name, []).append(("inc", id(ds), 16, skey))
        self._record(ds, ds.count, R, W)

    def finish(self, keys):
        for k in keys:
            ds = self.dsems[k]
            self.nc.sync.wait_ge(ds.sem, ds.count)


def build(NSUP=None, S=1024, dbg=None, stages=('ffn',)):
    NT = S // TL
    if NSUP is None:
        NSUP = SEQ // S
    LTOK = NSUP * S
    nc = bass.Bass("TRN2", target_bir_lowering=False)
    dr = lambda n, shp, kind="ExternalInput": nc.dram_tensor(n, shp, F32, kind=kind).ap()
    x_d = dr("x", [LTOK, D])
    out_d = dr("out", [LTOK, D], "ExternalOutput")
    norm_g_d = dr("norm_g", [96, 128])
    wg_d = dr("ffn_wg", [4, D, DFF])
    wu_d = dr("ffn_wu", [4, D, DFF])
    wd_d = dr("ffn_wd", [4, DFF, D])
    ewin_d = dr("even_w_in", [D, 1536])
    ewout_d = dr("even_w_out", [D, D])
    lre_d = dr("s5_lam_re", [32, 64])
    lim_d = dr("s5_lam_im", [32, 64])
    ldt_d = dr("s5_log_dt", [1, 32])
    bre_d = dr("s5_b_re", [32, 64, 16])
    bim_d = dr("s5_b_im", [32, 64, 16])
    cre_d = dr("s5_c_re", [512, 64])
    cim_d = dr("s5_c_im", [512, 64])
    s5d_d = dr("s5_d", [4, 128])
    glu_d = dr("s5_w_glu", [512, 512])
    sng_d = dr("sgu_norm_g", [1, 512])
    sgw_d = dr("sgu_w", [4, 128, 128])
    sgb_d = dr("sgu_b", [1, 512])
    owin_d = dr("odd_w_in", [D, 2560])
    owout_d = dr("odd_w_out", [D, D])
    poolw_d = dr("pool_w", [4, 128, 128])
    pscale_d = dr("pool_scale", [4, 128])
    hlb_d = dr("hgrn_lb", [8, 128])
    hong_d = dr("hgrn_onorm_g", [1, 128])
    dbg_d = dr("dump_out", [128, 16384], "ExternalOutput") if dbg else None
    DBG_MAP.clear()

    es = ExitStack()
    with es:
        kb = KB(nc, es)
        pe, act, dve, pool, sp = kb.pe, kb.act, kb.dve, kb.pool, kb.sp
        V, A, G, T = nc.vector, nc.scalar, nc.gpsimd, nc.tensor
        dbg_state = {"off": 0, "n": 0}

        def dump(name, ap, keys, ncols, nrows=128):
            import os as _o
            if dbg_d is None or name in DBG_MAP or (_o.environ.get("DUMPS") and name not in _o.environ["DUMPS"].split(",")):
                return
            o = dbg_state["off"]
            DBG_MAP[name] = (o, ncols, nrows)
            dbg_state["off"] += ncols
            kb.dma(pool, dbg_d[0:nrows, o:o + ncols], ap, keys, [], "xout0")

        x_sb = kb.sb("x_sb", [128, KD, S], F32)
        hn = kb.sb("hn", [128, KD, S], BF16)
        yacc = kb.sb("yacc", [128, KD, S], F32)
        ablk_all = kb.sb("ablk_all", [128, 2, 4, TL], BF16)
        ablk = [ablk_all[:, i] for i in range(2)]
        wgs = [kb.sb(f"wgs{i}", [128, KD, 512], BF16) for i in range(2)]
        wus = [kb.sb(f"wus{i}", [128, KD, 512], BF16) for i in range(2)]
        wds = [kb.sb(f"wds{i}", [128, 4, D], BF16) for i in range(2)]
        sgt = [kb.sb(f"sgt{i}", [128, TL], F32) for i in range(2)]
        sqt = [kb.sb(f"sqt{i}", [128, TL], F32) for i in range(2)]
        rstd = kb.sb("rstd", [128, TL], F32)
        tmpn = sgt
        assert S == 1024
        xin = [yacc[:, i, :] for i in range(2)]
        xout = [yacc[:, 2 + i, :] for i in range(2)]
        XK = lambda m: [f"ya0_{m}", f"ya1_{m}"]
        ymix = kb.sb("ymix", [128, KD, TL], BF16)
        ubf = ablk[0]
        ident = kb.sb("ident", [128, 128], F32)
        ones = kb.sb("ones", [128, 128], F32)
        prow = kb.sb("prow", [128, 128], F32)
        pcol = kb.sb("pcol", [128, 128], F32)
        iot = kb.sb("iot", [128, 128], F32)
        pidx = kb.sb("pidx", [128, 1], F32)
        epsc = kb.sb("epsc", [128, 1], F32)
        bank = [kb.ps(f"bank{i}", [128, TL]) for i in range(8)]

        for t_ in range(NT):
            kb.alias[f"ya{t_}_4"] = [f"ya{t_}_4_{r}{c}" for r in (0, 1) for c in "ab"]
            kb.alias[f"ya{t_}_5"] = [f"ya{t_}_5_{n}" for n in ("G0", "G1", "C0c", "C0s", "C1c", "C1s")]
            kb.alias[f"ya{t_}_7"] = [f"ya{t_}_7_d", f"ya{t_}_7_p0", f"ya{t_}_7_p1"]
        kb.op(pool, lambda: G.iota(iot[:], pattern=[[1, 128]], base=0, channel_multiplier=0,
                                   allow_small_or_imprecise_dtypes=True), [], ["iot"])
        kb.op(pool, lambda: G.iota(pidx[:], pattern=[[0, 1]], base=0, channel_multiplier=1,
                                   allow_small_or_imprecise_dtypes=True), [], ["pidx"])
        kb.op(dve, lambda: V.tensor_scalar(out=ident[:], in0=iot[:], scalar1=pidx[:, 0:1], scalar2=None,
                                           op0=ALU.is_equal), ["iot", "pidx"], ["ident"])
        kb.op(dve, lambda: V.memset(ones[:], 1.0), [], ["ones"])
        kb.op(dve, lambda: V.memset(epsc[:], EPS), [], ["epsc"])
        kb.op(dve, lambda: V.memset(prow[:], 0.0), [], ["prow"])
        kb.dma(pool, prow[0:96, :], norm_g_d[:, :], [], ["prow"], "prow")
        kb.dma(pool, prow[96:100, :], s5d_d[:, :], [], ["prow"], "prow")
        sngb = kb.sb("sngb", [128, 512], F32)
        dt_early = kb.sb("s5_dt", [128, 32], F32)
        st32 = sqt[1][0:32, 0:256]
        kb.alias["sqt1"] = ["sqt1_main", "st32"]
        BK = lambda i: [f"ya0_{i}", f"ya1_{i}"]
        HK = ["hn0", "hn1"]
        stA = yacc[:, 4, :]
        stB = yacc[:, 5, :]
        st_ri = yacc[0:32, 6:8, :]
        st_ir = yacc[0:32, 2:4, :]
        K_ri, K_ir = BK(6) + BK(7), BK(2) + BK(3)
        kb.dma(sp, sngb[:, :], sng_d[0:1, :].broadcast_to([128, 512]), [], ["sngb"], "sngb")
        kb.dma(sp, dt_early[:], ldt_d[0:1, :].broadcast_to([128, 32]), [], ["sm_dt"], "sm_dt")
        kb.dma(sp, stB[:, 512:1024], sgb_d[0:1, :].broadcast_to([128, 512]), [], BK(5), "brow")
        for ni, src in enumerate((lre_d, lim_d)):
            kb.dma(pool, st32[:, ni * 128:ni * 128 + 64], src[:, :], [], ["st32"], "st32")
            kb.dma(pool, st32[:, ni * 128 + 64:ni * 128 + 128], src[:, :], [], ["st32"], "st32")
        bre_n = bre_d.rearrange("g p c -> g (p c)")
        bim_n = bim_d.rearrange("g p c -> g (p c)")
        kb.dma(pool, st_ri[:, 0, :], bre_n, [], K_ri, "p1")
        kb.dma(pool, st_ri[:, 1, :], bim_n, [], K_ri, "p1")
        kb.dma(pool, st_ir[:, 0, :], bim_n, [], K_ir, "p2")
        kb.dma(pool, st_ir[:, 1, :], bre_n, [], K_ir, "p2")
        for which in range(2):
            for q in range(4):
                CQ_ = stA[:, (which * 4 + q) * 128:(which * 4 + q + 1) * 128]
                a_, b_ = (cre_d, cim_d) if which == 0 else (cim_d, cre_d)
                kb.dma(pool, CQ_[:, 0:64], a_[q * 128:(q + 1) * 128, :], [], BK(4), "cq")
                kb.dma(pool, CQ_[:, 64:128], b_[q * 128:(q + 1) * 128, :], [], BK(4), "cq")
        for h in range(4):
            kb.dma(pool, stB[:, h * 128:(h + 1) * 128], sgw_d[h, :, :], [], BK(5), "sgw")
        kb.dma(pool, prow[100:104, :], pscale_d[:, :], [], ["prow"], "prow")
        kb.dma(pool, prow[104:112, :], hlb_d[:, :], [], ["prow"], "prow")
        kb.dma(pool, prow[112:113, :], hong_d[:, :], [], ["prow"], "prow")
        poolw_stg = ablk_all[:].bitcast(F32).rearrange("p a b c -> p (a b c)")[:, 0:512].rearrange("p (g o) -> p g o", o=128)
        kb.dma(pool, poolw_stg, poolw_d.rearrange("g i o -> i g o"), [], ["ablk0", "ablk1"], "poolw")

        def gcol(l, i, k):
            c = (l * 6 + i) * 8 + k
            return pcol[:, c:c + 1]

        state = {"sq": 0, "sg": 0, "ab": 0, "w": 0, "xi": 0, "xo": 0, "tn": 0, "dn": 0}

        def rms_stats(src_fn, t):
            pbank, pkey = bank[6 + t % 2], f"b{6 + t % 2}"
            for k in range(KD):
                i = state["sq"] % 2
                state["sq"] += 1
                src, sk = src_fn(k)
                sqb = sqt[i][:].bitcast(BF16)[:, 0:TL]
                kb.op(act, lambda: A.activation(out=sqb, in_=src, func=AF.Square), [sk], [f"sqt{i}"])
                kb.mm(pbank[:, :], onesb[:, :], sqb, k == 0, k == KD - 1, ["onesb", f"sqt{i}"], [pkey], inc=True)
            kb.op(act, lambda: A.activation(out=pbank[:, :], in_=pbank[:, :], func=AF.Ln, scale=128.0 / D,
                                            bias=epsc[:, 0:1]), [pkey, "epsc"], [pkey])
            kb.op(act, lambda: A.activation(out=pbank[:, :], in_=pbank[:, :], func=AF.Exp, scale=-0.5), [pkey], [pkey])
            return pbank, pkey

        def prenorm(l, i, tiles=None):
            for t in (range(NT) if tiles is None else tiles):
                ts = slice(t * TL, (t + 1) * TL)
                rb, rk = rms_stats(lambda k: (x_sb[:, k, ts], f"x{t}"), t)
                for k in range(KD):
                    kb.op(dve, lambda: V.scalar_tensor_tensor(out=hn[:, k, ts], in0=x_sb[:, k, ts],
                                                              scalar=gcol(l, i, k), in1=rb[:, :],
                                                              op0=ALU.mult, op1=ALU.mult),
                          [f"x{t}", rk, "pcol"], [f"hn{t}"])

        def postnorm_add(l, i, half, tiles=None):
            for t in (range(NT) if tiles is None else tiles):
                ts = slice(t * TL, (t + 1) * TL)
                rb, rk = rms_stats(lambda k: (yacc[:, k, ts], f"ya{t}_{k}"), t)
                for k in range(KD):
                    j = state["tn"] % 2
                    state["tn"] += 1
                    kb.op(dve, lambda: V.scalar_tensor_tensor(out=tmpn[j][:], in0=yacc[:, k, ts],
                                                              scalar=gcol(l, i, k), in1=rb[:, :],
                                                              op0=ALU.mult, op1=ALU.mult),
                          [f"ya{t}_{k}", rk, "pcol"], [f"sgt{j}"])
                    kb.op(dve, lambda: V.scalar_tensor_tensor(out=x_sb[:, k, ts], in0=tmpn[j][:],
                                                              scalar=0.5 if half else 1.0, in1=x_sb[:, k, ts],
                                                              op0=ALU.mult, op1=ALU.add),
                          [f"sgt{j}", f"x{t}"], [f"x{t}"])

        def ffn(fi):
            blocks = [(c0, min(512, DFF - c0)) for c0 in range(0, DFF, 512)]
            stages = []
            for bi, (c0, cw) in enumerate(blocks):
                w = state["w"] % 2
                state["w"] += 1
                for t in range(NT):
                    ab = state["ab"] % 2
                    state["ab"] += 1
                    stages.append((bi, c0, cw, w, t, ab))

            def loadw(bi, c0, cw, w):
                nj = cw // 128
                for k in range(KD):
                    kb.dma(pool, wgs[w][:, k, 0:cw], wg_d[fi, k * 128:(k + 1) * 128, c0:c0 + cw], [], [f"wgs{w}"], f"wgs{w}")
                    kb.dma(pool, wus[w][:, k, 0:cw], wu_d[fi, k * 128:(k + 1) * 128, c0:c0 + cw], [], [f"wus{w}"], f"wus{w}")
                for j in range(nj):
                    kb.dma(pool, wds[w][:, j, :], wd_d[fi, c0 + j * 128:c0 + (j + 1) * 128, :], [], [f"wds{w}"], f"wds{w}")

            def gu(bi, c0, cw, w, t, ab):
                ts = slice(t * TL, (t + 1) * TL)
                nj = cw // 128
                for j in range(nj):
                    pg, pu = bank[(2 * j) % 4], bank[(2 * j + 1) % 4]
                    kg, ku = f"b{(2 * j) % 4}", f"b{(2 * j + 1) % 4}"
                    for k in range(KD):
                        kb.mm(pg[:, :], wgs[w][:, k, j * 128:(j + 1) * 128], hn[:, k, ts], k == 0, k == KD - 1,
                              [f"wgs{w}", f"hn{t}"], [kg])
                    for k in range(KD):
                        kb.mm(pu[:, :], wus[w][:, k, j * 128:(j + 1) * 128], hn[:, k, ts], k == 0, k == KD - 1,
                              [f"wus{w}", f"hn{t}"], [ku])
                    s_ = state["sg"] % 2
                    state["sg"] += 1
                    kb.op(act, lambda: A.activation(out=sgt[s_][:], in_=pg[:, :], func=AF.Silu), [kg], [f"sgt{s_}"])
                    kb.op(dve, lambda: V.tensor_tensor(out=ablk[ab][:, j, :], in0=sgt[s_][:], in1=pu[:, :], op=ALU.mult),
                          [f"sgt{s_}", ku], [f"ablk{ab}"])

            def down(bi, c0, cw, w, t, ab):
                ts = slice(t * TL, (t + 1) * TL)
                nj = cw // 128
                for m in range(KD):
                    bn = 4 + (state["dn"] % 3)
                    state["dn"] += 1
                    py, ky = bank[bn], f"b{bn}"
                    for j in range(nj):
                        kb.mm(py[:, :], wds[w][:, j, m * 128:(m + 1) * 128], ablk[ab][:, j, :], j == 0, j == nj - 1,
                              [f"wds{w}", f"ablk{ab}"], [ky])
                    if bi == 0:
                        kb.op(act, lambda: A.copy(out=yacc[:, m, ts], in_=py[:, :]), [ky], [f"ya{t}_{m}"])
                    else:
                        kb.op(dve, lambda: V.tensor_tensor(out=yacc[:, m, ts], in0=yacc[:, m, ts], in1=py[:, :], op=ALU.add),
                              [ky, f"ya{t}_{m}"], [f"ya{t}_{m}"])

            for i in range(len(stages) + 1):
                if i < len(stages):
                    if stages[i][4] == 0:
                        loadw(*stages[i][:4])
                    gu(*stages[i])
                if i > 0:
                    down(*stages[i - 1])

        def load_x(s):
            for b in range(S // 128):
                i = state["xi"] % 2
                state["xi"] += 1
                r0 = s * S + b * 128
                kb.dma(sp, xin[i], x_d[r0:r0 + 128, :], [], XK(i), f"xin{i}")
                t = (b * 128) // TL
                for k in range(KD):
                    pb = bank[4 + k % 4]
                    kb.op(pe, lambda: T.transpose(pb[:, 0:128], xin[i][:, k * 128:(k + 1) * 128], ident[:]),
                          XK(i) + ["ident"], [f"b{4 + k % 4}"])
                    eng, fn = (act, lambda: A.copy(out=x_sb[:, k, b * 128:(b + 1) * 128], in_=pb[:, 0:128])) if k % 2 == 0 else \
                              (dve, lambda: V.tensor_copy(out=x_sb[:, k, b * 128:(b + 1) * 128], in_=pb[:, 0:128]))
                    kb.op(eng, fn, [f"b{4 + k % 4}"], [f"x{t}"])

        def store_x(s):
            for b in range(S // 128):
                i = state["xo"] % 2
                state["xo"] += 1
                r0 = s * S + b * 128
                t = (b * 128) // TL
                for k in range(KD):
                    pb = bank[4 + k % 4]
                    kb.op(pe, lambda: T.transpose(pb[:, 0:128], x_sb[:, k, b * 128:(b + 1) * 128], ident[:]),
                          [f"x{t}", "ident"], [f"b{4 + k % 4}"])
                    eng, fn = (act, lambda: A.copy(out=xout[i][:, k * 128:(k + 1) * 128], in_=pb[:, 0:128])) if k % 2 == 0 else \
                              (dve, lambda: V.tensor_copy(out=xout[i][:, k * 128:(k + 1) * 128], in_=pb[:, 0:128]))
                    kb.op(eng, fn, [f"b{4 + k % 4}"], XK(2 + i))
                kb.dma(sp, out_d[r0:r0 + 128, :], xout[i], XK(2 + i), [], f"xout{i}")


        load_x(0)
        kb.op(pe, lambda: T.transpose(bank[7][:, 0:128], prow[:], ident[:]), ["prow", "ident"], ["b7"])
        kb.op(dve, lambda: V.tensor_copy(out=pcol[:], in_=bank[7][:, 0:128]), ["b7"], ["pcol"])
        def slab(t, i):
            return yacc[:, i, t * TL:(t + 1) * TL], f"ya{t}_{i}"

        s5dcol = lambda q: pcol[:, 96 + q:97 + q]

        sm = {}
        _tmp_names = ["lr", "li", "th", "ar", "ai", "den", "cre", "cim", "t0", "t1", "t2", "cbri", "cair", "ang"]
        kb.alias["sqt0"] = ["sqt0_main", "tmp_s", "sin_aux"] + ["sm_" + n for n in _tmp_names]
        def small(name, cols=32):
            if name in _tmp_names:
                i_ = _tmp_names.index(name)
                sm[name] = sqt[0][:, i_ * 32:(i_ + 1) * 32]
            else:
                sm[name] = kb.sb("s5_" + name, [128, cols], F32)
            return sm[name]
        sm["dt"] = dt_early
        for n_ in ["lr", "li", "dec", "th", "ar", "ai", "den", "cre", "cim", "t0", "t1", "t2", "cr", "ci",
                   "cbri", "cair", "ang"]:
            small(n_)
        sgn = kb.sb("sgn", [128, 1], F32)
        rmask = kb.sb("rmask", [128, 4], F32)
        cmask = kb.sb("cmask", [128, 4, 64], F32)
        pmat = kb.sb("pmat", [128, 128], F32)
        maskU = kb.sb("maskU", [128, 128], F32)
        COS = kb.sb("COS", [128, 32, 128], BF16)
        SIN = kb.sb("SIN", [128, 32, 128], BF16)
        BTRI = [kb.sb(f"BTRI{v}", [128, 4, 128], BF16) for v in range(4)]
        BTIR = [kb.sb(f"BTIR{v}", [128, 4, 128], BF16) for v in range(4)]
        W1s = kb.sb("W1s", [128, 32, 64], BF16)
        W2s = kb.sb("W2s", [128, 32, 64], BF16)
        s5init = kb.sb("s5init", [128, 32], F32)
        s5tmp = kb.sb("s5tmp", [128, 32], F32)
        WmT = kb.sb("WmT", [128, 4, 128], BF16)
        browb = kb.sb("browb", [128, 512], BF16)
        onesb = kb.sb("onesb", [128, 128], BF16)
        vnbf = [kb.sb(f"vnbf{i}", [128, 512], BF16) for i in range(2)]
        bnst = kb.sb("bnst", [128, 8], F32)
        mv = kb.sb("mv", [128, 4], F32)
        big = [hn[:, 2 * i:2 * i + 2, :].bitcast(F32).rearrange("p a b -> p (a b)") for i in range(4)]

        SM = lambda *names: ["sm_" + n for n in names]
        def vop(fn, R, W, eng=None, ss=False):
            kb.op(eng or dve, fn, R, W, ss=ss)
        kb.selfsync_all = True

        for ni, (name, src) in enumerate((("lr", lre_d), ("li", lim_d))):
            kb.op(pe, lambda: T.transpose(bank[6][:, 0:32], st32[:, ni * 128:(ni + 1) * 128], ident[0:32, 0:32]), ["st32", "ident"], ["b6"])
            vop(lambda: V.tensor_copy(out=sm[name][:], in_=bank[6][:, 0:32]), ["b6"], SM(name))
        kb.op(act, lambda: A.activation(out=sm["dt"][:], in_=sm["dt"][:], func=AF.Exp), SM("dt"), SM("dt"))
        vop(lambda: V.tensor_tensor(out=sm["t0"][:], in0=sm["lr"][:], in1=sm["dt"][:], op=ALU.mult), SM("lr", "dt"), SM("t0"))
        kb.op(act, lambda: A.activation(out=sm["dec"][:], in_=sm["t0"][:], func=AF.Exp), SM("t0"), SM("dec"))
        vop(lambda: V.tensor_tensor(out=sm["th"][:], in0=sm["li"][:], in1=sm["dt"][:], op=ALU.mult), SM("li", "dt"), SM("th"))

        def sin_of(out_ap, ang_ap, tmp_i, tmp_f, shift, R, W, keys_tmp):
            aux = out_ap_f32[0]
            vop(lambda: V.tensor_scalar(out=tmp_f, in0=ang_ap, scalar1=shift, scalar2=1.0 / TWO_PI, op0=ALU.add, op1=ALU.mult), R, keys_tmp)
            vop(lambda: V.tensor_copy(out=tmp_i, in_=tmp_f), keys_tmp, keys_tmp)
            vop(lambda: V.tensor_copy(out=tmp_f, in_=tmp_i), keys_tmp, keys_tmp)
            vop(lambda: V.tensor_scalar(out=tmp_f, in0=tmp_f, scalar1=-TWO_PI, scalar2=shift, op0=ALU.mult, op1=ALU.add), keys_tmp, keys_tmp)
            vop(lambda: V.tensor_tensor(out=tmp_f, in0=tmp_f, in1=ang_ap, op=ALU.add), keys_tmp + R, keys_tmp)
            vop(lambda: V.tensor_scalar(out=aux, in0=tmp_f, scalar1=math.pi, scalar2=-TWO_PI, op0=ALU.is_gt, op1=ALU.mult), keys_tmp, ["sin_aux"])
            vop(lambda: V.tensor_tensor(out=tmp_f, in0=tmp_f, in1=aux, op=ALU.add), keys_tmp + ["sin_aux"], keys_tmp)
            vop(lambda: V.tensor_scalar(out=aux, in0=tmp_f, scalar1=-math.pi, scalar2=TWO_PI, op0=ALU.is_lt, op1=ALU.mult), keys_tmp, ["sin_aux"])
            vop(lambda: V.tensor_tensor(out=tmp_f, in0=tmp_f, in1=aux, op=ALU.add), keys_tmp + ["sin_aux"], keys_tmp)
            kb.op(act, lambda: A.activation(out=out_ap, in_=tmp_f, func=AF.Sin), keys_tmp, W)

        out_ap_f32 = [None]
        aux_s = sqt[0][:, 14 * 32:15 * 32]
        tmp_s = sqt[0][:, 15 * 32:16 * 32]
        tmp_si = kb.sb("tmp_si", [128, 32], I32)
        def sin_small(outname, angname, shift, mult=1.0):
            out_ap_f32[0] = aux_s
            src = sm[angname][:]
            if mult != 1.0:
                vop(lambda: V.tensor_scalar(out=sm["ang"][:], in0=sm[angname][:], scalar1=mult, scalar2=None, op0=ALU.mult), SM(angname), SM("ang"))
                src = sm["ang"][:]
                R = SM("ang")
            else:
                R = SM(angname)
            sin_of(sm[outname][:], src, tmp_si[:], tmp_s, shift, R, SM(outname), ["tmp_s"])

        sin_small("ai", "th", 0.0)
        sin_small("ar", "th", math.pi / 2)
        sin_small("ci", "th", 0.0, mult=128.0)
        sin_small("cr", "th", math.pi / 2, mult=128.0)
        vop(lambda: V.tensor_tensor(out=sm["ar"][:], in0=sm["ar"][:], in1=sm["dec"][:], op=ALU.mult), SM("ar", "dec"), SM("ar"))
        vop(lambda: V.tensor_tensor(out=sm["ai"][:], in0=sm["ai"][:], in1=sm["dec"][:], op=ALU.mult), SM("ai", "dec"), SM("ai"))
        vop(lambda: V.tensor_tensor(out=sm["t0"][:], in0=sm["lr"][:], in1=sm["lr"][:], op=ALU.mult), SM("lr"), SM("t0"))
        vop(lambda: V.tensor_tensor(out=sm["t1"][:], in0=sm["li"][:], in1=sm["li"][:], op=ALU.mult), SM("li"), SM("t1"))
        vop(lambda: V.tensor_tensor(out=sm["den"][:], in0=sm["t0"][:], in1=sm["t1"][:], op=ALU.add), SM("t0", "t1"), SM("den"))
        vop(lambda: V.reciprocal(out=sm["den"][:], in_=sm["den"][:]), SM("den"), SM("den"))
        vop(lambda: V.tensor_scalar(out=sm["t2"][:], in0=sm["ar"][:], scalar1=-1.0, scalar2=None, op0=ALU.add), SM("ar"), SM("t2"))
        vop(lambda: V.tensor_tensor(out=sm["t0"][:], in0=sm["t2"][:], in1=sm["lr"][:], op=ALU.mult), SM("t2", "lr"), SM("t0"))
        vop(lambda: V.tensor_tensor(out=sm["t1"][:], in0=sm["ai"][:], in1=sm["li"][:], op=ALU.mult), SM("ai", "li"), SM("t1"))
        vop(lambda: V.tensor_tensor(out=sm["t0"][:], in0=sm["t0"][:], in1=sm["t1"][:], op=ALU.add), SM("t0", "t1"), SM("t0"))
        vop(lambda: V.tensor_tensor(out=sm["cre"][:], in0=sm["t0"][:], in1=sm["den"][:], op=ALU.mult), SM("t0", "den"), SM("cre"))
        vop(lambda: V.tensor_tensor(out=sm["t0"][:], in0=sm["ai"][:], in1=sm["lr"][:], op=ALU.mult), SM("ai", "lr"), SM("t0"))
        vop(lambda: V.tensor_tensor(out=sm["t1"][:], in0=sm["t2"][:], in1=sm["li"][:], op=ALU.mult), SM("t2", "li"), SM("t1"))
        vop(lambda: V.tensor_tensor(out=sm["t0"][:], in0=sm["t0"][:], in1=sm["t1"][:], op=ALU.subtract), SM("t0", "t1"), SM("t0"))
        vop(lambda: V.tensor_tensor(out=sm["cim"][:], in0=sm["t0"][:], in1=sm["den"][:], op=ALU.mult), SM("t0", "den"), SM("cim"))
        vop(lambda: V.tensor_scalar(out=sgn[:], in0=pidx[:], scalar1=64.0, scalar2=-2.0, op0=ALU.is_ge, op1=ALU.mult), ["pidx"], ["sgn"])
        vop(lambda: V.tensor_scalar(out=sgn[:], in0=sgn[:], scalar1=1.0, scalar2=None, op0=ALU.add), ["sgn"], ["sgn"])
        vop(lambda: V.tensor_scalar(out=sm["cbri"][:], in0=sm["cim"][:], scalar1=sgn[:, 0:1], scalar2=-1.0, op0=ALU.mult, op1=ALU.mult), SM("cim") + ["sgn"], SM("cbri"))
        vop(lambda: V.tensor_scalar(out=sm["cair"][:], in0=sm["cre"][:], scalar1=sgn[:, 0:1], scalar2=None, op0=ALU.mult), SM("cre") + ["sgn"], SM("cair"))
        vop(lambda: V.tensor_scalar(out=sm["t0"][:, 0:1], in0=pidx[:], scalar1=64.0, scalar2=-64.0, op0=ALU.is_ge, op1=ALU.mult), ["pidx"], SM("t0"))
        vop(lambda: V.tensor_tensor(out=sm["t0"][:, 0:1], in0=sm["t0"][:, 0:1], in1=pidx[:], op=ALU.add), SM("t0") + ["pidx"], SM("t0"))
        for v in range(4):
            vop(lambda: V.tensor_scalar(out=sm["t1"][:, 0:1], in0=sm["t0"][:, 0:1], scalar1=16.0 * v, scalar2=None, op0=ALU.is_ge), SM("t0"), SM("t1"))
            vop(lambda: V.tensor_scalar(out=sm["t1"][:, 1:2], in0=sm["t0"][:, 0:1], scalar1=16.0 * (v + 1), scalar2=None, op0=ALU.is_lt), SM("t0"), SM("t1"))
            vop(lambda: V.tensor_tensor(out=rmask[:, v:v + 1], in0=sm["t1"][:, 0:1], in1=sm["t1"][:, 1:2], op=ALU.mult), SM("t1"), ["rmask"])
        vop(lambda: V.memset(cmask[:], 0.0), [], ["cmask"])
        for v in range(4):
            vop(lambda: V.memset(cmask[:, v, 16 * v:16 * v + 16], 1.0), ["cmask"], ["cmask"])
        vop(lambda: V.tensor_scalar(out=pmat[:], in0=iot[:], scalar1=pidx[:, 0:1], scalar2=64.0, op0=ALU.subtract, op1=ALU.is_equal), ["iot", "pidx"], ["pmat"])
        vop(lambda: V.tensor_scalar(out=maskU[:], in0=iot[:], scalar1=pidx[:, 0:1], scalar2=-64.0, op0=ALU.subtract, op1=ALU.is_equal), ["iot", "pidx"], ["maskU"])
        vop(lambda: V.tensor_tensor(out=pmat[:], in0=pmat[:], in1=maskU[:], op=ALU.subtract), ["pmat", "maskU"], ["pmat"])
        vop(lambda: V.tensor_scalar(out=maskU[:], in0=iot[:], scalar1=pidx[:, 0:1], scalar2=None, op0=ALU.is_ge), ["iot", "pidx", "pmat"], ["maskU"])
        vop(lambda: V.memset(s5init[:], 0.0), [], ["s5init"])

        for gb in range(4):
            angt = big[0]
            tmpf = big[1]
            tmpi = big[2].bitcast(I32)
            out_ap_f32[0] = big[3]
            for gg in range(8):
                g = gb * 8 + gg
                vop(lambda: V.tensor_scalar(out=angt[:, gg * 128:(gg + 1) * 128], in0=iot[:], scalar1=sm["th"][:, g:g + 1], scalar2=None, op0=ALU.mult),
                    ["iot"] + SM("th"), HK)
            sin_of(SIN[:, gb * 8:(gb + 1) * 8, :].rearrange("p g t -> p (g t)"), angt, tmpi, tmpf, 0.0, HK, ["SIN"], HK)
            sin_of(COS[:, gb * 8:(gb + 1) * 8, :].rearrange("p g t -> p (g t)"), angt, tmpi, tmpf, math.pi / 2, HK, ["COS"], HK)

        P1 = big[0].rearrange("p (g c) -> p g c", c=16)[:, 0:32, :]
        P2 = big[1].rearrange("p (g c) -> p g c", c=16)[:, 0:32, :]
        XT = big[2].rearrange("p (g c) -> p g c", c=16)[:, 0:32, :]
        XT2 = big[3].rearrange("p (g c) -> p g c", c=16)[:, 0:32, :]
        for stt, sk_, Pdst in ((st_ri, K_ri, P1), (st_ir, K_ir, P2)):
            sv = stt.rearrange("g h (p c) -> g h p c", c=16)
            for c in range(16):
                kb.op(pe, lambda: T.transpose(bank[6][:, c * 32:(c + 1) * 32], sv[:, :, :, c], ident[0:32, 0:32]),
                      sk_ + ["ident"], ["b6"])
            vop(lambda: V.tensor_copy(out=Pdst, in_=bank[6][:, :].rearrange("p (c g) -> p g c", g=32)), ["b6"], HK)
        bc = lambda n_: sm[n_][:, :, None].broadcast_to([128, 32, 16])
        for which, (ca, pa, cb, pb_) in enumerate(((("cre"), P1, ("cbri"), P2), (("cair"), P2, ("cim"), P1))):
            vop(lambda: V.tensor_tensor(out=XT, in0=pa, in1=bc(ca), op=ALU.mult), HK + SM(ca), HK)
            vop(lambda: V.tensor_tensor(out=XT2, in0=pb_, in1=bc(cb), op=ALU.mult), HK + SM(cb), HK)
            vop(lambda: V.tensor_tensor(out=XT, in0=XT, in1=XT2, op=ALU.add), HK, HK)
            dst = BTRI if which == 0 else BTIR
            for q in range(4):
                src = big[2][:, q * 128:(q + 1) * 128]
                kb.op(pe, lambda: T.transpose(bank[6][:, 0:128], src, ident[:]), HK + ["ident"], ["b6"])
                for v in range(4):
                    vop(lambda: V.tensor_scalar(out=dst[v][:, q, :], in0=bank[6][:, 0:128], scalar1=rmask[:, v:v + 1], scalar2=None, op0=ALU.mult),
                        ["b6", "rmask"], [f"BT{which}{v}"])
        for which in range(2):
            for q in range(4):
                CQ = stA[:, (which * 4 + q) * 128:(which * 4 + q + 1) * 128]
                if which == 0:
                    vop(lambda: V.tensor_scalar(out=CQ[:, 64:128], in0=CQ[:, 64:128], scalar1=-1.0, scalar2=None, op0=ALU.mult), BK(4), BK(4))
                else:
                    vop(lambda: V.tensor_scalar(out=CQ, in0=CQ, scalar1=-1.0, scalar2=None, op0=ALU.mult), BK(4), BK(4))
                kb.op(pe, lambda: T.transpose(bank[6][:, 0:128], CQ, ident[:]), BK(4) + ["ident"], ["b6"])
                dstW = W1s if which == 0 else W2s
                for hb in range(2):
                    for v in range(4):
                        g = q * 8 + hb * 4 + v
                        vop(lambda: V.tensor_tensor(out=dstW[:, g, :], in0=bank[6][:, 64 * hb:64 * hb + 64], in1=cmask[:, v, :], op=ALU.mult),
                            ["b6", "cmask"], [f"Ws{which}"])
        for h in range(4):
            stg_ = stB[:, h * 128:(h + 1) * 128]
            kb.op(pe, lambda: T.transpose(bank[6][:, 0:128], stg_, ident[:]), BK(5) + ["ident"], ["b6"])
            vop(lambda: V.tensor_tensor(out=WmT[:, h, :], in0=bank[6][:, 0:128], in1=maskU[:], op=ALU.mult), ["b6", "maskU"], ["WmT"])
        vop(lambda: V.tensor_copy(out=browb[:], in_=stB[:, 512:1024]), BK(5), ["browb"])
        vop(lambda: V.memset(onesb[:], 1.0 / 128.0), [], ["onesb"])

        kb.selfsync_all = False
        kb.release(["prow", "st32", "sm_dt", "p1", "p2", "cq", "sgw", "brow", "sngb"])
        WSL = [wgs[0], wus[0], wgs[1], wus[1]]
        WSK = ["wgs0", "wus0", "wgs1", "wus1"]

        def load_win(wd_, nblk):
            for b in range(nblk):
                for k in range(KD):
                    kb.dma(pool, WSL[b][:, k, :], wd_[k * 128:(k + 1) * 128, b * 512:(b + 1) * 512], [], [WSK[b]], WSK[b])

        def load_wout(wd_):
            for k in range(KD):
                kb.dma(pool, wds[k // 4][:, k % 4, :], wd_[k * 128:(k + 1) * 128, :], [], [f"wds{k // 4}"], f"wds{k // 4}")

        def zfm(b, cc, t, pbank, pkey):
            ts = slice(t * TL, (t + 1) * TL)
            for k in range(KD):
                kb.mm(pbank[:, :], WSL[b][:, k, cc * 128:(cc + 1) * 128], hn[:, k, ts], k == 0, k == KD - 1, [WSK[b], f"hn{t}"], [pkey])

        def wout_apply(t):
            ts = slice(t * TL, (t + 1) * TL)
            for m in range(KD):
                pb, pk = bank[m % 2], f"b{m % 2}"
                for k in range(KD):
                    kb.mm(pb[:, :], wds[k // 4][:, k % 4, m * 128:(m + 1) * 128], ymix[:, k, :], k == 0, k == KD - 1,
                          [f"wds{k // 4}", "ymix"], [pk])
                kb.op(act, lambda: A.copy(out=yacc[:, m, ts], in_=pb[:, :]), [pk], [f"ya{t}_{m}"])
                if t == 0 and m in (0, 5):
                    dump(f"y{m}", yacc[:, m, ts], [f"ya{t}_{m}"], 512)

        def mixer0(s):
            load_win(ewin_d, 3)
            load_wout(ewout_d)
            for k in range(4):
                kb.dma(pool, wus[1][:, k, :], glu_d[k * 128:(k + 1) * 128, :], [], ["wus1"], "wus1")
            for t in range(NT):
                ts = slice(t * TL, (t + 1) * TL)
                import os as _os
                _dbg = _os.environ.get("MIXDBG", "")
                if "nosgu" in _dbg or "nos5" in _dbg:
                    vop(lambda: V.memset(ymix[:], 0.0), [], ["ymix"])
                for h in range(4 if "nosgu" not in _dbg else 0):
                    zfm(1, h, t, bank[h % 2], f"b{h % 2}")
                    sl, sk = slab(t, h)
                    kb.op(act, lambda: A.activation(out=sl, in_=bank[h % 2][:, :], func=AF.Gelu_apprx_tanh), [f"b{h % 2}"], [sk])
                    if t == 0 and h in (0, 1):
                        dump(f"u{h}", sl, [sk], 512)
                for tb in range(4 if "nosgu" not in _dbg else 0):
                    tok = slice(t * TL + tb * 128, t * TL + (tb + 1) * 128)
                    for k in range(KD):
                        kb.mm(bank[2][:, :], hn[:, k, tok], WSL[2][:, k, :], k == 0, k == KD - 1, [WSK[2], f"hn{t}"], ["b2"])
                    vt, vk = slab(t, 4 + tb % 2)
                    kb.op(act, lambda: A.activation(out=vt, in_=bank[2][:, :], func=AF.Gelu_apprx_tanh), ["b2"], [vk])
                    vop(lambda: V.bn_stats(out=bnst[:, 0:6], in_=vt), [vk], ["bnst"])
                    vop(lambda: V.tensor_tensor(out=mv[:, 0:1], in0=bnst[:, 1:2], in1=bnst[:, 4:5], op=ALU.add), ["bnst"], ["mv"], ss=True)
                    vop(lambda: V.tensor_scalar(out=mv[:, 0:1], in0=mv[:, 0:1], scalar1=0.5, scalar2=None, op0=ALU.mult), ["mv"], ["mv"], ss=True)
                    vop(lambda: V.tensor_tensor(out=mv[:, 3:4], in0=bnst[:, 1:2], in1=bnst[:, 4:5], op=ALU.subtract), ["bnst"], ["mv"], ss=True)
                    vop(lambda: V.tensor_tensor(out=mv[:, 3:4], in0=mv[:, 3:4], in1=mv[:, 3:4], op=ALU.mult), ["mv"], ["mv"], ss=True)
                    vop(lambda: V.tensor_tensor(out=mv[:, 1:2], in0=bnst[:, 2:3], in1=bnst[:, 5:6], op=ALU.add), ["bnst", "mv"], ["mv"], ss=True)
                    vop(lambda: V.tensor_scalar(out=mv[:, 1:2], in0=mv[:, 1:2], scalar1=1.0 / 512.0, scalar2=None, op0=ALU.mult), ["mv"], ["mv"], ss=True)
                    vop(lambda: V.scalar_tensor_tensor(out=mv[:, 1:2], in0=mv[:, 3:4], scalar=0.25, in1=mv[:, 1:2], op0=ALU.mult, op1=ALU.add), ["mv"], ["mv"], ss=True)
                    kb.op(act, lambda: A.activation(out=mv[:, 2:3], in_=mv[:, 1:2], func=AF.Ln, bias=epsc[:, 0:1], scale=1.0), ["mv", "epsc"], ["mv"], ss=True)
                    kb.op(act, lambda: A.activation(out=mv[:, 2:3], in_=mv[:, 2:3], func=AF.Exp, scale=-0.5), ["mv"], ["mv"], ss=True)
                    vop(lambda: V.tensor_scalar(out=vt, in0=vt, scalar1=mv[:, 0:1], scalar2=mv[:, 2:3], op0=ALU.subtract, op1=ALU.mult), [vk, "mv"], [vk], ss=True)
                    if t == 0 and tb == 0:
                        dump("vn0", vt, [vk], 512)
                        dump("mv0", mv[:, 0:4], ["mv"], 4)
                    vb = vnbf[tb % 2]
                    vop(lambda: V.tensor_tensor(out=vb[:], in0=vt, in1=sngb[:], op=ALU.mult), [vk, "sngb"], [f"vnbf{tb % 2}"])
                    for h in range(4):
                        pb = bank[4 + h]
                        kb.mm(pb[:, tb * 128:(tb + 1) * 128], vb[:, h * 128:(h + 1) * 128], WmT[:, h, :], True, False,
                              [f"vnbf{tb % 2}", "WmT"], [f"b{4 + h}"], inc=False)
                        kb.mm(pb[:, tb * 128:(tb + 1) * 128], onesb[:, :], browb[:, h * 128:(h + 1) * 128], False, True,
                              ["onesb", "browb"], [f"b{4 + h}"], inc=True)
                if t == 0 and dbg_d is not None and not _os.environ.get("NOS0"):
                    kb.op(dve, lambda: V.tensor_copy(out=sgt[0][:], in_=bank[4][:, :]), ["b4"], ["sgt0"])
                    dump("s0", sgt[0][:], ["sgt0"], 512)
                    kb.op(dve, lambda: V.tensor_copy(out=sgt[1][:], in_=bank[5][:, :]), ["b5"], ["sgt1"])
                    dump("s1", sgt[1][:], ["sgt1"], 512)
                for h in range(4 if "nosgu" not in _dbg else 0):
                    sl, sk = slab(t, h)
                    vop(lambda: V.tensor_tensor(out=ymix[:, 4 + h, :], in0=sl, in1=bank[4 + h][:, :], op=ALU.mult), [sk, f"b{4 + h}"], ["ymix"])
                if "nos5" in _dbg:
                    wout_apply(t)
                    postnorm_add(0, 3, False, [t])
                    continue
                for q in range(4):
                    zfm(0, q, t, bank[q % 2], f"b{q % 2}")
                    sl, sk = slab(t, q)
                    kb.op(act, lambda: A.copy(out=sl, in_=bank[q % 2][:, :]), [f"b{q % 2}"], [sk])
                    vop(lambda: V.tensor_copy(out=ubf[:, q, :], in_=bank[q % 2][:, :]), [f"b{q % 2}"], ["ablk0"])
                wk, wkk = slab(t, 4)
                gk, gkk = slab(t, 5)
                gcs = gk.bitcast(BF16)
                its = [(c4, g) for c4 in range(4) for g in range(32)]

                def s5proj(c4, g):
                    q, hb, v = g // 8, (g % 8) // 4, g % 4
                    rows = slice(64 * hb, 64 * hb + 64)
                    cs = slice(c4 * 128, (c4 + 1) * 128)
                    kb.mm(bank[g % 2][:, 0:128], BTRI[v][rows, q, :], ubf[rows, q, cs], True, True, [f"BT0{v}", "ablk0"], [f"b{g % 2}"])
                    kb.mm(bank[2 + g % 2][:, 0:128], BTIR[v][rows, q, :], ubf[rows, q, cs], True, True, [f"BT1{v}", "ablk0"], [f"b{2 + g % 2}"])

                s5proj(*its[0])
                for ii, (c4, g) in enumerate(its):
                    if True:
                        cs = slice(c4 * 128, (c4 + 1) * 128)
                        q, hb, v = g // 8, (g % 8) // 4, g % 4
                        rows = slice(64 * hb, 64 * hb + 64)
                        pb, pk = bank[g % 2], f"b{g % 2}"
                        pb2, pk2 = bank[2 + g % 2], f"b{2 + g % 2}"
                        r2 = g % 2
                        if ii + 1 < len(its):
                            s5proj(*its[ii + 1])
                        t1 = wk[:, r2 * 256:r2 * 256 + 128]
                        t2 = wk[:, r2 * 256 + 128:r2 * 256 + 256]
                        k1 = wkk + f"_{r2}"
                        vop(lambda: V.tensor_tensor(out=t1, in0=pb[:, 0:128], in1=COS[:, g, :], op=ALU.mult), [pk, "COS"], [k1 + "a"])
                        vop(lambda: V.tensor_tensor(out=t2, in0=pb2[:, 0:128], in1=SIN[:, g, :], op=ALU.mult), [pk2, "SIN"], [k1 + "b"])
                        vop(lambda: V.tensor_tensor(out=t1, in0=t1, in1=t2, op=ALU.add), [k1 + "a", k1 + "b"], [k1 + "a"])
                        Gt = gk[:, 256 + r2 * 128:256 + (r2 + 1) * 128]
                        kG = gkk + f"_G{r2}"
                        vop(lambda: V.tensor_tensor_scan(out=Gt, data0=sm["dec"][:, g:g + 1].broadcast_to([128, 128]), data1=t1,
                                                         initial=s5init[:, g:g + 1], op0=ALU.mult, op1=ALU.add),
                            [k1 + "a", "s5init", "sm_dec"], [kG])
                        kb.mm(pb[:, 256:257], pmat[:], Gt[:, 127:128], True, True, ["pmat", kG], [pk])
                        Gc = gcs[:, r2 * 256:r2 * 256 + 128]
                        Gs = gcs[:, r2 * 256 + 128:r2 * 256 + 256]
                        kC = gkk + f"_C{r2}"
                        vop(lambda: V.tensor_tensor(out=Gc, in0=Gt, in1=COS[:, g, :], op=ALU.mult), [kG, "COS"], [kC + "c"])
                        vop(lambda: V.tensor_scalar(out=s5tmp[:, g:g + 1], in0=Gt[:, 127:128], scalar1=sm["cr"][:, g:g + 1], scalar2=None, op0=ALU.mult),
                            [kG, "sm_cr"], ["s5tmp"])
                        vop(lambda: V.tensor_tensor(out=Gs, in0=Gt, in1=SIN[:, g, :], op=ALU.mult), [kG, "SIN"], [kC + "s"])
                        vop(lambda: V.scalar_tensor_tensor(out=s5init[:, g:g + 1], in0=pb[:, 256:257], scalar=sm["ci"][:, g:g + 1], in1=s5tmp[:, g:g + 1],
                                                           op0=ALU.mult, op1=ALU.add), [pk, "sm_ci", "s5tmp"], ["s5init"])
                        yb, yk = bank[4 + q], f"b{4 + q}"
                        kb.mm(yb[rows, cs], W1s[:, g, :], Gc, v == 0, False, ["Ws0", kC + "c"], [yk], inc=True)
                        kb.mm(yb[rows, cs], W2s[:, g, :], Gs, False, v == 3, ["Ws1", kC + "s"], [yk], inc=True)
                for q in range(4):
                    sl, sk = slab(t, q)
                    vop(lambda: V.scalar_tensor_tensor(out=sl, in0=sl, scalar=s5dcol(q), in1=bank[4 + q][:, :], op0=ALU.mult, op1=ALU.add),
                        [sk, f"b{4 + q}", "pcol"], [sk])
                    kb.op(act, lambda: A.activation(out=sl, in_=sl, func=AF.Gelu_apprx_tanh), [sk], [sk])
                    vop(lambda: V.tensor_copy(out=ubf[:, q, :], in_=sl), [sk], ["ablk0"])
                for q2 in range(4):
                    pb, pk = bank[q2 % 2], f"b{q2 % 2}"
                    for q in range(4):
                        kb.mm(pb[:, :], wus[1][:, q, q2 * 128:(q2 + 1) * 128], ubf[:, q, :], q == 0, q == 3, ["wus1", "ablk0"], [pk])
                    s_ = state["sg"] % 2
                    state["sg"] += 1
                    kb.op(act, lambda: A.activation(out=sgt[s_][:], in_=pb[:, :], func=AF.Sigmoid), [pk], [f"sgt{s_}"])
                    sl, sk = slab(t, q2)
                    vop(lambda: V.tensor_tensor(out=ymix[:, q2, :], in0=sl, in1=sgt[s_][:], op=ALU.mult), [sk, f"sgt{s_}"], ["ymix"])
                wout_apply(t)


        kb.selfsync_all = True
        poolW = kb.sb("poolW", [128, 4, 128], BF16)
        Mt = kb.sb("Mt", [128, 12, 128], BF16)
        zc_buf = [kb.sb(f"zc{i}", [128, 512], BF16) for i in range(2)]
        Sf = kb.sb("Sf", [128, 4, 128], F32)
        rmask512 = kb.sb("rmask512", [128, 512], BF16)
        BD16 = kb.sb("BD16", [128, 128], BF16)
        cm8 = kb.sb("cm8", [128, 8], BF16)
        lbc = kb.sb("lbc", [128, 8], F32)
        vop(lambda: V.tensor_copy(out=poolW[:], in_=poolw_stg), ["ablk0", "ablk1"], ["poolW"])
        vop(lambda: V.tensor_tensor(out=lbc[:, 0:4], in0=pcol[:, 108:112], in1=pcol[:, 104:108], op=ALU.subtract), ["pcol"], ["lbc"])
        kb.op(act, lambda: A.activation(out=lbc[:, 0:4], in_=lbc[:, 0:4], func=AF.Sigmoid), ["lbc"], ["lbc"])
        vop(lambda: V.tensor_scalar(out=lbc[:, 4:8], in0=lbc[:, 0:4], scalar1=-1.0, scalar2=1.0, op0=ALU.mult, op1=ALU.add), ["lbc"], ["lbc"])
        vop(lambda: V.memset(Sf[:], 0.0), [], ["Sf"])
        dm = big[0][:, 0:128]
        t_a = big[0][:, 128:256]
        t_b = big[0][:, 256:384]
        t_c = big[0][:, 384:512]
        t_d = big[0][:, 512:640]
        vop(lambda: V.tensor_scalar(out=dm, in0=iot[:], scalar1=pidx[:, 0:1], scalar2=None, op0=ALU.subtract), ["iot", "pidx"], HK)
        for gi, win in enumerate((2, 4, 8, 16)):
            vop(lambda: V.tensor_scalar(out=t_a, in0=dm, scalar1=0.0, scalar2=None, op0=ALU.is_ge), HK, HK)
            vop(lambda: V.tensor_scalar(out=t_b, in0=dm, scalar1=float(win), scalar2=None, op0=ALU.is_lt), HK, HK)
            vop(lambda: V.tensor_tensor(out=t_a, in0=t_a, in1=t_b, op=ALU.mult), HK, HK)
            vop(lambda: V.scalar_tensor_tensor(out=Mt[:, 3 * gi, :], in0=t_a, scalar=1.0 / win, in1=ident[:], op0=ALU.mult, op1=ALU.subtract), HK + ["ident"], ["Mt"])
            vop(lambda: V.tensor_scalar(out=t_c, in0=iot[:], scalar1=1.0, scalar2=float(win), op0=ALU.add, op1=ALU.min), ["iot"], HK)
            vop(lambda: V.reciprocal(out=t_c, in_=t_c), HK, HK)
            vop(lambda: V.tensor_tensor(out=t_c, in0=t_c, in1=t_a, op=ALU.mult), HK, HK)
            vop(lambda: V.tensor_tensor(out=Mt[:, 3 * gi + 2, :], in0=t_c, in1=ident[:], op=ALU.subtract), HK + ["ident"], ["Mt"])
            vop(lambda: V.tensor_scalar(out=t_d, in0=dm, scalar1=128.0, scalar2=float(win), op0=ALU.add, op1=ALU.is_lt), HK, HK)
            vop(lambda: V.tensor_scalar(out=Mt[:, 3 * gi + 1, :], in0=t_d, scalar1=1.0 / win, scalar2=None, op0=ALU.mult), HK, ["Mt"])
        colch = big[1][:, 0:128]
        rowch = big[1][:, 128:256]
        kb.op(pool, lambda: G.iota(colch, pattern=[[1, 8], [0, 16]], base=0, channel_multiplier=0, allow_small_or_imprecise_dtypes=True), [], HK)
        kb.op(pe, lambda: T.transpose(bank[6][:, 0:128], colch, ident[:]), HK + ["ident"], ["b6"])
        vop(lambda: V.tensor_copy(out=rowch, in_=bank[6][:, 0:128]), ["b6"], HK)
        vop(lambda: V.tensor_tensor(out=t_b, in0=colch, in1=rowch, op=ALU.is_equal), HK, HK)
        vop(lambda: V.tensor_scalar(out=t_a, in0=dm, scalar1=0.0, scalar2=None, op0=ALU.is_ge), HK, HK)
        vop(lambda: V.tensor_tensor(out=BD16[:], in0=t_a, in1=t_b, op=ALU.mult), HK, ["BD16"])
        vop(lambda: V.tensor_scalar(out=cm8[:], in0=iot[:, 0:8], scalar1=rowch[:, 0:1], scalar2=None, op0=ALU.is_equal), ["iot"] + HK, ["cm8"])
        jidx = big[2][:, 0:512]
        kb.op(pool, lambda: G.iota(jidx, pattern=[[0, 32], [1, 16]], base=0, channel_multiplier=0, allow_small_or_imprecise_dtypes=True), [], HK)
        vop(lambda: V.tensor_scalar(out=rmask512[:], in0=jidx, scalar1=0.0, scalar2=None, op0=ALU.is_gt), HK, ["rmask512"])
        kb.selfsync_all = False
        kb.release(["poolw"])
        WSL.append(ablk_all[:].rearrange("p a b c -> p (a b) c"))
        WSK.append("ablkW")
        kb.alias["ablkW"] = ["ablk0", "ablk1"]
        pscol = lambda gi: pcol[:, 100 + gi:101 + gi]
        ongcol = pcol[:, 112:113]
        l1 = {"blk": 0}

        import os as _os2
        _os_ss = int(_os2.environ.get("HGRN_SS", "0"))

        def mixer1(s):
            load_win(owin_d, 5)
            load_wout(owout_d)
            for t in range(NT):
                ts = slice(t * TL, (t + 1) * TL)
                S_ = lambda i: slab(t, i)
                qk, qkk = S_(2)
                Qt = qk.bitcast(BF16)[:, 0:512]
                Kt = qk.bitcast(BF16)[:, 512:1024]
                v34a, v3k = S_(3)
                v34b, v4k = S_(4)
                vtm = [v34a.bitcast(BF16)[:, 0:512], v34a.bitcast(BF16)[:, 512:1024],
                       v34b.bitcast(BF16)[:, 0:512], v34b.bitcast(BF16)[:, 512:1024]]
                vtk = [v3k, v3k, v4k, v4k]
                vx, vxk = S_(5)
                Vexp = vx.bitcast(BF16).rearrange("p (c v) -> p c v", v=128)
                sbs, sbk = S_(6)
                Sb = sbs.bitcast(BF16).rearrange("p (c v) -> p c v", v=128)
                m7, m7k = S_(7)
                m7b = m7.bitcast(BF16)
                scTm = m7b[:, 0:128]
                Khtm = m7b[:, 128:256]
                dbf = m7b[:, 256:768].rearrange("p (g t) -> p g t", t=128)
                Tg, Tgk = S_(0)
                Kh32, Kh32k = S_(1)
                T1, T1k = sgt[0][:], "sgt0"
                T2, T2k = sgt[1][:], "sgt1"
                T3, T3k = sqt[0][:], "sqt0"
                T4, T4k = sqt[1][:], "sqt1"
                for tb in range(4):
                    tok = slice(t * TL + tb * 128, t * TL + (tb + 1) * 128)
                    first = (s == 0 and t == 0 and tb == 0)
                    cur = l1["blk"] % 2
                    l1["blk"] += 1
                    zc, zck = zc_buf[cur], f"zc{cur}"
                    zp, zpk = zc_buf[1 - cur], f"zc{1 - cur}"
                    for k in range(KD):
                        kb.mm(bank[2][:, :], hn[:, k, tok], WSL[0][:, k, :], k == 0, k == KD - 1, [WSK[0], f"hn{t}"], ["b2"])
                    kb.op(act, lambda: A.copy(out=zc[:], in_=bank[2][:, :]), ["b2"], [zck])
                    for gi in range(4):
                        gsl = slice(gi * 128, (gi + 1) * 128)
                        kb.mm(bank[3][:, gsl], zc[:, gsl], Mt[:, 3 * gi + (2 if first else 0), :], True, first, [zck, "Mt"], ["b3"], inc=first)
                        if not first:
                            kb.mm(bank[3][:, gsl], zp[:, gsl], Mt[:, 3 * gi + 1, :], False, True, [zpk, "Mt"], ["b3"], inc=True)
                    vop(lambda: V.tensor_copy(out=dbf, in_=bank[3][:, :].rearrange("p (g t) -> p g t", t=128)), ["b3"], [m7k + "_d"])
                    for gi in range(4):
                        gsl = slice(gi * 128, (gi + 1) * 128)
                        kb.mm(bank[4][:, gsl], poolW[:, gi, :], dbf[:, gi, :], True, True, ["poolW", m7k + "_d"], ["b4"])
                    for gi in range(4):
                        gsl = slice(gi * 128, (gi + 1) * 128)
                        vop(lambda: V.tensor_scalar(out=ymix[:, gi, tb * 128:(tb + 1) * 128], in0=bank[4][:, gsl], scalar1=pscol(gi), scalar2=None, op0=ALU.mult),
                            ["b4", "pcol"], ["ymix"])
                    for k in range(KD):
                        kb.mm(bank[2][:, :], hn[:, k, tok], WSL[3][:, k, :], k == 0, k == KD - 1, [WSK[3], f"hn{t}"], ["b2"])
                    kb.op(act, lambda: A.copy(out=vtm[tb], in_=bank[2][:, :]), ["b2"], [vtk[tb]])
                for h in range(4):
                    zfm(1, h, t, bank[0], "b0")
                    kb.op(act, lambda: A.activation(out=T4, in_=bank[0][:, :], func=AF.Silu), ["b0"], [T4k])
                    zfm(2, h, t, bank[1], "b1")
                    kb.op(act, lambda: A.activation(out=T1, in_=bank[1][:, :], func=AF.Sigmoid), ["b1"], [T1k])
                    vop(lambda: V.tensor_scalar(out=T1, in0=T1, scalar1=lbc[:, 4 + h:5 + h], scalar2=lbc[:, h:h + 1], op0=ALU.mult, op1=ALU.add), [T1k, "lbc"], [T1k])
                    vop(lambda: V.tensor_scalar(out=T2, in0=T1, scalar1=-1.0, scalar2=1.0, op0=ALU.mult, op1=ALU.add), [T1k], [T2k])
                    kb.op(act, lambda: A.activation(out=T1, in_=T1, func=AF.Ln), [T1k], [T1k])
                    vop(lambda: V.tensor_tensor_scan(out=T3, data0=rmask512[:], data1=T1, initial=0.0, op0=ALU.mult, op1=ALU.add), [T1k, "rmask512"], [T3k])
                    kb.op(act, lambda: A.activation(out=T1, in_=T3, func=AF.Exp), [T3k], [T1k])
                    kb.op(act, lambda: A.activation(out=T3, in_=T3, func=AF.Exp, scale=-1.0), [T3k], [T3k])
                    vop(lambda: V.tensor_tensor(out=Qt, in0=T4, in1=T1, op=ALU.mult), [T4k, T1k], [qkk])
                    vop(lambda: V.tensor_tensor(out=T2, in0=T2, in1=T3, op=ALU.mult), [T2k, T3k], [T2k])
                    vop(lambda: V.tensor_copy(out=Kt, in_=T2), [T2k], [qkk])
                    eb3 = T1.rearrange("p (c j) -> p c j", j=16)
                    vop(lambda: V.tensor_tensor(out=Kh32.rearrange("p (c j) -> p c j", j=16), in0=T2.rearrange("p (c j) -> p c j", j=16),
                                                in1=eb3[:, :, 15:16].broadcast_to([128, 32, 16]), op=ALU.mult), [T2k, T1k], [Kh32k])
                    zfm(4, h, t, bank[0], "b0")
                    kb.op(act, lambda: A.activation(out=Tg, in_=bank[0][:, :], func=AF.Silu), ["b0"], [Tgk])
                    ob, obk = bank[7], "b7"

                    def Sc(c):
                        if c == 0 or c == 8:
                            return Sf[:, h, :], "Sf"
                        if c <= 4:
                            return T4[:, (c - 1) * 128:c * 128], T4k
                        return T2[:, (c - 5) * 128:(c - 4) * 128], T2k

                    def ubank(tb):
                        return ((bank[5], "b5"), (bank[6], "b6")) if tb % 2 == 0 else ((bank[0], "b0"), (bank[1], "b1"))

                    def hfront(tb):
                        cs = slice(tb * 128, (tb + 1) * 128)
                        p_ = tb % 2
                        scT_, Kht_ = m7b[:, 768 * p_:768 * p_ + 128], m7b[:, 768 * p_ + 128:768 * p_ + 256]
                        pk_ = m7k + f"_p{p_}"
                        kb.mm(bank[3][:, 0:128], Kt[:, cs], Qt[:, cs], True, True, [qkk], ["b3"])
                        vop(lambda: V.tensor_tensor(out=scT_, in0=bank[3][:, 0:128], in1=BD16[:], op=ALU.mult), ["b3", "BD16"], [pk_])
                        kb.op(pe, lambda: T.transpose(bank[4][:, 0:128], Kh32[:, cs], ident[:]), [Kh32k, "ident"], ["b4"])
                        kb.op(act, lambda: A.copy(out=Kht_, in_=bank[4][:, 0:128]), ["b4"], [pk_])
                        vop(lambda: V.tensor_tensor(out=Vexp, in0=vtm[tb][:, None, h * 128:(h + 1) * 128].broadcast_to([128, 8, 128]),
                                                    in1=cm8[:, :, None].broadcast_to([128, 8, 128]), op=ALU.mult), [vtk[tb], "cm8"], [vxk])
                        Vf = Vexp.rearrange("p c v -> p (c v)")
                        (u0, u0k), (u1, u1k) = ubank(tb)
                        kb.mm(u0[:, :], Kht_, Vf[:, 0:512], True, True, [pk_, vxk], [u0k])
                        kb.mm(u1[:, :], Kht_, Vf[:, 512:1024], True, True, [pk_, vxk], [u1k])

                    def hmid(tb):
                        ub2 = ubank(tb)
                        kb.op(act, lambda: A.copy(out=Sb[:, 0, :], in_=Sf[:, h, :]), ["Sf"], [sbk])
                        for c in range(8):
                            ub, ubk = ub2[0] if c < 4 else ub2[1]
                            col = tb * 128 + c * 16 + 15
                            src, srck = Sc(c)
                            dst, dstk = Sc(c + 1)
                            vop(lambda: V.scalar_tensor_tensor(out=dst, in0=src, scalar=T1[:, col:col + 1], in1=ub[:, (c % 4) * 128:(c % 4 + 1) * 128],
                                                               op0=ALU.mult, op1=ALU.add), [srck, T1k, ubk], [dstk], ss=bool(_os_ss))
                        kb.op(act, lambda: A.copy(out=Sb[:, 1:5, :].rearrange("p c v -> p (c v)"), in_=T4), [T4k], [sbk])
                        kb.op(act, lambda: A.copy(out=Sb[:, 5:8, :].rearrange("p c v -> p (c v)"), in_=T2[:, 0:384]), [T2k], [sbk])

                    def hback(tb):
                        cs = slice(tb * 128, (tb + 1) * 128)
                        p_ = tb % 2
                        scT_ = m7b[:, 768 * p_:768 * p_ + 128]
                        pk_ = m7k + f"_p{p_}"
                        kb.mm(ob[:, cs], vtm[tb][:, h * 128:(h + 1) * 128], scT_, True, False, [vtk[tb], pk_], [obk], inc=True)
                        for c in range(8):
                            kb.mm(ob[:, tb * 128 + c * 16:tb * 128 + (c + 1) * 16], Sb[:, c, :], Qt[:, tb * 128 + c * 16:tb * 128 + (c + 1) * 16],
                                  False, c == 7, [sbk, qkk], [obk], inc=True)

                    hfront(0)
                    for tb in range(4):
                        if tb + 1 < 4:
                            hfront(tb + 1)
                        hmid(tb)
                        hback(tb)
                    kb.op(act, lambda: A.activation(out=T2, in_=ob[:, :], func=AF.Square), [obk], [T2k])
                    kb.mm(bank[2][:, :], ones[:], T2, True, True, ["ones", T2k], ["b2"])
                    kb.op(act, lambda: A.activation(out=T3, in_=bank[2][:, :], func=AF.Ln, scale=1.0 / 128.0, bias=epsc[:, 0:1]), ["b2", "epsc"], [T3k])
                    kb.op(act, lambda: A.activation(out=T3, in_=T3, func=AF.Exp, scale=-0.5), [T3k], [T3k])
                    vop(lambda: V.tensor_tensor(out=T2, in0=T3, in1=ob[:, :], op=ALU.mult), [T3k, obk], [T2k])
                    vop(lambda: V.scalar_tensor_tensor(out=ymix[:, 4 + h, :], in0=T2, scalar=ongcol, in1=Tg, op0=ALU.mult, op1=ALU.mult),
                        [T2k, "pcol", Tgk], ["ymix"])
                wout_apply(t)

        for s in range(NSUP):
            if s > 0:
                load_x(s)
            FF = 'ffn' in stages
            for l in range(2):
                mix = ('mix0' in stages and l == 0) or ('mix1' in stages and l == 1)
                if FF:
                    if l == 0:
                        prenorm(0, 0)
                    ffn(l * 2 + 0)
                    for t in range(NT):
                        postnorm_add(l, 1, True, [t])
                        if mix:
                            prenorm(l, 2, [t])
                elif mix:
                    prenorm(l, 2)
                if mix:
                    (mixer0 if l == 0 else mixer1)(s)
                    for t in range(NT):
                        postnorm_add(l, 3, False, [t])
                        if FF:
                            prenorm(l, 4, [t])
                elif FF:
                    prenorm(l, 4)
                if FF:
                    ffn(l * 2 + 1)
                    for t in range(NT):
                        postnorm_add(l, 5, True, [t])
                        if l == 0:
                            prenorm(1, 0, [t])
            store_x(s)
        kb.finish(["xout0", "xout1"] )
        stuck = kb.simulate()
        if stuck:
            raise RuntimeError(f"semaphore deadlock: {stuck}")
    return nc


_CACHE = {}


def make_common(inputs):
    f = lambda k, shp: np.ascontiguousarray(inputs[k], dtype=np.float32).reshape(shp)
    return {
        "norm_g": f("norm_g", (96, 128)),
        "ffn_wg": f("ffn_wg", (4, D, DFF)), "ffn_wu": f("ffn_wu", (4, D, DFF)), "ffn_wd": f("ffn_wd", (4, DFF, D)),
        "even_w_in": f("even_w_in", (D, 1536)), "even_w_out": f("even_w_out", (D, D)),
        "s5_lam_re": f("s5_lam_re", (32, 64)), "s5_lam_im": f("s5_lam_im", (32, 64)), "s5_log_dt": f("s5_log_dt", (1, 32)),
        "s5_b_re": f("s5_b_re", (32, 64, 16)), "s5_b_im": f("s5_b_im", (32, 64, 16)),
        "s5_c_re": f("s5_c_re", (512, 64)), "s5_c_im": f("s5_c_im", (512, 64)),
        "s5_d": f("s5_d", (4, 128)), "s5_w_glu": f("s5_w_glu", (512, 512)),
        "sgu_norm_g": f("sgu_norm_g", (1, 512)), "sgu_w": f("sgu_w", (4, 128, 128)), "sgu_b": f("sgu_b", (1, 512)),
        "odd_w_in": f("odd_w_in", (D, 2560)), "odd_w_out": f("odd_w_out", (D, D)),
        "pool_w": f("pool_w", (4, 128, 128)), "pool_scale": f("pool_scale", (4, 128)),
        "hgrn_lb": f("hgrn_lb", (8, 128)), "hgrn_onorm_g": f("hgrn_onorm_g", (1, 128)),
    }


def kernel(**inputs):
    x = np.ascontiguousarray(inputs["x"], dtype=np.float32)
    B = x.shape[0]
    if "nc" not in _CACHE:
        _CACHE["nc"] = build(stages=("ffn", "mix0", "mix1"))
    nc = _CACHE["nc"]
    common = make_common(inputs)
    active = [0, 1, 4, 5]
    zeros = {k: np.zeros_like(v) for k, v in common.items()}
    zx = np.zeros_like(x[0])
    in_maps = []
    for c in range(8):
        if c in active:
            m = dict(common)
            m["x"] = x[active.index(c)]
        else:
            m = dict(zeros)
            m["x"] = zx
        in_maps.append(m)
    res = run_bass_kernel_spmd(nc, in_maps, core_ids=list(range(8)))
    out = np.stack([res.results[active[b]]["out"] for b in range(B)], axis=0)
    return out.astype(np.float32)
```

```python
import math
from contextlib import ExitStack
import numpy as np
import concourse.bass as bass
import concourse.mybir as mybir
from concourse.bass_utils import run_bass_kernel_spmd

F32 = mybir.dt.float32
BF16 = mybir.dt.bfloat16
I32 = mybir.dt.int32
AF = mybir.ActivationFunctionType
ALU = mybir.AluOpType

D = 1024
KD = 8
DFF = 2816
NF = 22
SEQ = 4096
TL = 512
EPS = 1e-6
TWO_PI = 2.0 * math.pi
DBG_MAP = {}


class Eng:
    def __init__(self, name, h, sem):
        self.name, self.h, self.sem = name, h, sem
        self.count = 0
        self.waited = {}
        self.pending = False


class DSem:
    def __init__(self, sem):
        self.sem = sem
        self.count = 0


class KB:
    def __init__(self, nc, es):
        self.nc, self.es = nc, es
        mk = lambda n: es.enter_context(nc.semaphore(n))
        self.pe = Eng("pe", nc.tensor, mk("s_pe"))
        self.act = Eng("act", nc.scalar, mk("s_act"))
        self.dve = Eng("dve", nc.vector, mk("s_dve"))
        self.pool = Eng("pool", nc.gpsimd, mk("s_pool"))
        self.sp = Eng("sp", nc.sync, mk("s_sp"))
        self.writer = {}
        self.readers = {}
        self.dsems = {}
        self.alias = {}
        self.selfsync_all = False
        self.free_ds = []
        self.prog = {}
        self.trace = []

    def simulate(self):
        pcs = {e: 0 for e in self.prog}
        sems = {}
        progress = True
        while progress:
            progress = False
            for e, lst in self.prog.items():
                while pcs[e] < len(lst):
                    kind, sid, val, tag = lst[pcs[e]]
                    if kind == "wait":
                        if sems.get(sid, 0) >= val:
                            pcs[e] += 1
                            progress = True
                        else:
                            break
                    else:
                        sems[sid] = sems.get(sid, 0) + val
                        pcs[e] += 1
                        progress = True
        stuck = {e: (pcs[e], len(l), l[pcs[e]]) for e, l in self.prog.items() if pcs[e] < len(l)}
        return stuck

    def _ex(self, keys):
        out = []
        for k in keys:
            if k in self.alias:
                out.extend(self.alias[k])
            else:
                out.append(k)
        return out

    def sb(self, name, shape, dt):
        return self.es.enter_context(self.nc.sbuf_tensor(name, shape, dt))

    def ps(self, name, shape, dt=F32):
        return self.es.enter_context(self.nc.psum_tensor(name, shape, dt))

    def dsem(self, key):
        if key not in self.dsems:
            if self.free_ds:
                self.dsems[key] = self.free_ds.pop()
            else:
                self.dsems[key] = DSem(self.es.enter_context(self.nc.semaphore("d_" + key)))
                self.nsem = getattr(self, "nsem", 5) + 1
                assert self.nsem <= 24, "semaphore budget (24) exceeded"
        return self.dsems[key]

    def release(self, keys):
        for k in keys:
            if k in self.dsems:
                self.free_ds.append(self.dsems.pop(k))

    def _deps(self, eng, R, W, selfsync=False):
        need = {}
        selfsync = selfsync or self.selfsync_all
        def add(src, val):
            if src is eng and not (selfsync and eng is not self.pe):
                return
            k = id(src)
            if k not in need or need[k][1] < val:
                need[k] = (src, val)
        for r in R:
            w = self.writer.get(r)
            if w:
                add(*w)
        for w_ in W:
            w = self.writer.get(w_)
            if w:
                add(*w)
            for rd in self.readers.get(w_, ()):
                add(*rd)
        for k, (src, val) in need.items():
            if eng.waited.get(k, 0) < val:
                eng.h.wait_ge(src.sem, val)
                eng.waited[k] = val
                self.prog.setdefault(eng.name, []).append(("wait", id(src), val, getattr(src, "name", "dsem")))

    def _record(self, src, val, R, W):
        for r in R:
            self.readers.setdefault(r, []).append((src, val))
        for w in W:
            self.writer[w] = (src, val)
            self.readers[w] = []

    def op(self, eng, fn, R, W, inc=True, ss=False):
        R, W = self._ex(R), self._ex(W)
        isbank = lambda k: len(k) >= 2 and k[0] == "b" and k[1:].isdigit()
        W = list(W) + [r for r in R if isbank(r)]
        R = [r for r in R if not isbank(r)]
        self._deps(eng, R, W, ss)
        ins = fn()
        if inc:
            ins.then_inc(eng.sem, 1)
            eng.count += 1
            self.prog.setdefault(eng.name, []).append(("inc", id(eng), 1, str(W[:1])))
            self._record(eng, eng.count, R, W)
        else:
            self._record(eng, eng.count + 1, R, W)
        return ins

    def mm(self, out, lhsT, rhs, start, stop, R, W, inc=None):
        if inc is None:
            inc = stop
        return self.op(self.pe, lambda: self.nc.tensor.matmul(out, lhsT, rhs, start=start, stop=stop), R, W, inc=inc)

    def dma(self, q, out, in_, R, W, skey):
        ds = self.dsem(skey)
        R, W = self._ex(R), self._ex(W)
        self._deps(q, R, W)
        q.h.dma_start(out=out, in_=in_).then_inc(ds.sem, 16)
        ds.count += 16
        self.prog.setdefault(q.name, []).append(("inc", id(ds), 16, skey))
        self._record(ds, ds.count, R, W)

    def finish(self, keys):
        for k in keys:
            ds = self.dsems[k]
            self.nc.sync.wait_ge(ds.sem, ds.count)


def build(NSUP=None, S=1024, dbg=None, stages=('ffn',)):
    NT = S // TL
    if NSUP is None:
        NSUP = SEQ // S
    LTOK = NSUP * S
    nc = bass.Bass("TRN2", target_bir_lowering=False)
    dr = lambda n, shp, kind="ExternalInput": nc.dram_tensor(n, shp, F32, kind=kind).ap()
    x_d = dr("x", [LTOK, D])
    out_d = dr("out", [LTOK, D], "ExternalOutput")
    norm_g_d = dr("norm_g", [96, 128])
    wg_d = dr("ffn_wg", [4, D, DFF])
    wu_d = dr("ffn_wu", [4, D, DFF])
    wd_d = dr("ffn_wd", [4, DFF, D])
    ewin_d = dr("even_w_in", [D, 1536])
    ewout_d = dr("even_w_out", [D, D])
    lre_d = dr("s5_lam_re", [32, 64])
    lim_d = dr("s5_lam_im", [32, 64])
    ldt_d = dr("s5_log_dt", [1, 32])
    bre_d = dr("s5_b_re", [32, 64, 16])
    bim_d = dr("s5_b_im", [32, 64, 16])
    cre_d = dr("s5_c_re", [512, 64])
    cim_d = dr("s5_c_im", [512, 64])
    s5d_d = dr("s5_d", [4, 128])
    glu_d = dr("s5_w_glu", [512, 512])
    sng_d = dr("sgu_norm_g", [1, 512])
    sgw_d = dr("sgu_w", [4, 128, 128])
    sgb_d = dr("sgu_b", [1, 512])
    owin_d = dr("odd_w_in", [D, 2560])
    owout_d = dr("odd_w_out", [D, D])
    poolw_d = dr("pool_w", [4, 128, 128])
    pscale_d = dr("pool_scale", [4, 128])
    hlb_d = dr("hgrn_lb", [8, 128])
    hong_d = dr("hgrn_onorm_g", [1, 128])
    dbg_d = dr("dump_out", [128, 16384], "ExternalOutput") if dbg else None
    DBG_MAP.clear()

    es = ExitStack()
    with es:
        kb = KB(nc, es)
        pe, act, dve, pool, sp = kb.pe, kb.act, kb.dve, kb.pool, kb.sp
        V, A, G, T = nc.vector, nc.scalar, nc.gpsimd, nc.tensor
        dbg_state = {"off": 0, "n": 0}

        def dump(name, ap, keys, ncols, nrows=128):
            import os as _o
            if dbg_d is None or name in DBG_MAP or (_o.environ.get("DUMPS") and name not in _o.environ["DUMPS"].split(",")):
                return
            o = dbg_state["off"]
            DBG_MAP[name] = (o, ncols, nrows)
            dbg_state["off"] += ncols
            kb.dma(pool, dbg_d[0:nrows, o:o + ncols], ap, keys, [], "xout0")

        x_sb = kb.sb("x_sb", [128, KD, S], F32)
        hn = kb.sb("hn", [128, KD, S], BF16)
        yacc = kb.sb("yacc", [128, KD, S], F32)
        ablk_all = kb.sb("ablk_all", [128, 2, 4, TL], BF16)
        ablk = [ablk_all[:, i] for i in range(2)]
        wgs = [kb.sb(f"wgs{i}", [128, KD, 512], BF16) for i in range(2)]
        wus = [kb.sb(f"wus{i}", [128, KD, 512], BF16) for i in range(2)]
        wds = [kb.sb(f"wds{i}", [128, 4, D], BF16) for i in range(2)]
        sgt = [kb.sb(f"sgt{i}", [128, TL], F32) for i in range(2)]
        sqt = [kb.sb(f"sqt{i}", [128, TL], F32) for i in range(2)]
        rstd = kb.sb("rstd", [128, TL], F32)
        tmpn = sgt
        assert S == 1024
        xin = [yacc[:, i, :] for i in range(2)]
        xout = [yacc[:, 2 + i, :] for i in range(2)]
        XK = lambda m: [f"ya0_{m}", f"ya1_{m}"]
        ymix = kb.sb("ymix", [128, KD, TL], BF16)
        ubf = ablk[0]
        ident = kb.sb("ident", [128, 128], F32)
        ones = kb.sb("ones", [128, 128], F32)
        prow = kb.sb("prow", [128, 128], F32)
        pcol = kb.sb("pcol", [128, 128], F32)
        iot = kb.sb("iot", [128, 128], F32)
        pidx = kb.sb("pidx", [128, 1], F32)
        epsc = kb.sb("epsc", [128, 1], F32)
        bank = [kb.ps(f"bank{i}", [128, TL]) for i in range(8)]

        for t_ in range(NT):
            kb.alias[f"ya{t_}_4"] = [f"ya{t_}_4_{r}{c}" for r in (0, 1) for c in "ab"]
            kb.alias[f"ya{t_}_5"] = [f"ya{t_}_5_{n}" for n in ("G0", "G1", "C0c", "C0s", "C1c", "C1s")]
            kb.alias[f"ya{t_}_7"] = [f"ya{t_}_7_d", f"ya{t_}_7_p0", f"ya{t_}_7_p1"]
        kb.op(pool, lambda: G.iota(iot[:], pattern=[[1, 128]], base=0, channel_multiplier=0,
                                   allow_small_or_imprecise_dtypes=True), [], ["iot"])
        kb.op(pool, lambda: G.iota(pidx[:], pattern=[[0, 1]], base=0, channel_multiplier=1,
                                   allow_small_or_imprecise_dtypes=True), [], ["pidx"])
        kb.op(dve, lambda: V.tensor_scalar(out=ident[:], in0=iot[:], scalar1=pidx[:, 0:1], scalar2=None,
                                           op0=ALU.is_equal), ["iot", "pidx"], ["ident"])
        kb.op(dve, lambda: V.memset(ones[:], 1.0), [], ["ones"])
        kb.op(dve, lambda: V.memset(epsc[:], EPS), [], ["epsc"])
        kb.op(dve, lambda: V.memset(prow[:], 0.0), [], ["prow"])
        kb.dma(pool, prow[0:96, :], norm_g_d[:, :], [], ["prow"], "prow")
        kb.dma(pool, prow[96:100, :], s5d_d[:, :], [], ["prow"], "prow")
        sngb = kb.sb("sngb", [128, 512], F32)
        dt_early = kb.sb("s5_dt", [128, 32], F32)
        st32 = sqt[1][0:32, 0:256]
        kb.alias["sqt1"] = ["sqt1_main", "st32"]
        BK = lambda i: [f"ya0_{i}", f"ya1_{i}"]
        HK = ["hn0", "hn1"]
        stA = yacc[:, 4, :]
        stB = yacc[:, 5, :]
        st_ri = yacc[0:32, 6:8, :]
        st_ir = yacc[0:32, 2:4, :]
        K_ri, K_ir = BK(6) + BK(7), BK(2) + BK(3)
        kb.dma(sp, sngb[:, :], sng_d[0:1, :].broadcast_to([128, 512]), [], ["sngb"], "sngb")
        kb.dma(sp, dt_early[:], ldt_d[0:1, :].broadcast_to([128, 32]), [], ["sm_dt"], "sm_dt")
        kb.dma(sp, stB[:, 512:1024], sgb_d[0:1, :].broadcast_to([128, 512]), [], BK(5), "brow")
        for ni, src in enumerate((lre_d, lim_d)):
            kb.dma(pool, st32[:, ni * 128:ni * 128 + 64], src[:, :], [], ["st32"], "st32")
            kb.dma(pool, st32[:, ni * 128 + 64:ni * 128 + 128], src[:, :], [], ["st32"], "st32")
        bre_n = bre_d.rearrange("g p c -> g (p c)")
        bim_n = bim_d.rearrange("g p c -> g (p c)")
        kb.dma(pool, st_ri[:, 0, :], bre_n, [], K_ri, "p1")
        kb.dma(pool, st_ri[:, 1, :], bim_n, [], K_ri, "p1")
        kb.dma(pool, st_ir[:, 0, :], bim_n, [], K_ir, "p2")
        kb.dma(pool, st_ir[:, 1, :], bre_n, [], K_ir, "p2")
        for which in range(2):
            for q in range(4):
                CQ_ = stA[:, (which * 4 + q) * 128:(which * 4 + q + 1) * 128]
                a_, b_ = (cre_d, cim_d) if which == 0 else (cim_d, cre_d)
                kb.dma(pool, CQ_[:, 0:64], a_[q * 128:(q + 1) * 128, :], [], BK(4), "cq")
                kb.dma(pool, CQ_[:, 64:128], b_[q * 128:(q + 1) * 128, :], [], BK(4), "cq")
        for h in range(4):
            kb.dma(pool, stB[:, h * 128:(h + 1) * 128], sgw_d[h, :, :], [], BK(5), "sgw")
        kb.dma(pool, prow[100:104, :], pscale_d[:, :], [], ["prow"], "prow")
        kb.dma(pool, prow[104:112, :], hlb_d[:, :], [], ["prow"], "prow")
        kb.dma(pool, prow[112:113, :], hong_d[:, :], [], ["prow"], "prow")
        poolw_stg = ablk_all[:].bitcast(F32).rearrange("p a b c -> p (a b c)")[:, 0:512].rearrange("p (g o) -> p g o", o=128)
        kb.dma(pool, poolw_stg, poolw_d.rearrange("g i o -> i g o"), [], ["ablk0", "ablk1"], "poolw")

        def gcol(l, i, k):
            c = (l * 6 + i) * 8 + k
            return pcol[:, c:c + 1]

        state = {"sq": 0, "sg": 0, "ab": 0, "w": 0, "xi": 0, "xo": 0, "tn": 0, "dn": 0}

        def rms_stats(src_fn, t):
            pbank, pkey = bank[6 + t % 2], f"b{6 + t % 2}"
            for k in range(KD):
                i = state["sq"] % 2
                state["sq"] += 1
                src, sk = src_fn(k)
                sqb = sqt[i][:].bitcast(BF16)[:, 0:TL]
                kb.op(act, lambda: A.activation(out=sqb, in_=src, func=AF.Square), [sk], [f"sqt{i}"])
                kb.mm(pbank[:, :], onesb[:, :], sqb, k == 0, k == KD - 1, ["onesb", f"sqt{i}"], [pkey], inc=True)
            kb.op(act, lambda: A.activation(out=pbank[:, :], in_=pbank[:, :], func=AF.Ln, scale=128.0 / D,
                                            bias=epsc[:, 0:1]), [pkey, "epsc"], [pkey])
            kb.op(act, lambda: A.activation(out=pbank[:, :], in_=pbank[:, :], func=AF.Exp, scale=-0.5), [pkey], [pkey])
            return pbank, pkey

        def prenorm(l, i):
            for t in range(NT):
                ts = slice(t * TL, (t + 1) * TL)
                rb, rk = rms_stats(lambda k: (x_sb[:, k, ts], f"x{t}"), t)
                for k in range(KD):
                    kb.op(dve, lambda: V.scalar_tensor_tensor(out=hn[:, k, ts], in0=x_sb[:, k, ts],
                                                              scalar=gcol(l, i, k), in1=rb[:, :],
                                                              op0=ALU.mult, op1=ALU.mult),
                          [f"x{t}", rk, "pcol"], [f"hn{t}"])

        def postnorm_add(l, i, half, tiles=None):
            for t in (range(NT) if tiles is None else tiles):
                ts = slice(t * TL, (t + 1) * TL)
                rb, rk = rms_stats(lambda k: (yacc[:, k, ts], f"ya{t}_{k}"), t)
                for k in range(KD):
                    j = state["tn"] % 2
                    state["tn"] += 1
                    kb.op(dve, lambda: V.scalar_tensor_tensor(out=tmpn[j][:], in0=yacc[:, k, ts],
                                                              scalar=gcol(l, i, k), in1=rb[:, :],
                                                              op0=ALU.mult, op1=ALU.mult),
                          [f"ya{t}_{k}", rk, "pcol"], [f"sgt{j}"])
                    kb.op(dve, lambda: V.scalar_tensor_tensor(out=x_sb[:, k, ts], in0=tmpn[j][:],
                                                              scalar=0.5 if half else 1.0, in1=x_sb[:, k, ts],
                                                              op0=ALU.mult, op1=ALU.add),
                          [f"sgt{j}", f"x{t}"], [f"x{t}"])

        def ffn(fi):
            blocks = [(c0, min(512, DFF - c0)) for c0 in range(0, DFF, 512)]
            stages = []
            for bi, (c0, cw) in enumerate(blocks):
                w = state["w"] % 2
                state["w"] += 1
                for t in range(NT):
                    ab = state["ab"] % 2
                    state["ab"] += 1
                    stages.append((bi, c0, cw, w, t, ab))

            def loadw(bi, c0, cw, w):
                nj = cw // 128
                for k in range(KD):
                    kb.dma(pool, wgs[w][:, k, 0:cw], wg_d[fi, k * 128:(k + 1) * 128, c0:c0 + cw], [], [f"wgs{w}"], f"wgs{w}")
                    kb.dma(pool, wus[w][:, k, 0:cw], wu_d[fi, k * 128:(k + 1) * 128, c0:c0 + cw], [], [f"wus{w}"], f"wus{w}")
                for j in range(nj):
                    kb.dma(pool, wds[w][:, j, :], wd_d[fi, c0 + j * 128:c0 + (j + 1) * 128, :], [], [f"wds{w}"], f"wds{w}")

            def gu(bi, c0, cw, w, t, ab):
                ts = slice(t * TL, (t + 1) * TL)
                nj = cw // 128
                for j in range(nj):
                    pg, pu = bank[(2 * j) % 4], bank[(2 * j + 1) % 4]
                    kg, ku = f"b{(2 * j) % 4}", f"b{(2 * j + 1) % 4}"
                    for k in range(KD):
                        kb.mm(pg[:, :], wgs[w][:, k, j * 128:(j + 1) * 128], hn[:, k, ts], k == 0, k == KD - 1,
                              [f"wgs{w}", f"hn{t}"], [kg])
                    for k in range(KD):
                        kb.mm(pu[:, :], wus[w][:, k, j * 128:(j + 1) * 128], hn[:, k, ts], k == 0, k == KD - 1,
                              [f"wus{w}", f"hn{t}"], [ku])
                    s_ = state["sg"] % 2
                    state["sg"] += 1
                    kb.op(act, lambda: A.activation(out=sgt[s_][:], in_=pg[:, :], func=AF.Silu), [kg], [f"sgt{s_}"])
                    kb.op(dve, lambda: V.tensor_tensor(out=ablk[ab][:, j, :], in0=sgt[s_][:], in1=pu[:, :], op=ALU.mult),
                          [f"sgt{s_}", ku], [f"ablk{ab}"])

            def down(bi, c0, cw, w, t, ab):
                ts = slice(t * TL, (t + 1) * TL)
                nj = cw // 128
                for m in range(KD):
                    bn = 4 + (state["dn"] % 3)
                    state["dn"] += 1
                    py, ky = bank[bn], f"b{bn}"
                    for j in range(nj):
                        kb.mm(py[:, :], wds[w][:, j, m * 128:(m + 1) * 128], ablk[ab][:, j, :], j == 0, j == nj - 1,
                              [f"wds{w}", f"ablk{ab}"], [ky])
                    if bi == 0:
                        kb.op(act, lambda: A.copy(out=yacc[:, m, ts], in_=py[:, :]), [ky], [f"ya{t}_{m}"])
                    else:
                        kb.op(dve, lambda: V.tensor_tensor(out=yacc[:, m, ts], in0=yacc[:, m, ts], in1=py[:, :], op=ALU.add),
                              [ky, f"ya{t}_{m}"], [f"ya{t}_{m}"])

            for i in range(len(stages) + 1):
                if i < len(stages):
                    if stages[i][4] == 0:
                        loadw(*stages[i][:4])
                    gu(*stages[i])
                if i > 0:
                    down(*stages[i - 1])

        def load_x(s):
            for b in range(S // 128):
                i = state["xi"] % 2
                state["xi"] += 1
                r0 = s * S + b * 128
                kb.dma(sp, xin[i], x_d[r0:r0 + 128, :], [], XK(i), f"xin{i}")
                t = (b * 128) // TL
                for k in range(KD):
                    pb = bank[4 + k % 4]
                    kb.op(pe, lambda: T.transpose(pb[:, 0:128], xin[i][:, k * 128:(k + 1) * 128], ident[:]),
                          XK(i) + ["ident"], [f"b{4 + k % 4}"])
                    eng, fn = (act, lambda: A.copy(out=x_sb[:, k, b * 128:(b + 1) * 128], in_=pb[:, 0:128])) if k % 2 == 0 else \
                              (dve, lambda: V.tensor_copy(out=x_sb[:, k, b * 128:(b + 1) * 128], in_=pb[:, 0:128]))
                    kb.op(eng, fn, [f"b{4 + k % 4}"], [f"x{t}"])

        def store_x(s):
            for b in range(S // 128):
                i = state["xo"] % 2
                state["xo"] += 1
                r0 = s * S + b * 128
                t = (b * 128) // TL
                for k in range(KD):
                    pb = bank[4 + k % 4]
                    kb.op(pe, lambda: T.transpose(pb[:, 0:128], x_sb[:, k, b * 128:(b + 1) * 128], ident[:]),
                          [f"x{t}", "ident"], [f"b{4 + k % 4}"])
                    eng, fn = (act, lambda: A.copy(out=xout[i][:, k * 128:(k + 1) * 128], in_=pb[:, 0:128])) if k % 2 == 0 else \
                              (dve, lambda: V.tensor_copy(out=xout[i][:, k * 128:(k + 1) * 128], in_=pb[:, 0:128]))
                    kb.op(eng, fn, [f"b{4 + k % 4}"], XK(2 + i))
                kb.dma(sp, out_d[r0:r0 + 128, :], xout[i], XK(2 + i), [], f"xout{i}")


        load_x(0)
        kb.op(pe, lambda: T.transpose(bank[7][:, 0:128], prow[:], ident[:]), ["prow", "ident"], ["b7"])
        kb.op(dve, lambda: V.tensor_copy(out=pcol[:], in_=bank[7][:, 0:128]), ["b7"], ["pcol"])
        def slab(t, i):
            return yacc[:, i, t * TL:(t + 1) * TL], f"ya{t}_{i}"

        s5dcol = lambda q: pcol[:, 96 + q:97 + q]

        sm = {}
        _tmp_names = ["lr", "li", "th", "ar", "ai", "den", "cre", "cim", "t0", "t1", "t2", "cbri", "cair", "ang"]
        kb.alias["sqt0"] = ["sqt0_main", "tmp_s", "sin_aux"] + ["sm_" + n for n in _tmp_names]
        def small(name, cols=32):
            if name in _tmp_names:
                i_ = _tmp_names.index(name)
                sm[name] = sqt[0][:, i_ * 32:(i_ + 1) * 32]
            else:
                sm[name] = kb.sb("s5_" + name, [128, cols], F32)
            return sm[name]
        sm["dt"] = dt_early
        for n_ in ["lr", "li", "dec", "th", "ar", "ai", "den", "cre", "cim", "t0", "t1", "t2", "cr", "ci",
                   "cbri", "cair", "ang"]:
            small(n_)
        sgn = kb.sb("sgn", [128, 1], F32)
        rmask = kb.sb("rmask", [128, 4], F32)
        cmask = kb.sb("cmask", [128, 4, 64], F32)
        pmat = kb.sb("pmat", [128, 128], F32)
        maskU = kb.sb("maskU", [128, 128], F32)
        COS = kb.sb("COS", [128, 32, 128], BF16)
        SIN = kb.sb("SIN", [128, 32, 128], BF16)
        BTRI = [kb.sb(f"BTRI{v}", [128, 4, 128], BF16) for v in range(4)]
        BTIR = [kb.sb(f"BTIR{v}", [128, 4, 128], BF16) for v in range(4)]
        W1s = kb.sb("W1s", [128, 32, 64], BF16)
        W2s = kb.sb("W2s", [128, 32, 64], BF16)
        s5init = kb.sb("s5init", [128, 32], F32)
        s5tmp = kb.sb("s5tmp", [128, 32], F32)
        WmT = kb.sb("WmT", [128, 4, 128], BF16)
        browb = kb.sb("browb", [128, 512], BF16)
        onesb = kb.sb("onesb", [128, 128], BF16)
        vnbf = [kb.sb(f"vnbf{i}", [128, 512], BF16) for i in range(2)]
        bnst = kb.sb("bnst", [128, 8], F32)
        mv = kb.sb("mv", [128, 4], F32)
        big = [hn[:, 2 * i:2 * i + 2, :].bitcast(F32).rearrange("p a b -> p (a b)") for i in range(4)]

        SM = lambda *names: ["sm_" + n for n in names]
        def vop(fn, R, W, eng=None, ss=False):
            kb.op(eng or dve, fn, R, W, ss=ss)
        kb.selfsync_all = True

        for ni, (name, src) in enumerate((("lr", lre_d), ("li", lim_d))):
            kb.op(pe, lambda: T.transpose(bank[6][:, 0:32], st32[:, ni * 128:(ni + 1) * 128], ident[0:32, 0:32]), ["st32", "ident"], ["b6"])
            vop(lambda: V.tensor_copy(out=sm[name][:], in_=bank[6][:, 0:32]), ["b6"], SM(name))
        kb.op(act, lambda: A.activation(out=sm["dt"][:], in_=sm["dt"][:], func=AF.Exp), SM("dt"), SM("dt"))
        vop(lambda: V.tensor_tensor(out=sm["t0"][:], in0=sm["lr"][:], in1=sm["dt"][:], op=ALU.mult), SM("lr", "dt"), SM("t0"))
        kb.op(act, lambda: A.activation(out=sm["dec"][:], in_=sm["t0"][:], func=AF.Exp), SM("t0"), SM("dec"))
        vop(lambda: V.tensor_tensor(out=sm["th"][:], in0=sm["li"][:], in1=sm["dt"][:], op=ALU.mult), SM("li", "dt"), SM("th"))

        def sin_of(out_ap, ang_ap, tmp_i, tmp_f, shift, R, W, keys_tmp):
            aux = out_ap_f32[0]
            vop(lambda: V.tensor_scalar(out=tmp_f, in0=ang_ap, scalar1=shift, scalar2=1.0 / TWO_PI, op0=ALU.add, op1=ALU.mult), R, keys_tmp)
            vop(lambda: V.tensor_copy(out=tmp_i, in_=tmp_f), keys_tmp, keys_tmp)
            vop(lambda: V.tensor_copy(out=tmp_f, in_=tmp_i), keys_tmp, keys_tmp)
            vop(lambda: V.tensor_scalar(out=tmp_f, in0=tmp_f, scalar1=-TWO_PI, scalar2=shift, op0=ALU.mult, op1=ALU.add), keys_tmp, keys_tmp)
            vop(lambda: V.tensor_tensor(out=tmp_f, in0=tmp_f, in1=ang_ap, op=ALU.add), keys_tmp + R, keys_tmp)
            vop(lambda: V.tensor_scalar(out=aux, in0=tmp_f, scalar1=math.pi, scalar2=-TWO_PI, op0=ALU.is_gt, op1=ALU.mult), keys_tmp, ["sin_aux"])
            vop(lambda: V.tensor_tensor(out=tmp_f, in0=tmp_f, in1=aux, op=ALU.add), keys_tmp + ["sin_aux"], keys_tmp)
            vop(lambda: V.tensor_scalar(out=aux, in0=tmp_f, scalar1=-math.pi, scalar2=TWO_PI, op0=ALU.is_lt, op1=ALU.mult), keys_tmp, ["sin_aux"])
            vop(lambda: V.tensor_tensor(out=tmp_f, in0=tmp_f, in1=aux, op=ALU.add), keys_tmp + ["sin_aux"], keys_tmp)
            kb.op(act, lambda: A.activation(out=out_ap, in_=tmp_f, func=AF.Sin), keys_tmp, W)

        out_ap_f32 = [None]
        aux_s = sqt[0][:, 14 * 32:15 * 32]
        tmp_s = sqt[0][:, 15 * 32:16 * 32]
        tmp_si = kb.sb("tmp_si", [128, 32], I32)
        def sin_small(outname, angname, shift, mult=1.0):
            out_ap_f32[0] = aux_s
            src = sm[angname][:]
            if mult != 1.0:
                vop(lambda: V.tensor_scalar(out=sm["ang"][:], in0=sm[angname][:], scalar1=mult, scalar2=None, op0=ALU.mult), SM(angname), SM("ang"))
                src = sm["ang"][:]
                R = SM("ang")
            else:
                R = SM(angname)
            sin_of(sm[outname][:], src, tmp_si[:], tmp_s, shift, R, SM(outname), ["tmp_s"])

        sin_small("ai", "th", 0.0)
        sin_small("ar", "th", math.pi / 2)
        sin_small("ci", "th", 0.0, mult=128.0)
        sin_small("cr", "th", math.pi / 2, mult=128.0)
        vop(lambda: V.tensor_tensor(out=sm["ar"][:], in0=sm["ar"][:], in1=sm["dec"][:], op=ALU.mult), SM("ar", "dec"), SM("ar"))
        vop(lambda: V.tensor_tensor(out=sm["ai"][:], in0=sm["ai"][:], in1=sm["dec"][:], op=ALU.mult), SM("ai", "dec"), SM("ai"))
        vop(lambda: V.tensor_tensor(out=sm["t0"][:], in0=sm["lr"][:], in1=sm["lr"][:], op=ALU.mult), SM("lr"), SM("t0"))
        vop(lambda: V.tensor_tensor(out=sm["t1"][:], in0=sm["li"][:], in1=sm["li"][:], op=ALU.mult), SM("li"), SM("t1"))
        vop(lambda: V.tensor_tensor(out=sm["den"][:], in0=sm["t0"][:], in1=sm["t1"][:], op=ALU.add), SM("t0", "t1"), SM("den"))
        vop(lambda: V.reciprocal(out=sm["den"][:], in_=sm["den"][:]), SM("den"), SM("den"))
        vop(lambda: V.tensor_scalar(out=sm["t2"][:], in0=sm["ar"][:], scalar1=-1.0, scalar2=None, op0=ALU.add), SM("ar"), SM("t2"))
        vop(lambda: V.tensor_tensor(out=sm["t0"][:], in0=sm["t2"][:], in1=sm["lr"][:], op=ALU.mult), SM("t2", "lr"), SM("t0"))
        vop(lambda: V.tensor_tensor(out=sm["t1"][:], in0=sm["ai"][:], in1=sm["li"][:], op=ALU.mult), SM("ai", "li"), SM("t1"))
        vop(lambda: V.tensor_tensor(out=sm["t0"][:], in0=sm["t0"][:], in1=sm["t1"][:], op=ALU.add), SM("t0", "t1"), SM("t0"))
        vop(lambda: V.tensor_tensor(out=sm["cre"][:], in0=sm["t0"][:], in1=sm["den"][:], op=ALU.mult), SM("t0", "den"), SM("cre"))
        vop(lambda: V.tensor_tensor(out=sm["t0"][:], in0=sm["ai"][:], in1=sm["lr"][:], op=ALU.mult), SM("ai", "lr"), SM("t0"))
        vop(lambda: V.tensor_tensor(out=sm["t1"][:], in0=sm["t2"][:], in1=sm["li"][:], op=ALU.mult), SM("t2", "li"), SM("t1"))
        vop(lambda: V.tensor_tensor(out=sm["t0"][:], in0=sm["t0"][:], in1=sm["t1"][:], op=ALU.subtract), SM("t0", "t1"), SM("t0"))
        vop(lambda: V.tensor_tensor(out=sm["cim"][:], in0=sm["t0"][:], in1=sm["den"][:], op=ALU.mult), SM("t0", "den"), SM("cim"))
        vop(lambda: V.tensor_scalar(out=sgn[:], in0=pidx[:], scalar1=64.0, scalar2=-2.0, op0=ALU.is_ge, op1=ALU.mult), ["pidx"], ["sgn"])
        vop(lambda: V.tensor_scalar(out=sgn[:], in0=sgn[:], scalar1=1.0, scalar2=None, op0=ALU.add), ["sgn"], ["sgn"])
        vop(lambda: V.tensor_scalar(out=sm["cbri"][:], in0=sm["cim"][:], scalar1=sgn[:, 0:1], scalar2=-1.0, op0=ALU.mult, op1=ALU.mult), SM("cim") + ["sgn"], SM("cbri"))
        vop(lambda: V.tensor_scalar(out=sm["cair"][:], in0=sm["cre"][:], scalar1=sgn[:, 0:1], scalar2=None, op0=ALU.mult), SM("cre") + ["sgn"], SM("cair"))
        vop(lambda: V.tensor_scalar(out=sm["t0"][:, 0:1], in0=pidx[:], scalar1=64.0, scalar2=-64.0, op0=ALU.is_ge, op1=ALU.mult), ["pidx"], SM("t0"))
        vop(lambda: V.tensor_tensor(out=sm["t0"][:, 0:1], in0=sm["t0"][:, 0:1], in1=pidx[:], op=ALU.add), SM("t0") + ["pidx"], SM("t0"))
        for v in range(4):
            vop(lambda: V.tensor_scalar(out=sm["t1"][:, 0:1], in0=sm["t0"][:, 0:1], scalar1=16.0 * v, scalar2=None, op0=ALU.is_ge), SM("t0"), SM("t1"))
            vop(lambda: V.tensor_scalar(out=sm["t1"][:, 1:2], in0=sm["t0"][:, 0:1], scalar1=16.0 * (v + 1), scalar2=None, op0=ALU.is_lt), SM("t0"), SM("t1"))
            vop(lambda: V.tensor_tensor(out=rmask[:, v:v + 1], in0=sm["t1"][:, 0:1], in1=sm["t1"][:, 1:2], op=ALU.mult), SM("t1"), ["rmask"])
        vop(lambda: V.memset(cmask[:], 0.0), [], ["cmask"])
        for v in range(4):
            vop(lambda: V.memset(cmask[:, v, 16 * v:16 * v + 16], 1.0), ["cmask"], ["cmask"])
        vop(lambda: V.tensor_scalar(out=pmat[:], in0=iot[:], scalar1=pidx[:, 0:1], scalar2=64.0, op0=ALU.subtract, op1=ALU.is_equal), ["iot", "pidx"], ["pmat"])
        vop(lambda: V.tensor_scalar(out=maskU[:], in0=iot[:], scalar1=pidx[:, 0:1], scalar2=-64.0, op0=ALU.subtract, op1=ALU.is_equal), ["iot", "pidx"], ["maskU"])
        vop(lambda: V.tensor_tensor(out=pmat[:], in0=pmat[:], in1=maskU[:], op=ALU.subtract), ["pmat", "maskU"], ["pmat"])
        vop(lambda: V.tensor_scalar(out=maskU[:], in0=iot[:], scalar1=pidx[:, 0:1], scalar2=None, op0=ALU.is_ge), ["iot", "pidx", "pmat"], ["maskU"])
        vop(lambda: V.memset(s5init[:], 0.0), [], ["s5init"])

        for gb in range(4):
            angt = big[0]
            tmpf = big[1]
            tmpi = big[2].bitcast(I32)
            out_ap_f32[0] = big[3]
            for gg in range(8):
                g = gb * 8 + gg
                vop(lambda: V.tensor_scalar(out=angt[:, gg * 128:(gg + 1) * 128], in0=iot[:], scalar1=sm["th"][:, g:g + 1], scalar2=None, op0=ALU.mult),
                    ["iot"] + SM("th"), HK)
            sin_of(SIN[:, gb * 8:(gb + 1) * 8, :].rearrange("p g t -> p (g t)"), angt, tmpi, tmpf, 0.0, HK, ["SIN"], HK)
            sin_of(COS[:, gb * 8:(gb + 1) * 8, :].rearrange("p g t -> p (g t)"), angt, tmpi, tmpf, math.pi / 2, HK, ["COS"], HK)

        P1 = big[0].rearrange("p (g c) -> p g c", c=16)[:, 0:32, :]
        P2 = big[1].rearrange("p (g c) -> p g c", c=16)[:, 0:32, :]
        XT = big[2].rearrange("p (g c) -> p g c", c=16)[:, 0:32, :]
        XT2 = big[3].rearrange("p (g c) -> p g c", c=16)[:, 0:32, :]
        for stt, sk_, Pdst in ((st_ri, K_ri, P1), (st_ir, K_ir, P2)):
            sv = stt.rearrange("g h (p c) -> g h p c", c=16)
            for c in range(16):
                kb.op(pe, lambda: T.transpose(bank[6][:, c * 32:(c + 1) * 32], sv[:, :, :, c], ident[0:32, 0:32]),
                      sk_ + ["ident"], ["b6"])
            vop(lambda: V.tensor_copy(out=Pdst, in_=bank[6][:, :].rearrange("p (c g) -> p g c", g=32)), ["b6"], HK)
        bc = lambda n_: sm[n_][:, :, None].broadcast_to([128, 32, 16])
        for which, (ca, pa, cb, pb_) in enumerate(((("cre"), P1, ("cbri"), P2), (("cair"), P2, ("cim"), P1))):
            vop(lambda: V.tensor_tensor(out=XT, in0=pa, in1=bc(ca), op=ALU.mult), HK + SM(ca), HK)
            vop(lambda: V.tensor_tensor(out=XT2, in0=pb_, in1=bc(cb), op=ALU.mult), HK + SM(cb), HK)
            vop(lambda: V.tensor_tensor(out=XT, in0=XT, in1=XT2, op=ALU.add), HK, HK)
            dst = BTRI if which == 0 else BTIR
            for q in range(4):
                src = big[2][:, q * 128:(q + 1) * 128]
                kb.op(pe, lambda: T.transpose(bank[6][:, 0:128], src, ident[:]), HK + ["ident"], ["b6"])
                for v in range(4):
                    vop(lambda: V.tensor_scalar(out=dst[v][:, q, :], in0=bank[6][:, 0:128], scalar1=rmask[:, v:v + 1], scalar2=None, op0=ALU.mult),
                        ["b6", "rmask"], [f"BT{which}{v}"])
        for which in range(2):
            for q in range(4):
                CQ = stA[:, (which * 4 + q) * 128:(which * 4 + q + 1) * 128]
                if which == 0:
                    vop(lambda: V.tensor_scalar(out=CQ[:, 64:128], in0=CQ[:, 64:128], scalar1=-1.0, scalar2=None, op0=ALU.mult), BK(4), BK(4))
                else:
                    vop(lambda: V.tensor_scalar(out=CQ, in0=CQ, scalar1=-1.0, scalar2=None, op0=ALU.mult), BK(4), BK(4))
                kb.op(pe, lambda: T.transpose(bank[6][:, 0:128], CQ, ident[:]), BK(4) + ["ident"], ["b6"])
                dstW = W1s if which == 0 else W2s
                for hb in range(2):
                    for v in range(4):
                        g = q * 8 + hb * 4 + v
                        vop(lambda: V.tensor_tensor(out=dstW[:, g, :], in0=bank[6][:, 64 * hb:64 * hb + 64], in1=cmask[:, v, :], op=ALU.mult),
                            ["b6", "cmask"], [f"Ws{which}"])
        for h in range(4):
            stg_ = stB[:, h * 128:(h + 1) * 128]
            kb.op(pe, lambda: T.transpose(bank[6][:, 0:128], stg_, ident[:]), BK(5) + ["ident"], ["b6"])
            vop(lambda: V.tensor_tensor(out=WmT[:, h, :], in0=bank[6][:, 0:128], in1=maskU[:], op=ALU.mult), ["b6", "maskU"], ["WmT"])
        vop(lambda: V.tensor_copy(out=browb[:], in_=stB[:, 512:1024]), BK(5), ["browb"])
        vop(lambda: V.memset(onesb[:], 1.0 / 128.0), [], ["onesb"])

        kb.selfsync_all = False
        kb.release(["prow", "st32", "sm_dt", "p1", "p2", "cq", "sgw", "brow", "sngb"])
        WSL = [wgs[0], wus[0], wgs[1], wus[1]]
        WSK = ["wgs0", "wus0", "wgs1", "wus1"]

        def load_win(wd_, nblk):
            for b in range(nblk):
                for k in range(KD):
                    kb.dma(pool, WSL[b][:, k, :], wd_[k * 128:(k + 1) * 128, b * 512:(b + 1) * 512], [], [WSK[b]], WSK[b])

        def load_wout(wd_):
            for k in range(KD):
                kb.dma(pool, wds[k // 4][:, k % 4, :], wd_[k * 128:(k + 1) * 128, :], [], [f"wds{k // 4}"], f"wds{k // 4}")

        def zfm(b, cc, t, pbank, pkey):
            ts = slice(t * TL, (t + 1) * TL)
            for k in range(KD):
                kb.mm(pbank[:, :], WSL[b][:, k, cc * 128:(cc + 1) * 128], hn[:, k, ts], k == 0, k == KD - 1, [WSK[b], f"hn{t}"], [pkey])

        def wout_apply(t):
            ts = slice(t * TL, (t + 1) * TL)
            for m in range(KD):
                pb, pk = bank[m % 2], f"b{m % 2}"
                for k in range(KD):
                    kb.mm(pb[:, :], wds[k // 4][:, k % 4, m * 128:(m + 1) * 128], ymix[:, k, :], k == 0, k == KD - 1,
                          [f"wds{k // 4}", "ymix"], [pk])
                kb.op(act, lambda: A.copy(out=yacc[:, m, ts], in_=pb[:, :]), [pk], [f"ya{t}_{m}"])
                if t == 0 and m in (0, 5):
                    dump(f"y{m}", yacc[:, m, ts], [f"ya{t}_{m}"], 512)

        def mixer0(s):
            prenorm(0, 2)
            load_win(ewin_d, 3)
            load_wout(ewout_d)
            for k in range(4):
                kb.dma(pool, wus[1][:, k, :], glu_d[k * 128:(k + 1) * 128, :], [], ["wus1"], "wus1")
            for t in range(NT):
                ts = slice(t * TL, (t + 1) * TL)
                import os as _os
                _dbg = _os.environ.get("MIXDBG", "")
                if "nosgu" in _dbg or "nos5" in _dbg:
                    vop(lambda: V.memset(ymix[:], 0.0), [], ["ymix"])
                for h in range(4 if "nosgu" not in _dbg else 0):
                    zfm(1, h, t, bank[h % 2], f"b{h % 2}")
                    sl, sk = slab(t, h)
                    kb.op(act, lambda: A.activation(out=sl, in_=bank[h % 2][:, :], func=AF.Gelu_apprx_tanh), [f"b{h % 2}"], [sk])
                    if t == 0 and h in (0, 1):
                        dump(f"u{h}", sl, [sk], 512)
                for tb in range(4 if "nosgu" not in _dbg else 0):
                    tok = slice(t * TL + tb * 128, t * TL + (tb + 1) * 128)
                    for k in range(KD):
                        kb.mm(bank[2][:, :], hn[:, k, tok], WSL[2][:, k, :], k == 0, k == KD - 1, [WSK[2], f"hn{t}"], ["b2"])
                    vt, vk = slab(t, 4 + tb % 2)
                    kb.op(act, lambda: A.activation(out=vt, in_=bank[2][:, :], func=AF.Gelu_apprx_tanh), ["b2"], [vk])
                    vop(lambda: V.bn_stats(out=bnst[:, 0:6], in_=vt), [vk], ["bnst"])
                    vop(lambda: V.tensor_tensor(out=mv[:, 0:1], in0=bnst[:, 1:2], in1=bnst[:, 4:5], op=ALU.add), ["bnst"], ["mv"], ss=True)
                    vop(lambda: V.tensor_scalar(out=mv[:, 0:1], in0=mv[:, 0:1], scalar1=0.5, scalar2=None, op0=ALU.mult), ["mv"], ["mv"], ss=True)
                    vop(lambda: V.tensor_tensor(out=mv[:, 3:4], in0=bnst[:, 1:2], in1=bnst[:, 4:5], op=ALU.subtract), ["bnst"], ["mv"], ss=True)
                    vop(lambda: V.tensor_tensor(out=mv[:, 3:4], in0=mv[:, 3:4], in1=mv[:, 3:4], op=ALU.mult), ["mv"], ["mv"], ss=True)
                    vop(lambda: V.tensor_tensor(out=mv[:, 1:2], in0=bnst[:, 2:3], in1=bnst[:, 5:6], op=ALU.add), ["bnst", "mv"], ["mv"], ss=True)
                    vop(lambda: V.tensor_scalar(out=mv[:, 1:2], in0=mv[:, 1:2], scalar1=1.0 / 512.0, scalar2=None, op0=ALU.mult), ["mv"], ["mv"], ss=True)
                    vop(lambda: V.scalar_tensor_tensor(out=mv[:, 1:2], in0=mv[:, 3:4], scalar=0.25, in1=mv[:, 1:2], op0=ALU.mult, op1=ALU.add), ["mv"], ["mv"], ss=True)
                    kb.op(act, lambda: A.activation(out=mv[:, 2:3], in_=mv[:, 1:2], func=AF.Ln, bias=epsc[:, 0:1], scale=1.0), ["mv", "epsc"], ["mv"], ss=True)
                    kb.op(act, lambda: A.activation(out=mv[:, 2:3], in_=mv[:, 2:3], func=AF.Exp, scale=-0.5), ["mv"], ["mv"], ss=True)
                    vop(lambda: V.tensor_scalar(out=vt, in0=vt, scalar1=mv[:, 0:1], scalar2=mv[:, 2:3], op0=ALU.subtract, op1=ALU.mult), [vk, "mv"], [vk], ss=True)
                    if t == 0 and tb == 0:
                        dump("vn0", vt, [vk], 512)
                        dump("mv0", mv[:, 0:4], ["mv"], 4)
                    vb = vnbf[tb % 2]
                    vop(lambda: V.tensor_tensor(out=vb[:], in0=vt, in1=sngb[:], op=ALU.mult), [vk, "sngb"], [f"vnbf{tb % 2}"])
                    for h in range(4):
                        pb = bank[4 + h]
                        kb.mm(pb[:, tb * 128:(tb + 1) * 128], vb[:, h * 128:(h + 1) * 128], WmT[:, h, :], True, False,
                              [f"vnbf{tb % 2}", "WmT"], [f"b{4 + h}"], inc=False)
                        kb.mm(pb[:, tb * 128:(tb + 1) * 128], onesb[:, :], browb[:, h * 128:(h + 1) * 128], False, True,
                              ["onesb", "browb"], [f"b{4 + h}"], inc=True)
                if t == 0 and dbg_d is not None and not _os.environ.get("NOS0"):
                    kb.op(dve, lambda: V.tensor_copy(out=sgt[0][:], in_=bank[4][:, :]), ["b4"], ["sgt0"])
                    dump("s0", sgt[0][:], ["sgt0"], 512)
                    kb.op(dve, lambda: V.tensor_copy(out=sgt[1][:], in_=bank[5][:, :]), ["b5"], ["sgt1"])
                    dump("s1", sgt[1][:], ["sgt1"], 512)
                for h in range(4 if "nosgu" not in _dbg else 0):
                    sl, sk = slab(t, h)
                    vop(lambda: V.tensor_tensor(out=ymix[:, 4 + h, :], in0=sl, in1=bank[4 + h][:, :], op=ALU.mult), [sk, f"b{4 + h}"], ["ymix"])
                if "nos5" in _dbg:
                    wout_apply(t)
                    continue
                for q in range(4):
                    zfm(0, q, t, bank[q % 2], f"b{q % 2}")
                    sl, sk = slab(t, q)
                    kb.op(act, lambda: A.copy(out=sl, in_=bank[q % 2][:, :]), [f"b{q % 2}"], [sk])
                    vop(lambda: V.tensor_copy(out=ubf[:, q, :], in_=bank[q % 2][:, :]), [f"b{q % 2}"], ["ablk0"])
                wk, wkk = slab(t, 4)
                gk, gkk = slab(t, 5)
                gcs = gk.bitcast(BF16)
                its = [(c4, g) for c4 in range(4) for g in range(32)]

                def s5proj(c4, g):
                    q, hb, v = g // 8, (g % 8) // 4, g % 4
                    rows = slice(64 * hb, 64 * hb + 64)
                    cs = slice(c4 * 128, (c4 + 1) * 128)
                    kb.mm(bank[g % 2][:, 0:128], BTRI[v][rows, q, :], ubf[rows, q, cs], True, True, [f"BT0{v}", "ablk0"], [f"b{g % 2}"])
                    kb.mm(bank[2 + g % 2][:, 0:128], BTIR[v][rows, q, :], ubf[rows, q, cs], True, True, [f"BT1{v}", "ablk0"], [f"b{2 + g % 2}"])

                s5proj(*its[0])
                for ii, (c4, g) in enumerate(its):
                    if True:
                        cs = slice(c4 * 128, (c4 + 1) * 128)
                        q, hb, v = g // 8, (g % 8) // 4, g % 4
                        rows = slice(64 * hb, 64 * hb + 64)
                        pb, pk = bank[g % 2], f"b{g % 2}"
                        pb2, pk2 = bank[2 + g % 2], f"b{2 + g % 2}"
                        r2 = g % 2
                        if ii + 1 < len(its):
                            s5proj(*its[ii + 1])
                        t1 = wk[:, r2 * 256:r2 * 256 + 128]
                        t2 = wk[:, r2 * 256 + 128:r2 * 256 + 256]
                        k1 = wkk + f"_{r2}"
                        vop(lambda: V.tensor_tensor(out=t1, in0=pb[:, 0:128], in1=COS[:, g, :], op=ALU.mult), [pk, "COS"], [k1 + "a"])
                        vop(lambda: V.tensor_tensor(out=t2, in0=pb2[:, 0:128], in1=SIN[:, g, :], op=ALU.mult), [pk2, "SIN"], [k1 + "b"])
                        vop(lambda: V.tensor_tensor(out=t1, in0=t1, in1=t2, op=ALU.add), [k1 + "a", k1 + "b"], [k1 + "a"])
                        Gt = gk[:, 256 + r2 * 128:256 + (r2 + 1) * 128]
                        kG = gkk + f"_G{r2}"
                        vop(lambda: V.tensor_tensor_scan(out=Gt, data0=sm["dec"][:, g:g + 1].broadcast_to([128, 128]), data1=t1,
                                                         initial=s5init[:, g:g + 1], op0=ALU.mult, op1=ALU.add),
                            [k1 + "a", "s5init", "sm_dec"], [kG])
                        kb.mm(pb[:, 256:257], pmat[:], Gt[:, 127:128], True, True, ["pmat", kG], [pk])
                        Gc = gcs[:, r2 * 256:r2 * 256 + 128]
                        Gs = gcs[:, r2 * 256 + 128:r2 * 256 + 256]
                        kC = gkk + f"_C{r2}"
                        vop(lambda: V.tensor_tensor(out=Gc, in0=Gt, in1=COS[:, g, :], op=ALU.mult), [kG, "COS"], [kC + "c"])
                        vop(lambda: V.tensor_scalar(out=s5tmp[:, g:g + 1], in0=Gt[:, 127:128], scalar1=sm["cr"][:, g:g + 1], scalar2=None, op0=ALU.mult),
                            [kG, "sm_cr"], ["s5tmp"])
                        vop(lambda: V.tensor_tensor(out=Gs, in0=Gt, in1=SIN[:, g, :], op=ALU.mult), [kG, "SIN"], [kC + "s"])
                        vop(lambda: V.scalar_tensor_tensor(out=s5init[:, g:g + 1], in0=pb[:, 256:257], scalar=sm["ci"][:, g:g + 1], in1=s5tmp[:, g:g + 1],
                                                           op0=ALU.mult, op1=ALU.add), [pk, "sm_ci", "s5tmp"], ["s5init"])
                        yb, yk = bank[4 + q], f"b{4 + q}"
                        kb.mm(yb[rows, cs], W1s[:, g, :], Gc, v == 0, False, ["Ws0", kC + "c"], [yk], inc=True)
                        kb.mm(yb[rows, cs], W2s[:, g, :], Gs, False, v == 3, ["Ws1", kC + "s"], [yk], inc=True)
                for q in range(4):
                    sl, sk = slab(t, q)
                    vop(lambda: V.scalar_tensor_tensor(out=sl, in0=sl, scalar=s5dcol(q), in1=bank[4 + q][:, :], op0=ALU.mult, op1=ALU.add),
                        [sk, f"b{4 + q}", "pcol"], [sk])
                    kb.op(act, lambda: A.activation(out=sl, in_=sl, func=AF.Gelu_apprx_tanh), [sk], [sk])
                    vop(lambda: V.tensor_copy(out=ubf[:, q, :], in_=sl), [sk], ["ablk0"])
                for q2 in range(4):
                    pb, pk = bank[q2 % 2], f"b{q2 % 2}"
                    for q in range(4):
                        kb.mm(pb[:, :], wus[1][:, q, q2 * 128:(q2 + 1) * 128], ubf[:, q, :], q == 0, q == 3, ["wus1", "ablk0"], [pk])
                    s_ = state["sg"] % 2
                    state["sg"] += 1
                    kb.op(act, lambda: A.activation(out=sgt[s_][:], in_=pb[:, :], func=AF.Sigmoid), [pk], [f"sgt{s_}"])
                    sl, sk = slab(t, q2)
                    vop(lambda: V.tensor_tensor(out=ymix[:, q2, :], in0=sl, in1=sgt[s_][:], op=ALU.mult), [sk, f"sgt{s_}"], ["ymix"])
                wout_apply(t)
                postnorm_add(0, 3, False, [t])


        kb.selfsync_all = True
        poolW = kb.sb("poolW", [128, 4, 128], BF16)
        Mt = kb.sb("Mt", [128, 12, 128], BF16)
        zc_buf = [kb.sb(f"zc{i}", [128, 512], BF16) for i in range(2)]
        Sf = kb.sb("Sf", [128, 4, 128], F32)
        rmask512 = kb.sb("rmask512", [128, 512], BF16)
        BD16 = kb.sb("BD16", [128, 128], BF16)
        cm8 = kb.sb("cm8", [128, 8], BF16)
        lbc = kb.sb("lbc", [128, 8], F32)
        vop(lambda: V.tensor_copy(out=poolW[:], in_=poolw_stg), ["ablk0", "ablk1"], ["poolW"])
        vop(lambda: V.tensor_tensor(out=lbc[:, 0:4], in0=pcol[:, 108:112], in1=pcol[:, 104:108], op=ALU.subtract), ["pcol"], ["lbc"])
        kb.op(act, lambda: A.activation(out=lbc[:, 0:4], in_=lbc[:, 0:4], func=AF.Sigmoid), ["lbc"], ["lbc"])
        vop(lambda: V.tensor_scalar(out=lbc[:, 4:8], in0=lbc[:, 0:4], scalar1=-1.0, scalar2=1.0, op0=ALU.mult, op1=ALU.add), ["lbc"], ["lbc"])
        vop(lambda: V.memset(Sf[:], 0.0), [], ["Sf"])
        dm = big[0][:, 0:128]
        t_a = big[0][:, 128:256]
        t_b = big[0][:, 256:384]
        t_c = big[0][:, 384:512]
        t_d = big[0][:, 512:640]
        vop(lambda: V.tensor_scalar(out=dm, in0=iot[:], scalar1=pidx[:, 0:1], scalar2=None, op0=ALU.subtract), ["iot", "pidx"], HK)
        for gi, win in enumerate((2, 4, 8, 16)):
            vop(lambda: V.tensor_scalar(out=t_a, in0=dm, scalar1=0.0, scalar2=None, op0=ALU.is_ge), HK, HK)
            vop(lambda: V.tensor_scalar(out=t_b, in0=dm, scalar1=float(win), scalar2=None, op0=ALU.is_lt), HK, HK)
            vop(lambda: V.tensor_tensor(out=t_a, in0=t_a, in1=t_b, op=ALU.mult), HK, HK)
            vop(lambda: V.scalar_tensor_tensor(out=Mt[:, 3 * gi, :], in0=t_a, scalar=1.0 / win, in1=ident[:], op0=ALU.mult, op1=ALU.subtract), HK + ["ident"], ["Mt"])
            vop(lambda: V.tensor_scalar(out=t_c, in0=iot[:], scalar1=1.0, scalar2=float(win), op0=ALU.add, op1=ALU.min), ["iot"], HK)
            vop(lambda: V.reciprocal(out=t_c, in_=t_c), HK, HK)
            vop(lambda: V.tensor_tensor(out=t_c, in0=t_c, in1=t_a, op=ALU.mult), HK, HK)
            vop(lambda: V.tensor_tensor(out=Mt[:, 3 * gi + 2, :], in0=t_c, in1=ident[:], op=ALU.subtract), HK + ["ident"], ["Mt"])
            vop(lambda: V.tensor_scalar(out=t_d, in0=dm, scalar1=128.0, scalar2=float(win), op0=ALU.add, op1=ALU.is_lt), HK, HK)
            vop(lambda: V.tensor_scalar(out=Mt[:, 3 * gi + 1, :], in0=t_d, scalar1=1.0 / win, scalar2=None, op0=ALU.mult), HK, ["Mt"])
        colch = big[1][:, 0:128]
        rowch = big[1][:, 128:256]
        kb.op(pool, lambda: G.iota(colch, pattern=[[1, 8], [0, 16]], base=0, channel_multiplier=0, allow_small_or_imprecise_dtypes=True), [], HK)
        kb.op(pe, lambda: T.transpose(bank[6][:, 0:128], colch, ident[:]), HK + ["ident"], ["b6"])
        vop(lambda: V.tensor_copy(out=rowch, in_=bank[6][:, 0:128]), ["b6"], HK)
        vop(lambda: V.tensor_tensor(out=t_b, in0=colch, in1=rowch, op=ALU.is_equal), HK, HK)
        vop(lambda: V.tensor_scalar(out=t_a, in0=dm, scalar1=0.0, scalar2=None, op0=ALU.is_ge), HK, HK)
        vop(lambda: V.tensor_tensor(out=BD16[:], in0=t_a, in1=t_b, op=ALU.mult), HK, ["BD16"])
        vop(lambda: V.tensor_scalar(out=cm8[:], in0=iot[:, 0:8], scalar1=rowch[:, 0:1], scalar2=None, op0=ALU.is_equal), ["iot"] + HK, ["cm8"])
        jidx = big[2][:, 0:512]
        kb.op(pool, lambda: G.iota(jidx, pattern=[[0, 32], [1, 16]], base=0, channel_multiplier=0, allow_small_or_imprecise_dtypes=True), [], HK)
        vop(lambda: V.tensor_scalar(out=rmask512[:], in0=jidx, scalar1=0.0, scalar2=None, op0=ALU.is_gt), HK, ["rmask512"])
        kb.selfsync_all = False
        kb.release(["poolw"])
        WSL.append(ablk_all[:].rearrange("p a b c -> p (a b) c"))
        WSK.append("ablkW")
        kb.alias["ablkW"] = ["ablk0", "ablk1"]
        pscol = lambda gi: pcol[:, 100 + gi:101 + gi]
        ongcol = pcol[:, 112:113]
        l1 = {"blk": 0}

        import os as _os2
        _os_ss = int(_os2.environ.get("HGRN_SS", "0"))

        def mixer1(s):
            prenorm(1, 2)
            load_win(owin_d, 5)
            load_wout(owout_d)
            for t in range(NT):
                ts = slice(t * TL, (t + 1) * TL)
                S_ = lambda i: slab(t, i)
                qk, qkk = S_(2)
                Qt = qk.bitcast(BF16)[:, 0:512]
                Kt = qk.bitcast(BF16)[:, 512:1024]
                v34a, v3k = S_(3)
                v34b, v4k = S_(4)
                vtm = [v34a.bitcast(BF16)[:, 0:512], v34a.bitcast(BF16)[:, 512:1024],
                       v34b.bitcast(BF16)[:, 0:512], v34b.bitcast(BF16)[:, 512:1024]]
                vtk = [v3k, v3k, v4k, v4k]
                vx, vxk = S_(5)
                Vexp = vx.bitcast(BF16).rearrange("p (c v) -> p c v", v=128)
                sbs, sbk = S_(6)
                Sb = sbs.bitcast(BF16).rearrange("p (c v) -> p c v", v=128)
                m7, m7k = S_(7)
                m7b = m7.bitcast(BF16)
                scTm = m7b[:, 0:128]
                Khtm = m7b[:, 128:256]
                dbf = m7b[:, 256:768].rearrange("p (g t) -> p g t", t=128)
                Tg, Tgk = S_(0)
                Kh32, Kh32k = S_(1)
                T1, T1k = sgt[0][:], "sgt0"
                T2, T2k = sgt[1][:], "sgt1"
                T3, T3k = sqt[0][:], "sqt0"
                T4, T4k = sqt[1][:], "sqt1"
                for tb in range(4):
                    tok = slice(t * TL + tb * 128, t * TL + (tb + 1) * 128)
                    first = (s == 0 and t == 0 and tb == 0)
                    cur = l1["blk"] % 2
                    l1["blk"] += 1
                    zc, zck = zc_buf[cur], f"zc{cur}"
                    zp, zpk = zc_buf[1 - cur], f"zc{1 - cur}"
                    for k in range(KD):
                        kb.mm(bank[2][:, :], hn[:, k, tok], WSL[0][:, k, :], k == 0, k == KD - 1, [WSK[0], f"hn{t}"], ["b2"])
                    kb.op(act, lambda: A.copy(out=zc[:], in_=bank[2][:, :]), ["b2"], [zck])
                    for gi in range(4):
                        gsl = slice(gi * 128, (gi + 1) * 128)
                        kb.mm(bank[3][:, gsl], zc[:, gsl], Mt[:, 3 * gi + (2 if first else 0), :], True, first, [zck, "Mt"], ["b3"], inc=first)
                        if not first:
                            kb.mm(bank[3][:, gsl], zp[:, gsl], Mt[:, 3 * gi + 1, :], False, True, [zpk, "Mt"], ["b3"], inc=True)
                    vop(lambda: V.tensor_copy(out=dbf, in_=bank[3][:, :].rearrange("p (g t) -> p g t", t=128)), ["b3"], [m7k + "_d"])
                    for gi in range(4):
                        gsl = slice(gi * 128, (gi + 1) * 128)
                        kb.mm(bank[4][:, gsl], poolW[:, gi, :], dbf[:, gi, :], True, True, ["poolW", m7k + "_d"], ["b4"])
                    for gi in range(4):
                        gsl = slice(gi * 128, (gi + 1) * 128)
                        vop(lambda: V.tensor_scalar(out=ymix[:, gi, tb * 128:(tb + 1) * 128], in0=bank[4][:, gsl], scalar1=pscol(gi), scalar2=None, op0=ALU.mult),
                            ["b4", "pcol"], ["ymix"])
                    for k in range(KD):
                        kb.mm(bank[2][:, :], hn[:, k, tok], WSL[3][:, k, :], k == 0, k == KD - 1, [WSK[3], f"hn{t}"], ["b2"])
                    kb.op(act, lambda: A.copy(out=vtm[tb], in_=bank[2][:, :]), ["b2"], [vtk[tb]])
                for h in range(4):
                    zfm(1, h, t, bank[0], "b0")
                    kb.op(act, lambda: A.activation(out=T4, in_=bank[0][:, :], func=AF.Silu), ["b0"], [T4k])
                    zfm(2, h, t, bank[1], "b1")
                    kb.op(act, lambda: A.activation(out=T1, in_=bank[1][:, :], func=AF.Sigmoid), ["b1"], [T1k])
                    vop(lambda: V.tensor_scalar(out=T1, in0=T1, scalar1=lbc[:, 4 + h:5 + h], scalar2=lbc[:, h:h + 1], op0=ALU.mult, op1=ALU.add), [T1k, "lbc"], [T1k])
                    vop(lambda: V.tensor_scalar(out=T2, in0=T1, scalar1=-1.0, scalar2=1.0, op0=ALU.mult, op1=ALU.add), [T1k], [T2k])
                    kb.op(act, lambda: A.activation(out=T1, in_=T1, func=AF.Ln), [T1k], [T1k])
                    vop(lambda: V.tensor_tensor_scan(out=T3, data0=rmask512[:], data1=T1, initial=0.0, op0=ALU.mult, op1=ALU.add), [T1k, "rmask512"], [T3k])
                    kb.op(act, lambda: A.activation(out=T1, in_=T3, func=AF.Exp), [T3k], [T1k])
                    kb.op(act, lambda: A.activation(out=T3, in_=T3, func=AF.Exp, scale=-1.0), [T3k], [T3k])
                    vop(lambda: V.tensor_tensor(out=Qt, in0=T4, in1=T1, op=ALU.mult), [T4k, T1k], [qkk])
                    vop(lambda: V.tensor_tensor(out=T2, in0=T2, in1=T3, op=ALU.mult), [T2k, T3k], [T2k])
                    vop(lambda: V.tensor_copy(out=Kt, in_=T2), [T2k], [qkk])
                    eb3 = T1.rearrange("p (c j) -> p c j", j=16)
                    vop(lambda: V.tensor_tensor(out=Kh32.rearrange("p (c j) -> p c j", j=16), in0=T2.rearrange("p (c j) -> p c j", j=16),
                                                in1=eb3[:, :, 15:16].broadcast_to([128, 32, 16]), op=ALU.mult), [T2k, T1k], [Kh32k])
                    zfm(4, h, t, bank[0], "b0")
                    kb.op(act, lambda: A.activation(out=Tg, in_=bank[0][:, :], func=AF.Silu), ["b0"], [Tgk])
                    ob, obk = bank[7], "b7"

                    def Sc(c):
                        if c == 0 or c == 8:
                            return Sf[:, h, :], "Sf"
                        if c <= 4:
                            return T4[:, (c - 1) * 128:c * 128], T4k
                        return T2[:, (c - 5) * 128:(c - 4) * 128], T2k

                    def ubank(tb):
                        return ((bank[5], "b5"), (bank[6], "b6")) if tb % 2 == 0 else ((bank[0], "b0"), (bank[1], "b1"))

                    def hfront(tb):
                        cs = slice(tb * 128, (tb + 1) * 128)
                        p_ = tb % 2
                        scT_, Kht_ = m7b[:, 768 * p_:768 * p_ + 128], m7b[:, 768 * p_ + 128:768 * p_ + 256]
                        pk_ = m7k + f"_p{p_}"
                        kb.mm(bank[3][:, 0:128], Kt[:, cs], Qt[:, cs], True, True, [qkk], ["b3"])
                        vop(lambda: V.tensor_tensor(out=scT_, in0=bank[3][:, 0:128], in1=BD16[:], op=ALU.mult), ["b3", "BD16"], [pk_])
                        kb.op(pe, lambda: T.transpose(bank[4][:, 0:128], Kh32[:, cs], ident[:]), [Kh32k, "ident"], ["b4"])
                        kb.op(act, lambda: A.copy(out=Kht_, in_=bank[4][:, 0:128]), ["b4"], [pk_])
                        vop(lambda: V.tensor_tensor(out=Vexp, in0=vtm[tb][:, None, h * 128:(h + 1) * 128].broadcast_to([128, 8, 128]),
                                                    in1=cm8[:, :, None].broadcast_to([128, 8, 128]), op=ALU.mult), [vtk[tb], "cm8"], [vxk])
                        Vf = Vexp.rearrange("p c v -> p (c v)")
                        (u0, u0k), (u1, u1k) = ubank(tb)
                        kb.mm(u0[:, :], Kht_, Vf[:, 0:512], True, True, [pk_, vxk], [u0k])
                        kb.mm(u1[:, :], Kht_, Vf[:, 512:1024], True, True, [pk_, vxk], [u1k])

                    def hmid(tb):
                        ub2 = ubank(tb)
                        kb.op(act, lambda: A.copy(out=Sb[:, 0, :], in_=Sf[:, h, :]), ["Sf"], [sbk])
                        for c in range(8):
                            ub, ubk = ub2[0] if c < 4 else ub2[1]
                            col = tb * 128 + c * 16 + 15
                            src, srck = Sc(c)
                            dst, dstk = Sc(c + 1)
                            vop(lambda: V.scalar_tensor_tensor(out=dst, in0=src, scalar=T1[:, col:col + 1], in1=ub[:, (c % 4) * 128:(c % 4 + 1) * 128],
                                                               op0=ALU.mult, op1=ALU.add), [srck, T1k, ubk], [dstk], ss=bool(_os_ss))
                        kb.op(act, lambda: A.copy(out=Sb[:, 1:5, :].rearrange("p c v -> p (c v)"), in_=T4), [T4k], [sbk])
                        kb.op(act, lambda: A.copy(out=Sb[:, 5:8, :].rearrange("p c v -> p (c v)"), in_=T2[:, 0:384]), [T2k], [sbk])

                    def hback(tb):
                        cs = slice(tb * 128, (tb + 1) * 128)
                        p_ = tb % 2
                        scT_ = m7b[:, 768 * p_:768 * p_ + 128]
                        pk_ = m7k + f"_p{p_}"
                        kb.mm(ob[:, cs], vtm[tb][:, h * 128:(h + 1) * 128], scT_, True, False, [vtk[tb], pk_], [obk], inc=True)
                        for c in range(8):
                            kb.mm(ob[:, tb * 128 + c * 16:tb * 128 + (c + 1) * 16], Sb[:, c, :], Qt[:, tb * 128 + c * 16:tb * 128 + (c + 1) * 16],
                                  False, c == 7, [sbk, qkk], [obk], inc=True)

                    hfront(0)
                    for tb in range(4):
                        if tb + 1 < 4:
                            hfront(tb + 1)
                        hmid(tb)
                        hback(tb)
                    kb.op(act, lambda: A.activation(out=T2, in_=ob[:, :], func=AF.Square), [obk], [T2k])
                    kb.mm(bank[2][:, :], ones[:], T2, True, True, ["ones", T2k], ["b2"])
                    kb.op(act, lambda: A.activation(out=T3, in_=bank[2][:, :], func=AF.Ln, scale=1.0 / 128.0, bias=epsc[:, 0:1]), ["b2", "epsc"], [T3k])
                    kb.op(act, lambda: A.activation(out=T3, in_=T3, func=AF.Exp, scale=-0.5), [T3k], [T3k])
                    vop(lambda: V.tensor_tensor(out=T2, in0=T3, in1=ob[:, :], op=ALU.mult), [T3k, obk], [T2k])
                    vop(lambda: V.scalar_tensor_tensor(out=ymix[:, 4 + h, :], in0=T2, scalar=ongcol, in1=Tg, op0=ALU.mult, op1=ALU.mult),
                        [T2k, "pcol", Tgk], ["ymix"])
                wout_apply(t)
                postnorm_add(1, 3, False, [t])

        for s in range(NSUP):
            if s > 0:
                load_x(s)
            for l in range(2):
                if 'ffn' in stages:
                    prenorm(l, 0)
                    ffn(l * 2 + 0)
                    postnorm_add(l, 1, True)
                if 'mix0' in stages and l == 0:
                    mixer0(s)
                if 'mix1' in stages and l == 1:
                    mixer1(s)
                if 'ffn' in stages:
                    prenorm(l, 4)
                    ffn(l * 2 + 1)
                    postnorm_add(l, 5, True)
            store_x(s)
        kb.finish(["xout0", "xout1"] )
        stuck = kb.simulate()
        if stuck:
            raise RuntimeError(f"semaphore deadlock: {stuck}")
    return nc


_CACHE = {}


def make_common(inputs):
    f = lambda k, shp: np.ascontiguousarray(inputs[k], dtype=np.float32).reshape(shp)
    return {
        "norm_g": f("norm_g", (96, 128)),
        "ffn_wg": f("ffn_wg", (4, D, DFF)), "ffn_wu": f("ffn_wu", (4, D, DFF)), "ffn_wd": f("ffn_wd", (4, DFF, D)),
        "even_w_in": f("even_w_in", (D, 1536)), "even_w_out": f("even_w_out", (D, D)),
        "s5_lam_re": f("s5_lam_re", (32, 64)), "s5_lam_im": f("s5_lam_im", (32, 64)), "s5_log_dt": f("s5_log_dt", (1, 32)),
        "s5_b_re": f("s5_b_re", (32, 64, 16)), "s5_b_im": f("s5_b_im", (32, 64, 16)),
        "s5_c_re": f("s5_c_re", (512, 64)), "s5_c_im": f("s5_c_im", (512, 64)),
        "s5_d": f("s5_d", (4, 128)), "s5_w_glu": f("s5_w_glu", (512, 512)),
        "sgu_norm_g": f("sgu_norm_g", (1, 512)), "sgu_w": f("sgu_w", (4, 128, 128)), "sgu_b": f("sgu_b", (1, 512)),
        "odd_w_in": f("odd_w_in", (D, 2560)), "odd_w_out": f("odd_w_out", (D, D)),
        "pool_w": f("pool_w", (4, 128, 128)), "pool_scale": f("pool_scale", (4, 128)),
        "hgrn_lb": f("hgrn_lb", (8, 128)), "hgrn_onorm_g": f("hgrn_onorm_g", (1, 128)),
    }


def kernel(**inputs):
    x = np.ascontiguousarray(inputs["x"], dtype=np.float32)
    B = x.shape[0]
    if "nc" not in _CACHE:
        _CACHE["nc"] = build(stages=("ffn", "mix0", "mix1"))
    nc = _CACHE["nc"]
    common = make_common(inputs)
    active = [0, 1, 4, 5]
    zeros = {k: np.zeros_like(v) for k, v in common.items()}
    zx = np.zeros_like(x[0])
    in_maps = []
    for c in range(8):
        if c in active:
            m = dict(common)
            m["x"] = x[active.index(c)]
        else:
            m = dict(zeros)
            m["x"] = zx
        in_maps.append(m)
    res = run_bass_kernel_spmd(nc, in_maps, core_ids=list(range(8)))
    out = np.stack([res.results[active[b]]["out"] for b in range(B)], axis=0)
    return out.astype(np.float32)
```

```python
import math
from contextlib import ExitStack
import numpy as np
import concourse.bass as bass
import concourse.mybir as mybir
from concourse.bass_utils import run_bass_kernel_spmd

F32 = mybir.dt.float32
BF16 = mybir.dt.bfloat16
I32 = mybir.dt.int32
AF = mybir.ActivationFunctionType
ALU = mybir.AluOpType

D = 1024
KD = 8
DFF = 2816
NF = 22
SEQ = 4096
TL = 512
EPS = 1e-6
TWO_PI = 2.0 * math.pi
DBG_MAP = {}


class Eng:
    def __init__(self, name, h, sem):
        self.name, self.h, self.sem = name, h, sem
        self.count = 0
        self.waited = {}
        self.pending = False


class DSem:
    def __init__(self, sem):
        self.sem = sem
        self.count = 0


class KB:
    def __init__(self, nc, es):
        self.nc, self.es = nc, es
        mk = lambda n: es.enter_context(nc.semaphore(n))
        self.pe = Eng("pe", nc.tensor, mk("s_pe"))
        self.act = Eng("act", nc.scalar, mk("s_act"))
        self.dve = Eng("dve", nc.vector, mk("s_dve"))
        self.pool = Eng("pool", nc.gpsimd, mk("s_pool"))
        self.sp = Eng("sp", nc.sync, mk("s_sp"))
        self.writer = {}
        self.readers = {}
        self.dsems = {}
        self.alias = {}
        self.selfsync_all = False
        self.free_ds = []
        self.prog = {}
        self.trace = []

    def simulate(self):
        pcs = {e: 0 for e in self.prog}
        sems = {}
        progress = True
        while progress:
            progress = False
            for e, lst in self.prog.items():
                while pcs[e] < len(lst):
                    kind, sid, val, tag = lst[pcs[e]]
                    if kind == "wait":
                        if sems.get(sid, 0) >= val:
                            pcs[e] += 1
                            progress = True
                        else:
                            break
                    else:
                        sems[sid] = sems.get(sid, 0) + val
                        pcs[e] += 1
                        progress = True
        stuck = {e: (pcs[e], len(l), l[pcs[e]]) for e, l in self.prog.items() if pcs[e] < len(l)}
        return stuck

    def _ex(self, keys):
        out = []
        for k in keys:
            if k in self.alias:
                out.extend(self.alias[k])
            else:
                out.append(k)
        return out

    def sb(self, name, shape, dt):
        return self.es.enter_context(self.nc.sbuf_tensor(name, shape, dt))

    def ps(self, name, shape, dt=F32):
        return self.es.enter_context(self.nc.psum_tensor(name, shape, dt))

    def dsem(self, key):
        if key not in self.dsems:
            if self.free_ds:
                self.dsems[key] = self.free_ds.pop()
            else:
                self.dsems[key] = DSem(self.es.enter_context(self.nc.semaphore("d_" + key)))
                self.nsem = getattr(self, "nsem", 5) + 1
                assert self.nsem <= 24, "semaphore budget (24) exceeded"
        return self.dsems[key]

    def release(self, keys):
        for k in keys:
            if k in self.dsems:
                self.free_ds.append(self.dsems.pop(k))

    def _deps(self, eng, R, W, selfsync=False):
        need = {}
        selfsync = selfsync or self.selfsync_all
        def add(src, val):
            if src is eng and not (selfsync and eng is not self.pe):
                return
            k = id(src)
            if k not in need or need[k][1] < val:
                need[k] = (src, val)
        for r in R:
            w = self.writer.get(r)
            if w:
                add(*w)
        for w_ in W:
            w = self.writer.get(w_)
            if w:
                add(*w)
            for rd in self.readers.get(w_, ()):
                add(*rd)
        for k, (src, val) in need.items():
            if eng.waited.get(k, 0) < val:
                eng.h.wait_ge(src.sem, val)
                eng.waited[k] = val
                self.prog.setdefault(eng.name, []).append(("wait", id(src), val, getattr(src, "name", "dsem")))

    def _record(self, src, val, R, W):
        for r in R:
            self.readers.setdefault(r, []).append((src, val))
        for w in W:
            self.writer[w] = (src, val)
            self.readers[w] = []

    def op(self, eng, fn, R, W, inc=True, ss=False):
        R, W = self._ex(R), self._ex(W)
        isbank = lambda k: len(k) >= 2 and k[0] == "b" and k[1:].isdigit()
        W = list(W) + [r for r in R if isbank(r)]
        R = [r for r in R if not isbank(r)]
        self._deps(eng, R, W, ss)
        ins = fn()
        if inc:
            ins.then_inc(eng.sem, 1)
            eng.count += 1
            self.prog.setdefault(eng.name, []).append(("inc", id(eng), 1, str(W[:1])))
            self._record(eng, eng.count, R, W)
        else:
            self._record(eng, eng.count + 1, R, W)
        return ins

    def mm(self, out, lhsT, rhs, start, stop, R, W, inc=None):
        if inc is None:
            inc = stop
        return self.op(self.pe, lambda: self.nc.tensor.matmul(out, lhsT, rhs, start=start, stop=stop), R, W, inc=inc)

    def dma(self, q, out, in_, R, W, skey):
        ds = self.dsem(skey)
        R, W = self._ex(R), self._ex(W)
        self._deps(q, R, W)
        q.h.dma_start(out=out, in_=in_).then_inc(ds.sem, 16)
        ds.count += 16
        self.prog.setdefault(q.name, []).append(("inc", id(ds), 16, skey))
        self._record(ds, ds.count, R, W)

    def finish(self, keys):
        for k in keys:
            ds = self.dsems[k]
            self.nc.sync.wait_ge(ds.sem, ds.count)


def build(NSUP=None, S=1024, dbg=None, stages=('ffn',)):
    NT = S // TL
    if NSUP is None:
        NSUP = SEQ // S
    LTOK = NSUP * S
    nc = bass.Bass("TRN2", target_bir_lowering=False)
    dr = lambda n, shp, kind="ExternalInput": nc.dram_tensor(n, shp, F32, kind=kind).ap()
    x_d = dr("x", [LTOK, D])
    out_d = dr("out", [LTOK, D], "ExternalOutput")
    norm_g_d = dr("norm_g", [96, 128])
    wg_d = dr("ffn_wg", [4, D, DFF])
    wu_d = dr("ffn_wu", [4, D, DFF])
    wd_d = dr("ffn_wd", [4, DFF, D])
    ewin_d = dr("even_w_in", [D, 1536])
    ewout_d = dr("even_w_out", [D, D])
    lre_d = dr("s5_lam_re", [32, 64])
    lim_d = dr("s5_lam_im", [32, 64])
    ldt_d = dr("s5_log_dt", [1, 32])
    bre_d = dr("s5_b_re", [32, 64, 16])
    bim_d = dr("s5_b_im", [32, 64, 16])
    cre_d = dr("s5_c_re", [512, 64])
    cim_d = dr("s5_c_im", [512, 64])
    s5d_d = dr("s5_d", [4, 128])
    glu_d = dr("s5_w_glu", [512, 512])
    sng_d = dr("sgu_norm_g", [1, 512])
    sgw_d = dr("sgu_w", [4, 128, 128])
    sgb_d = dr("sgu_b", [1, 512])
    owin_d = dr("odd_w_in", [D, 2560])
    owout_d = dr("odd_w_out", [D, D])
    poolw_d = dr("pool_w", [4, 128, 128])
    pscale_d = dr("pool_scale", [4, 128])
    hlb_d = dr("hgrn_lb", [8, 128])
    hong_d = dr("hgrn_onorm_g", [1, 128])
    dbg_d = dr("dump_out", [128, 16384], "ExternalOutput") if dbg else None
    DBG_MAP.clear()

    es = ExitStack()
    with es:
        kb = KB(nc, es)
        pe, act, dve, pool, sp = kb.pe, kb.act, kb.dve, kb.pool, kb.sp
        V, A, G, T = nc.vector, nc.scalar, nc.gpsimd, nc.tensor
        dbg_state = {"off": 0, "n": 0}

        def dump(name, ap, keys, ncols, nrows=128):
            import os as _o
            if dbg_d is None or name in DBG_MAP or (_o.environ.get("DUMPS") and name not in _o.environ["DUMPS"].split(",")):
                return
            o = dbg_state["off"]
            DBG_MAP[name] = (o, ncols, nrows)
            dbg_state["off"] += ncols
            kb.dma(pool, dbg_d[0:nrows, o:o + ncols], ap, keys, [], "xout0")

        x_sb = kb.sb("x_sb", [128, KD, S], F32)
        hn = kb.sb("hn", [128, KD, S], BF16)
        yacc = kb.sb("yacc", [128, KD, S], F32)
        ablk_all = kb.sb("ablk_all", [128, 2, 4, TL], BF16)
        ablk = [ablk_all[:, i] for i in range(2)]
        wgs = [kb.sb(f"wgs{i}", [128, KD, 512], BF16) for i in range(2)]
        wus = [kb.sb(f"wus{i}", [128, KD, 512], BF16) for i in range(2)]
        wds = [kb.sb(f"wds{i}", [128, 4, D], BF16) for i in range(2)]
        sgt = [kb.sb(f"sgt{i}", [128, TL], F32) for i in range(2)]
        sqt = [kb.sb(f"sqt{i}", [128, TL], F32) for i in range(2)]
        rstd = kb.sb("rstd", [128, TL], F32)
        tmpn = sgt
        assert S == 1024
        xin = [yacc[:, i, :] for i in range(2)]
        xout = [yacc[:, 2 + i, :] for i in range(2)]
        XK = lambda m: [f"ya0_{m}", f"ya1_{m}"]
        ymix = kb.sb("ymix", [128, KD, TL], BF16)
        ubf = ablk[0]
        ident = kb.sb("ident", [128, 128], F32)
        ones = kb.sb("ones", [128, 128], F32)
        prow = kb.sb("prow", [128, 128], F32)
        pcol = kb.sb("pcol", [128, 128], F32)
        iot = kb.sb("iot", [128, 128], F32)
        pidx = kb.sb("pidx", [128, 1], F32)
        epsc = kb.sb("epsc", [128, 1], F32)
        bank = [kb.ps(f"bank{i}", [128, TL]) for i in range(8)]

        for t_ in range(NT):
            kb.alias[f"ya{t_}_4"] = [f"ya{t_}_4_{r}{c}" for r in (0, 1) for c in "ab"]
            kb.alias[f"ya{t_}_5"] = [f"ya{t_}_5_{n}" for n in ("G0", "G1", "C0c", "C0s", "C1c", "C1s")]
            kb.alias[f"ya{t_}_7"] = [f"ya{t_}_7_d", f"ya{t_}_7_p0", f"ya{t_}_7_p1"]
        kb.op(pool, lambda: G.iota(iot[:], pattern=[[1, 128]], base=0, channel_multiplier=0,
                                   allow_small_or_imprecise_dtypes=True), [], ["iot"])
        kb.op(pool, lambda: G.iota(pidx[:], pattern=[[0, 1]], base=0, channel_multiplier=1,
                                   allow_small_or_imprecise_dtypes=True), [], ["pidx"])
        kb.op(dve, lambda: V.tensor_scalar(out=ident[:], in0=iot[:], scalar1=pidx[:, 0:1], scalar2=None,
                                           op0=ALU.is_equal), ["iot", "pidx"], ["ident"])
        kb.op(dve, lambda: V.memset(ones[:], 1.0), [], ["ones"])
        kb.op(dve, lambda: V.memset(epsc[:], EPS), [], ["epsc"])
        kb.op(dve, lambda: V.memset(prow[:], 0.0), [], ["prow"])
        kb.dma(pool, prow[0:96, :], norm_g_d[:, :], [], ["prow"], "prow")
        kb.dma(pool, prow[96:100, :], s5d_d[:, :], [], ["prow"], "prow")
        sngb = kb.sb("sngb", [128, 512], F32)
        dt_early = kb.sb("s5_dt", [128, 32], F32)
        st32 = sqt[1][0:32, 0:256]
        kb.alias["sqt1"] = ["sqt1_main", "st32"]
        BK = lambda i: [f"ya0_{i}", f"ya1_{i}"]
        HK = ["hn0", "hn1"]
        stA = yacc[:, 4, :]
        stB = yacc[:, 5, :]
        st_ri = yacc[0:32, 6:8, :]
        st_ir = yacc[0:32, 2:4, :]
        K_ri, K_ir = BK(6) + BK(7), BK(2) + BK(3)
        kb.dma(sp, sngb[:, :], sng_d[0:1, :].broadcast_to([128, 512]), [], ["sngb"], "sngb")
        kb.dma(sp, dt_early[:], ldt_d[0:1, :].broadcast_to([128, 32]), [], ["sm_dt"], "sm_dt")
        kb.dma(sp, stB[:, 512:1024], sgb_d[0:1, :].broadcast_to([128, 512]), [], BK(5), "brow")
        for ni, src in enumerate((lre_d, lim_d)):
            kb.dma(pool, st32[:, ni * 128:ni * 128 + 64], src[:, :], [], ["st32"], "st32")
            kb.dma(pool, st32[:, ni * 128 + 64:ni * 128 + 128], src[:, :], [], ["st32"], "st32")
        bre_n = bre_d.rearrange("g p c -> g (p c)")
        bim_n = bim_d.rearrange("g p c -> g (p c)")
        kb.dma(pool, st_ri[:, 0, :], bre_n, [], K_ri, "p1")
        kb.dma(pool, st_ri[:, 1, :], bim_n, [], K_ri, "p1")
        kb.dma(pool, st_ir[:, 0, :], bim_n, [], K_ir, "p2")
        kb.dma(pool, st_ir[:, 1, :], bre_n, [], K_ir, "p2")
        for which in range(2):
            for q in range(4):
                CQ_ = stA[:, (which * 4 + q) * 128:(which * 4 + q + 1) * 128]
                a_, b_ = (cre_d, cim_d) if which == 0 else (cim_d, cre_d)
                kb.dma(pool, CQ_[:, 0:64], a_[q * 128:(q + 1) * 128, :], [], BK(4), "cq")
                kb.dma(pool, CQ_[:, 64:128], b_[q * 128:(q + 1) * 128, :], [], BK(4), "cq")
        for h in range(4):
            kb.dma(pool, stB[:, h * 128:(h + 1) * 128], sgw_d[h, :, :], [], BK(5), "sgw")
        kb.dma(pool, prow[100:104, :], pscale_d[:, :], [], ["prow"], "prow")
        kb.dma(pool, prow[104:112, :], hlb_d[:, :], [], ["prow"], "prow")
        kb.dma(pool, prow[112:113, :], hong_d[:, :], [], ["prow"], "prow")
        poolw_stg = ablk_all[:].bitcast(F32).rearrange("p a b c -> p (a b c)")[:, 0:512].rearrange("p (g o) -> p g o", o=128)
        kb.dma(pool, poolw_stg, poolw_d.rearrange("g i o -> i g o"), [], ["ablk0", "ablk1"], "poolw")

        def gcol(l, i, k):
            c = (l * 6 + i) * 8 + k
            return pcol[:, c:c + 1]

        state = {"sq": 0, "sg": 0, "ab": 0, "w": 0, "xi": 0, "xo": 0, "tn": 0, "dn": 0}

        def rms_stats(src_fn, t):
            pbank, pkey = bank[6 + t % 2], f"b{6 + t % 2}"
            for k in range(KD):
                i = state["sq"] % 2
                state["sq"] += 1
                src, sk = src_fn(k)
                sqb = sqt[i][:].bitcast(BF16)[:, 0:TL]
                kb.op(act, lambda: A.activation(out=sqb, in_=src, func=AF.Square), [sk], [f"sqt{i}"])
                kb.mm(pbank[:, :], onesb[:, :], sqb, k == 0, k == KD - 1, ["onesb", f"sqt{i}"], [pkey], inc=True)
            kb.op(act, lambda: A.activation(out=pbank[:, :], in_=pbank[:, :], func=AF.Ln, scale=128.0 / D,
                                            bias=epsc[:, 0:1]), [pkey, "epsc"], [pkey])
            kb.op(act, lambda: A.activation(out=pbank[:, :], in_=pbank[:, :], func=AF.Exp, scale=-0.5), [pkey], [pkey])
            return pbank, pkey

        def prenorm(l, i):
            for t in range(NT):
                ts = slice(t * TL, (t + 1) * TL)
                rb, rk = rms_stats(lambda k: (x_sb[:, k, ts], f"x{t}"), t)
                for k in range(KD):
                    kb.op(dve, lambda: V.scalar_tensor_tensor(out=hn[:, k, ts], in0=x_sb[:, k, ts],
                                                              scalar=gcol(l, i, k), in1=rb[:, :],
                                                              op0=ALU.mult, op1=ALU.mult),
                          [f"x{t}", rk, "pcol"], [f"hn{t}"])

        def postnorm_add(l, i, half, tiles=None):
            for t in (range(NT) if tiles is None else tiles):
                ts = slice(t * TL, (t + 1) * TL)
                rb, rk = rms_stats(lambda k: (yacc[:, k, ts], f"ya{t}_{k}"), t)
                for k in range(KD):
                    j = state["tn"] % 2
                    state["tn"] += 1
                    kb.op(dve, lambda: V.scalar_tensor_tensor(out=tmpn[j][:], in0=yacc[:, k, ts],
                                                              scalar=gcol(l, i, k), in1=rb[:, :],
                                                              op0=ALU.mult, op1=ALU.mult),
                          [f"ya{t}_{k}", rk, "pcol"], [f"sgt{j}"])
                    kb.op(dve, lambda: V.scalar_tensor_tensor(out=x_sb[:, k, ts], in0=tmpn[j][:],
                                                              scalar=0.5 if half else 1.0, in1=x_sb[:, k, ts],
                                                              op0=ALU.mult, op1=ALU.add),
                          [f"sgt{j}", f"x{t}"], [f"x{t}"])

        def ffn(fi):
            blocks = [(c0, min(512, DFF - c0)) for c0 in range(0, DFF, 512)]
            stages = []
            for bi, (c0, cw) in enumerate(blocks):
                w = state["w"] % 2
                state["w"] += 1
                for t in range(NT):
                    ab = state["ab"] % 2
                    state["ab"] += 1
                    stages.append((bi, c0, cw, w, t, ab))

            def loadw(bi, c0, cw, w):
                nj = cw // 128
                wgv = wg_d[fi, :, :].rearrange("(k p) c -> p k c", p=128)
                wuv = wu_d[fi, :, :].rearrange("(k p) c -> p k c", p=128)
                kb.dma(pool, wgs[w][:, :, 0:cw], wgv[:, :, c0:c0 + cw], [], [f"wgs{w}"], f"wgs{w}")
                kb.dma(pool, wus[w][:, :, 0:cw], wuv[:, :, c0:c0 + cw], [], [f"wus{w}"], f"wus{w}")
                wdv = wd_d[fi, c0:c0 + cw, :].rearrange("(j p) d -> p j d", p=128)
                kb.dma(pool, wds[w][:, 0:nj, :], wdv, [], [f"wds{w}"], f"wds{w}")

            def gu(bi, c0, cw, w, t, ab):
                ts = slice(t * TL, (t + 1) * TL)
                nj = cw // 128
                for j in range(nj):
                    pg, pu = bank[(2 * j) % 4], bank[(2 * j + 1) % 4]
                    kg, ku = f"b{(2 * j) % 4}", f"b{(2 * j + 1) % 4}"
                    for k in range(KD):
                        kb.mm(pg[:, :], wgs[w][:, k, j * 128:(j + 1) * 128], hn[:, k, ts], k == 0, k == KD - 1,
                              [f"wgs{w}", f"hn{t}"], [kg])
                    for k in range(KD):
                        kb.mm(pu[:, :], wus[w][:, k, j * 128:(j + 1) * 128], hn[:, k, ts], k == 0, k == KD - 1,
                              [f"wus{w}", f"hn{t}"], [ku])
                    s_ = state["sg"] % 2
                    state["sg"] += 1
                    kb.op(act, lambda: A.activation(out=sgt[s_][:], in_=pg[:, :], func=AF.Silu), [kg], [f"sgt{s_}"])
                    kb.op(dve, lambda: V.tensor_tensor(out=ablk[ab][:, j, :], in0=sgt[s_][:], in1=pu[:, :], op=ALU.mult),
                          [f"sgt{s_}", ku], [f"ablk{ab}"])

            def down(bi, c0, cw, w, t, ab):
                ts = slice(t * TL, (t + 1) * TL)
                nj = cw // 128
                for m in range(KD):
                    bn = 4 + (state["dn"] % 3)
                    state["dn"] += 1
                    py, ky = bank[bn], f"b{bn}"
                    for j in range(nj):
                        kb.mm(py[:, :], wds[w][:, j, m * 128:(m + 1) * 128], ablk[ab][:, j, :], j == 0, j == nj - 1,
                              [f"wds{w}", f"ablk{ab}"], [ky])
                    if bi == 0:
                        kb.op(act, lambda: A.copy(out=yacc[:, m, ts], in_=py[:, :]), [ky], [f"ya{t}_{m}"])
                    else:
                        kb.op(dve, lambda: V.tensor_tensor(out=yacc[:, m, ts], in0=yacc[:, m, ts], in1=py[:, :], op=ALU.add),
                              [ky, f"ya{t}_{m}"], [f"ya{t}_{m}"])

            for i in range(len(stages) + 1):
                if i < len(stages):
                    if stages[i][4] == 0:
                        loadw(*stages[i][:4])
                    gu(*stages[i])
                if i > 0:
                    down(*stages[i - 1])

        def load_x(s):
            for b in range(S // 128):
                i = state["xi"] % 2
                state["xi"] += 1
                r0 = s * S + b * 128
                kb.dma(sp, xin[i], x_d[r0:r0 + 128, :], [], XK(i), f"xin{i}")
                t = (b * 128) // TL
                for k in range(KD):
                    pb = bank[4 + k % 4]
                    kb.op(pe, lambda: T.transpose(pb[:, 0:128], xin[i][:, k * 128:(k + 1) * 128], ident[:]),
                          XK(i) + ["ident"], [f"b{4 + k % 4}"])
                    eng, fn = (act, lambda: A.copy(out=x_sb[:, k, b * 128:(b + 1) * 128], in_=pb[:, 0:128])) if k % 2 == 0 else \
                              (dve, lambda: V.tensor_copy(out=x_sb[:, k, b * 128:(b + 1) * 128], in_=pb[:, 0:128]))
                    kb.op(eng, fn, [f"b{4 + k % 4}"], [f"x{t}"])

        def store_x(s):
            for b in range(S // 128):
                i = state["xo"] % 2
                state["xo"] += 1
                r0 = s * S + b * 128
                t = (b * 128) // TL
                for k in range(KD):
                    pb = bank[4 + k % 4]
                    kb.op(pe, lambda: T.transpose(pb[:, 0:128], x_sb[:, k, b * 128:(b + 1) * 128], ident[:]),
                          [f"x{t}", "ident"], [f"b{4 + k % 4}"])
                    eng, fn = (act, lambda: A.copy(out=xout[i][:, k * 128:(k + 1) * 128], in_=pb[:, 0:128])) if k % 2 == 0 else \
                              (dve, lambda: V.tensor_copy(out=xout[i][:, k * 128:(k + 1) * 128], in_=pb[:, 0:128]))
                    kb.op(eng, fn, [f"b{4 + k % 4}"], XK(2 + i))
                kb.dma(sp, out_d[r0:r0 + 128, :], xout[i], XK(2 + i), [], f"xout{i}")


        load_x(0)
        kb.op(pe, lambda: T.transpose(bank[7][:, 0:128], prow[:], ident[:]), ["prow", "ident"], ["b7"])
        kb.op(dve, lambda: V.tensor_copy(out=pcol[:], in_=bank[7][:, 0:128]), ["b7"], ["pcol"])
        def slab(t, i):
            return yacc[:, i, t * TL:(t + 1) * TL], f"ya{t}_{i}"

        s5dcol = lambda q: pcol[:, 96 + q:97 + q]

        sm = {}
        _tmp_names = ["lr", "li", "th", "ar", "ai", "den", "cre", "cim", "t0", "t1", "t2", "cbri", "cair", "ang"]
        kb.alias["sqt0"] = ["sqt0_main", "tmp_s", "sin_aux"] + ["sm_" + n for n in _tmp_names]
        def small(name, cols=32):
            if name in _tmp_names:
                i_ = _tmp_names.index(name)
                sm[name] = sqt[0][:, i_ * 32:(i_ + 1) * 32]
            else:
                sm[name] = kb.sb("s5_" + name, [128, cols], F32)
            return sm[name]
        sm["dt"] = dt_early
        for n_ in ["lr", "li", "dec", "th", "ar", "ai", "den", "cre", "cim", "t0", "t1", "t2", "cr", "ci",
                   "cbri", "cair", "ang"]:
            small(n_)
        sgn = kb.sb("sgn", [128, 1], F32)
        rmask = kb.sb("rmask", [128, 4], F32)
        cmask = kb.sb("cmask", [128, 4, 64], F32)
        pmat = kb.sb("pmat", [128, 128], F32)
        maskU = kb.sb("maskU", [128, 128], F32)
        COS = kb.sb("COS", [128, 32, 128], BF16)
        SIN = kb.sb("SIN", [128, 32, 128], BF16)
        BTRI = [kb.sb(f"BTRI{v}", [128, 4, 128], BF16) for v in range(4)]
        BTIR = [kb.sb(f"BTIR{v}", [128, 4, 128], BF16) for v in range(4)]
        W1s = kb.sb("W1s", [128, 32, 64], BF16)
        W2s = kb.sb("W2s", [128, 32, 64], BF16)
        s5init = kb.sb("s5init", [128, 32], F32)
        s5tmp = kb.sb("s5tmp", [128, 32], F32)
        WmT = kb.sb("WmT", [128, 4, 128], BF16)
        browb = kb.sb("browb", [128, 512], BF16)
        onesb = kb.sb("onesb", [128, 128], BF16)
        vnbf = [kb.sb(f"vnbf{i}", [128, 512], BF16) for i in range(2)]
        bnst = kb.sb("bnst", [128, 8], F32)
        mv = kb.sb("mv", [128, 4], F32)
        big = [hn[:, 2 * i:2 * i + 2, :].bitcast(F32).rearrange("p a b -> p (a b)") for i in range(4)]

        SM = lambda *names: ["sm_" + n for n in names]
        def vop(fn, R, W, eng=None, ss=False):
            kb.op(eng or dve, fn, R, W, ss=ss)
        kb.selfsync_all = True

        for ni, (name, src) in enumerate((("lr", lre_d), ("li", lim_d))):
            kb.op(pe, lambda: T.transpose(bank[6][:, 0:32], st32[:, ni * 128:(ni + 1) * 128], ident[0:32, 0:32]), ["st32", "ident"], ["b6"])
            vop(lambda: V.tensor_copy(out=sm[name][:], in_=bank[6][:, 0:32]), ["b6"], SM(name))
        kb.op(act, lambda: A.activation(out=sm["dt"][:], in_=sm["dt"][:], func=AF.Exp), SM("dt"), SM("dt"))
        vop(lambda: V.tensor_tensor(out=sm["t0"][:], in0=sm["lr"][:], in1=sm["dt"][:], op=ALU.mult), SM("lr", "dt"), SM("t0"))
        kb.op(act, lambda: A.activation(out=sm["dec"][:], in_=sm["t0"][:], func=AF.Exp), SM("t0"), SM("dec"))
        vop(lambda: V.tensor_tensor(out=sm["th"][:], in0=sm["li"][:], in1=sm["dt"][:], op=ALU.mult), SM("li", "dt"), SM("th"))

        def sin_of(out_ap, ang_ap, tmp_i, tmp_f, shift, R, W, keys_tmp):
            aux = out_ap_f32[0]
            vop(lambda: V.tensor_scalar(out=tmp_f, in0=ang_ap, scalar1=shift, scalar2=1.0 / TWO_PI, op0=ALU.add, op1=ALU.mult), R, keys_tmp)
            vop(lambda: V.tensor_copy(out=tmp_i, in_=tmp_f), keys_tmp, keys_tmp)
            vop(lambda: V.tensor_copy(out=tmp_f, in_=tmp_i), keys_tmp, keys_tmp)
            vop(lambda: V.tensor_scalar(out=tmp_f, in0=tmp_f, scalar1=-TWO_PI, scalar2=shift, op0=ALU.mult, op1=ALU.add), keys_tmp, keys_tmp)
            vop(lambda: V.tensor_tensor(out=tmp_f, in0=tmp_f, in1=ang_ap, op=ALU.add), keys_tmp + R, keys_tmp)
            vop(lambda: V.tensor_scalar(out=aux, in0=tmp_f, scalar1=math.pi, scalar2=-TWO_PI, op0=ALU.is_gt, op1=ALU.mult), keys_tmp, ["sin_aux"])
            vop(lambda: V.tensor_tensor(out=tmp_f, in0=tmp_f, in1=aux, op=ALU.add), keys_tmp + ["sin_aux"], keys_tmp)
            vop(lambda: V.tensor_scalar(out=aux, in0=tmp_f, scalar1=-math.pi, scalar2=TWO_PI, op0=ALU.is_lt, op1=ALU.mult), keys_tmp, ["sin_aux"])
            vop(lambda: V.tensor_tensor(out=tmp_f, in0=tmp_f, in1=aux, op=ALU.add), keys_tmp + ["sin_aux"], keys_tmp)
            kb.op(act, lambda: A.activation(out=out_ap, in_=tmp_f, func=AF.Sin), keys_tmp, W)

        out_ap_f32 = [None]
        aux_s = sqt[0][:, 14 * 32:15 * 32]
        tmp_s = sqt[0][:, 15 * 32:16 * 32]
        tmp_si = kb.sb("tmp_si", [128, 32], I32)
        def sin_small(outname, angname, shift, mult=1.0):
            out_ap_f32[0] = aux_s
            src = sm[angname][:]
            if mult != 1.0:
                vop(lambda: V.tensor_scalar(out=sm["ang"][:], in0=sm[angname][:], scalar1=mult, scalar2=None, op0=ALU.mult), SM(angname), SM("ang"))
                src = sm["ang"][:]
                R = SM("ang")
            else:
                R = SM(angname)
            sin_of(sm[outname][:], src, tmp_si[:], tmp_s, shift, R, SM(outname), ["tmp_s"])

        sin_small("ai", "th", 0.0)
        sin_small("ar", "th", math.pi / 2)
        sin_small("ci", "th", 0.0, mult=128.0)
        sin_small("cr", "th", math.pi / 2, mult=128.0)
        vop(lambda: V.tensor_tensor(out=sm["ar"][:], in0=sm["ar"][:], in1=sm["dec"][:], op=ALU.mult), SM("ar", "dec"), SM("ar"))
        vop(lambda: V.tensor_tensor(out=sm["ai"][:], in0=sm["ai"][:], in1=sm["dec"][:], op=ALU.mult), SM("ai", "dec"), SM("ai"))
        vop(lambda: V.tensor_tensor(out=sm["t0"][:], in0=sm["lr"][:], in1=sm["lr"][:], op=ALU.mult), SM("lr"), SM("t0"))
        vop(lambda: V.tensor_tensor(out=sm["t1"][:], in0=sm["li"][:], in1=sm["li"][:], op=ALU.mult), SM("li"), SM("t1"))
        vop(lambda: V.tensor_tensor(out=sm["den"][:], in0=sm["t0"][:], in1=sm["t1"][:], op=ALU.add), SM("t0", "t1"), SM("den"))
        vop(lambda: V.reciprocal(out=sm["den"][:], in_=sm["den"][:]), SM("den"), SM("den"))
        vop(lambda: V.tensor_scalar(out=sm["t2"][:], in0=sm["ar"][:], scalar1=-1.0, scalar2=None, op0=ALU.add), SM("ar"), SM("t2"))
        vop(lambda: V.tensor_tensor(out=sm["t0"][:], in0=sm["t2"][:], in1=sm["lr"][:], op=ALU.mult), SM("t2", "lr"), SM("t0"))
        vop(lambda: V.tensor_tensor(out=sm["t1"][:], in0=sm["ai"][:], in1=sm["li"][:], op=ALU.mult), SM("ai", "li"), SM("t1"))
        vop(lambda: V.tensor_tensor(out=sm["t0"][:], in0=sm["t0"][:], in1=sm["t1"][:], op=ALU.add), SM("t0", "t1"), SM("t0"))
        vop(lambda: V.tensor_tensor(out=sm["cre"][:], in0=sm["t0"][:], in1=sm["den"][:], op=ALU.mult), SM("t0", "den"), SM("cre"))
        vop(lambda: V.tensor_tensor(out=sm["t0"][:], in0=sm["ai"][:], in1=sm["lr"][:], op=ALU.mult), SM("ai", "lr"), SM("t0"))
        vop(lambda: V.tensor_tensor(out=sm["t1"][:], in0=sm["t2"][:], in1=sm["li"][:], op=ALU.mult), SM("t2", "li"), SM("t1"))
        vop(lambda: V.tensor_tensor(out=sm["t0"][:], in0=sm["t0"][:], in1=sm["t1"][:], op=ALU.subtract), SM("t0", "t1"), SM("t0"))
        vop(lambda: V.tensor_tensor(out=sm["cim"][:], in0=sm["t0"][:], in1=sm["den"][:], op=ALU.mult), SM("t0", "den"), SM("cim"))
        vop(lambda: V.tensor_scalar(out=sgn[:], in0=pidx[:], scalar1=64.0, scalar2=-2.0, op0=ALU.is_ge, op1=ALU.mult), ["pidx"], ["sgn"])
        vop(lambda: V.tensor_scalar(out=sgn[:], in0=sgn[:], scalar1=1.0, scalar2=None, op0=ALU.add), ["sgn"], ["sgn"])
        vop(lambda: V.tensor_scalar(out=sm["cbri"][:], in0=sm["cim"][:], scalar1=sgn[:, 0:1], scalar2=-1.0, op0=ALU.mult, op1=ALU.mult), SM("cim") + ["sgn"], SM("cbri"))
        vop(lambda: V.tensor_scalar(out=sm["cair"][:], in0=sm["cre"][:], scalar1=sgn[:, 0:1], scalar2=None, op0=ALU.mult), SM("cre") + ["sgn"], SM("cair"))
        vop(lambda: V.tensor_scalar(out=sm["t0"][:, 0:1], in0=pidx[:], scalar1=64.0, scalar2=-64.0, op0=ALU.is_ge, op1=ALU.mult), ["pidx"], SM("t0"))
        vop(lambda: V.tensor_tensor(out=sm["t0"][:, 0:1], in0=sm["t0"][:, 0:1], in1=pidx[:], op=ALU.add), SM("t0") + ["pidx"], SM("t0"))
        for v in range(4):
            vop(lambda: V.tensor_scalar(out=sm["t1"][:, 0:1], in0=sm["t0"][:, 0:1], scalar1=16.0 * v, scalar2=None, op0=ALU.is_ge), SM("t0"), SM("t1"))
            vop(lambda: V.tensor_scalar(out=sm["t1"][:, 1:2], in0=sm["t0"][:, 0:1], scalar1=16.0 * (v + 1), scalar2=None, op0=ALU.is_lt), SM("t0"), SM("t1"))
            vop(lambda: V.tensor_tensor(out=rmask[:, v:v + 1], in0=sm["t1"][:, 0:1], in1=sm["t1"][:, 1:2], op=ALU.mult), SM("t1"), ["rmask"])
        vop(lambda: V.memset(cmask[:], 0.0), [], ["cmask"])
        for v in range(4):
            vop(lambda: V.memset(cmask[:, v, 16 * v:16 * v + 16], 1.0), ["cmask"], ["cmask"])
        vop(lambda: V.tensor_scalar(out=pmat[:], in0=iot[:], scalar1=pidx[:, 0:1], scalar2=64.0, op0=ALU.subtract, op1=ALU.is_equal), ["iot", "pidx"], ["pmat"])
        vop(lambda: V.tensor_scalar(out=maskU[:], in0=iot[:], scalar1=pidx[:, 0:1], scalar2=-64.0, op0=ALU.subtract, op1=ALU.is_equal), ["iot", "pidx"], ["maskU"])
        vop(lambda: V.tensor_tensor(out=pmat[:], in0=pmat[:], in1=maskU[:], op=ALU.subtract), ["pmat", "maskU"], ["pmat"])
        vop(lambda: V.tensor_scalar(out=maskU[:], in0=iot[:], scalar1=pidx[:, 0:1], scalar2=None, op0=ALU.is_ge), ["iot", "pidx", "pmat"], ["maskU"])
        vop(lambda: V.memset(s5init[:], 0.0), [], ["s5init"])

        for gb in range(4):
            angt = big[0]
            tmpf = big[1]
            tmpi = big[2].bitcast(I32)
            out_ap_f32[0] = big[3]
            for gg in range(8):
                g = gb * 8 + gg
                vop(lambda: V.tensor_scalar(out=angt[:, gg * 128:(gg + 1) * 128], in0=iot[:], scalar1=sm["th"][:, g:g + 1], scalar2=None, op0=ALU.mult),
                    ["iot"] + SM("th"), HK)
            sin_of(SIN[:, gb * 8:(gb + 1) * 8, :].rearrange("p g t -> p (g t)"), angt, tmpi, tmpf, 0.0, HK, ["SIN"], HK)
            sin_of(COS[:, gb * 8:(gb + 1) * 8, :].rearrange("p g t -> p (g t)"), angt, tmpi, tmpf, math.pi / 2, HK, ["COS"], HK)

        P1 = big[0].rearrange("p (g c) -> p g c", c=16)[:, 0:32, :]
        P2 = big[1].rearrange("p (g c) -> p g c", c=16)[:, 0:32, :]
        XT = big[2].rearrange("p (g c) -> p g c", c=16)[:, 0:32, :]
        XT2 = big[3].rearrange("p (g c) -> p g c", c=16)[:, 0:32, :]
        for stt, sk_, Pdst in ((st_ri, K_ri, P1), (st_ir, K_ir, P2)):
            sv = stt.rearrange("g h (p c) -> g h p c", c=16)
            for c in range(16):
                kb.op(pe, lambda: T.transpose(bank[6][:, c * 32:(c + 1) * 32], sv[:, :, :, c], ident[0:32, 0:32]),
                      sk_ + ["ident"], ["b6"])
            vop(lambda: V.tensor_copy(out=Pdst, in_=bank[6][:, :].rearrange("p (c g) -> p g c", g=32)), ["b6"], HK)
        bc = lambda n_: sm[n_][:, :, None].broadcast_to([128, 32, 16])
        for which, (ca, pa, cb, pb_) in enumerate(((("cre"), P1, ("cbri"), P2), (("cair"), P2, ("cim"), P1))):
            vop(lambda: V.tensor_tensor(out=XT, in0=pa, in1=bc(ca), op=ALU.mult), HK + SM(ca), HK)
            vop(lambda: V.tensor_tensor(out=XT2, in0=pb_, in1=bc(cb), op=ALU.mult), HK + SM(cb), HK)
            vop(lambda: V.tensor_tensor(out=XT, in0=XT, in1=XT2, op=ALU.add), HK, HK)
            dst = BTRI if which == 0 else BTIR
            for q in range(4):
                src = big[2][:, q * 128:(q + 1) * 128]
                kb.op(pe, lambda: T.transpose(bank[6][:, 0:128], src, ident[:]), HK + ["ident"], ["b6"])
                for v in range(4):
                    vop(lambda: V.tensor_scalar(out=dst[v][:, q, :], in0=bank[6][:, 0:128], scalar1=rmask[:, v:v + 1], scalar2=None, op0=ALU.mult),
                        ["b6", "rmask"], [f"BT{which}{v}"])
        for which in range(2):
            for q in range(4):
                CQ = stA[:, (which * 4 + q) * 128:(which * 4 + q + 1) * 128]
                if which == 0:
                    vop(lambda: V.tensor_scalar(out=CQ[:, 64:128], in0=CQ[:, 64:128], scalar1=-1.0, scalar2=None, op0=ALU.mult), BK(4), BK(4))
                else:
                    vop(lambda: V.tensor_scalar(out=CQ, in0=CQ, scalar1=-1.0, scalar2=None, op0=ALU.mult), BK(4), BK(4))
                kb.op(pe, lambda: T.transpose(bank[6][:, 0:128], CQ, ident[:]), BK(4) + ["ident"], ["b6"])
                dstW = W1s if which == 0 else W2s
                for hb in range(2):
                    for v in range(4):
                        g = q * 8 + hb * 4 + v
                        vop(lambda: V.tensor_tensor(out=dstW[:, g, :], in0=bank[6][:, 64 * hb:64 * hb + 64], in1=cmask[:, v, :], op=ALU.mult),
                            ["b6", "cmask"], [f"Ws{which}"])
        for h in range(4):
            stg_ = stB[:, h * 128:(h + 1) * 128]
            kb.op(pe, lambda: T.transpose(bank[6][:, 0:128], stg_, ident[:]), BK(5) + ["ident"], ["b6"])
            vop(lambda: V.tensor_tensor(out=WmT[:, h, :], in0=bank[6][:, 0:128], in1=maskU[:], op=ALU.mult), ["b6", "maskU"], ["WmT"])
        vop(lambda: V.tensor_copy(out=browb[:], in_=stB[:, 512:1024]), BK(5), ["browb"])
        vop(lambda: V.memset(onesb[:], 1.0 / 128.0), [], ["onesb"])

        kb.selfsync_all = False
        kb.release(["prow", "st32", "sm_dt", "p1", "p2", "cq", "sgw", "brow", "sngb"])
        WSL = [wgs[0], wus[0], wgs[1], wus[1]]
        WSK = ["wgs0", "wus0", "wgs1", "wus1"]

        def load_win(wd_, nblk):
            for b in range(nblk):
                for k in range(KD):
                    kb.dma(pool, WSL[b][:, k, :], wd_[k * 128:(k + 1) * 128, b * 512:(b + 1) * 512], [], [WSK[b]], WSK[b])

        def load_wout(wd_):
            for k in range(KD):
                kb.dma(pool, wds[k // 4][:, k % 4, :], wd_[k * 128:(k + 1) * 128, :], [], [f"wds{k // 4}"], f"wds{k // 4}")

        def zfm(b, cc, t, pbank, pkey):
            ts = slice(t * TL, (t + 1) * TL)
            for k in range(KD):
                kb.mm(pbank[:, :], WSL[b][:, k, cc * 128:(cc + 1) * 128], hn[:, k, ts], k == 0, k == KD - 1, [WSK[b], f"hn{t}"], [pkey])

        def wout_apply(t):
            ts = slice(t * TL, (t + 1) * TL)
            for m in range(KD):
                pb, pk = bank[m % 2], f"b{m % 2}"
                for k in range(KD):
                    kb.mm(pb[:, :], wds[k // 4][:, k % 4, m * 128:(m + 1) * 128], ymix[:, k, :], k == 0, k == KD - 1,
                          [f"wds{k // 4}", "ymix"], [pk])
                kb.op(act, lambda: A.copy(out=yacc[:, m, ts], in_=pb[:, :]), [pk], [f"ya{t}_{m}"])
                if t == 0 and m in (0, 5):
                    dump(f"y{m}", yacc[:, m, ts], [f"ya{t}_{m}"], 512)

        def mixer0(s):
            prenorm(0, 2)
            load_win(ewin_d, 3)
            load_wout(ewout_d)
            for k in range(4):
                kb.dma(pool, wus[1][:, k, :], glu_d[k * 128:(k + 1) * 128, :], [], ["wus1"], "wus1")
            for t in range(NT):
                ts = slice(t * TL, (t + 1) * TL)
                import os as _os
                _dbg = _os.environ.get("MIXDBG", "")
                if "nosgu" in _dbg or "nos5" in _dbg:
                    vop(lambda: V.memset(ymix[:], 0.0), [], ["ymix"])
                for h in range(4 if "nosgu" not in _dbg else 0):
                    zfm(1, h, t, bank[h % 2], f"b{h % 2}")
                    sl, sk = slab(t, h)
                    kb.op(act, lambda: A.activation(out=sl, in_=bank[h % 2][:, :], func=AF.Gelu_apprx_tanh), [f"b{h % 2}"], [sk])
                    if t == 0 and h in (0, 1):
                        dump(f"u{h}", sl, [sk], 512)
                for tb in range(4 if "nosgu" not in _dbg else 0):
                    tok = slice(t * TL + tb * 128, t * TL + (tb + 1) * 128)
                    for k in range(KD):
                        kb.mm(bank[2][:, :], hn[:, k, tok], WSL[2][:, k, :], k == 0, k == KD - 1, [WSK[2], f"hn{t}"], ["b2"])
                    vt, vk = slab(t, 4 + tb % 2)
                    kb.op(act, lambda: A.activation(out=vt, in_=bank[2][:, :], func=AF.Gelu_apprx_tanh), ["b2"], [vk])
                    vop(lambda: V.bn_stats(out=bnst[:, 0:6], in_=vt), [vk], ["bnst"])
                    vop(lambda: V.tensor_tensor(out=mv[:, 0:1], in0=bnst[:, 1:2], in1=bnst[:, 4:5], op=ALU.add), ["bnst"], ["mv"], ss=True)
                    vop(lambda: V.tensor_scalar(out=mv[:, 0:1], in0=mv[:, 0:1], scalar1=0.5, scalar2=None, op0=ALU.mult), ["mv"], ["mv"], ss=True)
                    vop(lambda: V.tensor_tensor(out=mv[:, 3:4], in0=bnst[:, 1:2], in1=bnst[:, 4:5], op=ALU.subtract), ["bnst"], ["mv"], ss=True)
                    vop(lambda: V.tensor_tensor(out=mv[:, 3:4], in0=mv[:, 3:4], in1=mv[:, 3:4], op=ALU.mult), ["mv"], ["mv"], ss=True)
                    vop(lambda: V.tensor_tensor(out=mv[:, 1:2], in0=bnst[:, 2:3], in1=bnst[:, 5:6], op=ALU.add), ["bnst", "mv"], ["mv"], ss=True)
                    vop(lambda: V.tensor_scalar(out=mv[:, 1:2], in0=mv[:, 1:2], scalar1=1.0 / 512.0, scalar2=None, op0=ALU.mult), ["mv"], ["mv"], ss=True)
                    vop(lambda: V.scalar_tensor_tensor(out=mv[:, 1:2], in0=mv[:, 3:4], scalar=0.25, in1=mv[:, 1:2], op0=ALU.mult, op1=ALU.add), ["mv"], ["mv"], ss=True)
                    kb.op(act, lambda: A.activation(out=mv[:, 2:3], in_=mv[:, 1:2], func=AF.Ln, bias=epsc[:, 0:1], scale=1.0), ["mv", "epsc"], ["mv"], ss=True)
                    kb.op(act, lambda: A.activation(out=mv[:, 2:3], in_=mv[:, 2:3], func=AF.Exp, scale=-0.5), ["mv"], ["mv"], ss=True)
                    vop(lambda: V.tensor_scalar(out=vt, in0=vt, scalar1=mv[:, 0:1], scalar2=mv[:, 2:3], op0=ALU.subtract, op1=ALU.mult), [vk, "mv"], [vk], ss=True)
                    if t == 0 and tb == 0:
                        dump("vn0", vt, [vk], 512)
                        dump("mv0", mv[:, 0:4], ["mv"], 4)
                    vb = vnbf[tb % 2]
                    vop(lambda: V.tensor_tensor(out=vb[:], in0=vt, in1=sngb[:], op=ALU.mult), [vk, "sngb"], [f"vnbf{tb % 2}"])
                    for h in range(4):
                        pb = bank[4 + h]
                        kb.mm(pb[:, tb * 128:(tb + 1) * 128], vb[:, h * 128:(h + 1) * 128], WmT[:, h, :], True, False,
                              [f"vnbf{tb % 2}", "WmT"], [f"b{4 + h}"], inc=False)
                        kb.mm(pb[:, tb * 128:(tb + 1) * 128], onesb[:, :], browb[:, h * 128:(h + 1) * 128], False, True,
                              ["onesb", "browb"], [f"b{4 + h}"], inc=True)
                if t == 0 and dbg_d is not None and not _os.environ.get("NOS0"):
                    kb.op(dve, lambda: V.tensor_copy(out=sgt[0][:], in_=bank[4][:, :]), ["b4"], ["sgt0"])
                    dump("s0", sgt[0][:], ["sgt0"], 512)
                    kb.op(dve, lambda: V.tensor_copy(out=sgt[1][:], in_=bank[5][:, :]), ["b5"], ["sgt1"])
                    dump("s1", sgt[1][:], ["sgt1"], 512)
                for h in range(4 if "nosgu" not in _dbg else 0):
                    sl, sk = slab(t, h)
                    vop(lambda: V.tensor_tensor(out=ymix[:, 4 + h, :], in0=sl, in1=bank[4 + h][:, :], op=ALU.mult), [sk, f"b{4 + h}"], ["ymix"])
                if "nos5" in _dbg:
                    wout_apply(t)
                    continue
                for q in range(4):
                    zfm(0, q, t, bank[q % 2], f"b{q % 2}")
                    sl, sk = slab(t, q)
                    kb.op(act, lambda: A.copy(out=sl, in_=bank[q % 2][:, :]), [f"b{q % 2}"], [sk])
                    vop(lambda: V.tensor_copy(out=ubf[:, q, :], in_=bank[q % 2][:, :]), [f"b{q % 2}"], ["ablk0"])
                wk, wkk = slab(t, 4)
                gk, gkk = slab(t, 5)
                gcs = gk.bitcast(BF16)
                its = [(c4, g) for c4 in range(4) for g in range(32)]

                def s5proj(c4, g):
                    q, hb, v = g // 8, (g % 8) // 4, g % 4
                    rows = slice(64 * hb, 64 * hb + 64)
                    cs = slice(c4 * 128, (c4 + 1) * 128)
                    kb.mm(bank[g % 2][:, 0:128], BTRI[v][rows, q, :], ubf[rows, q, cs], True, True, [f"BT0{v}", "ablk0"], [f"b{g % 2}"])
                    kb.mm(bank[2 + g % 2][:, 0:128], BTIR[v][rows, q, :], ubf[rows, q, cs], True, True, [f"BT1{v}", "ablk0"], [f"b{2 + g % 2}"])

                s5proj(*its[0])
                for ii, (c4, g) in enumerate(its):
                    if True:
                        cs = slice(c4 * 128, (c4 + 1) * 128)
                        q, hb, v = g // 8, (g % 8) // 4, g % 4
                        rows = slice(64 * hb, 64 * hb + 64)
                        pb, pk = bank[g % 2], f"b{g % 2}"
                        pb2, pk2 = bank[2 + g % 2], f"b{2 + g % 2}"
                        r2 = g % 2
                        if ii + 1 < len(its):
                            s5proj(*its[ii + 1])
                        t1 = wk[:, r2 * 256:r2 * 256 + 128]
                        t2 = wk[:, r2 * 256 + 128:r2 * 256 + 256]
                        k1 = wkk + f"_{r2}"
                        vop(lambda: V.tensor_tensor(out=t1, in0=pb[:, 0:128], in1=COS[:, g, :], op=ALU.mult), [pk, "COS"], [k1 + "a"])
                        vop(lambda: V.tensor_tensor(out=t2, in0=pb2[:, 0:128], in1=SIN[:, g, :], op=ALU.mult), [pk2, "SIN"], [k1 + "b"])
                        vop(lambda: V.tensor_tensor(out=t1, in0=t1, in1=t2, op=ALU.add), [k1 + "a", k1 + "b"], [k1 + "a"])
                        Gt = gk[:, 256 + r2 * 128:256 + (r2 + 1) * 128]
                        kG = gkk + f"_G{r2}"
                        vop(lambda: V.tensor_tensor_scan(out=Gt, data0=sm["dec"][:, g:g + 1].broadcast_to([128, 128]), data1=t1,
                                                         initial=s5init[:, g:g + 1], op0=ALU.mult, op1=ALU.add),
                            [k1 + "a", "s5init", "sm_dec"], [kG])
                        kb.mm(pb[:, 256:257], pmat[:], Gt[:, 127:128], True, True, ["pmat", kG], [pk])
                        Gc = gcs[:, r2 * 256:r2 * 256 + 128]
                        Gs = gcs[:, r2 * 256 + 128:r2 * 256 + 256]
                        kC = gkk + f"_C{r2}"
                        vop(lambda: V.tensor_tensor(out=Gc, in0=Gt, in1=COS[:, g, :], op=ALU.mult), [kG, "COS"], [kC + "c"])
                        vop(lambda: V.tensor_scalar(out=s5tmp[:, g:g + 1], in0=Gt[:, 127:128], scalar1=sm["cr"][:, g:g + 1], scalar2=None, op0=ALU.mult),
                            [kG, "sm_cr"], ["s5tmp"])
                        vop(lambda: V.tensor_tensor(out=Gs, in0=Gt, in1=SIN[:, g, :], op=ALU.mult), [kG, "SIN"], [kC + "s"])
                        vop(lambda: V.scalar_tensor_tensor(out=s5init[:, g:g + 1], in0=pb[:, 256:257], scalar=sm["ci"][:, g:g + 1], in1=s5tmp[:, g:g + 1],
                                                           op0=ALU.mult, op1=ALU.add), [pk, "sm_ci", "s5tmp"], ["s5init"])
                        yb, yk = bank[4 + q], f"b{4 + q}"
                        kb.mm(yb[rows, cs], W1s[:, g, :], Gc, v == 0, False, ["Ws0", kC + "c"], [yk], inc=True)
                        kb.mm(yb[rows, cs], W2s[:, g, :], Gs, False, v == 3, ["Ws1", kC + "s"], [yk], inc=True)
                for q in range(4):
                    sl, sk = slab(t, q)
                    vop(lambda: V.scalar_tensor_tensor(out=sl, in0=sl, scalar=s5dcol(q), in1=bank[4 + q][:, :], op0=ALU.mult, op1=ALU.add),
                        [sk, f"b{4 + q}", "pcol"], [sk])
                    kb.op(act, lambda: A.activation(out=sl, in_=sl, func=AF.Gelu_apprx_tanh), [sk], [sk])
                    vop(lambda: V.tensor_copy(out=ubf[:, q, :], in_=sl), [sk], ["ablk0"])
                for q2 in range(4):
                    pb, pk = bank[q2 % 2], f"b{q2 % 2}"
                    for q in range(4):
                        kb.mm(pb[:, :], wus[1][:, q, q2 * 128:(q2 + 1) * 128], ubf[:, q, :], q == 0, q == 3, ["wus1", "ablk0"], [pk])
                    s_ = state["sg"] % 2
                    state["sg"] += 1
                    kb.op(act, lambda: A.activation(out=sgt[s_][:], in_=pb[:, :], func=AF.Sigmoid), [pk], [f"sgt{s_}"])
                    sl, sk = slab(t, q2)
                    vop(lambda: V.tensor_tensor(out=ymix[:, q2, :], in0=sl, in1=sgt[s_][:], op=ALU.mult), [sk, f"sgt{s_}"], ["ymix"])
                wout_apply(t)
                postnorm_add(0, 3, False, [t])


        kb.selfsync_all = True
        poolW = kb.sb("poolW", [128, 4, 128], BF16)
        Mt = kb.sb("Mt", [128, 12, 128], BF16)
        zc_buf = [kb.sb(f"zc{i}", [128, 512], BF16) for i in range(2)]
        Sf = kb.sb("Sf", [128, 4, 128], F32)
        rmask512 = kb.sb("rmask512", [128, 512], BF16)
        BD16 = kb.sb("BD16", [128, 128], BF16)
        cm8 = kb.sb("cm8", [128, 8], BF16)
        lbc = kb.sb("lbc", [128, 8], F32)
        vop(lambda: V.tensor_copy(out=poolW[:], in_=poolw_stg), ["ablk0", "ablk1"], ["poolW"])
        vop(lambda: V.tensor_tensor(out=lbc[:, 0:4], in0=pcol[:, 108:112], in1=pcol[:, 104:108], op=ALU.subtract), ["pcol"], ["lbc"])
        kb.op(act, lambda: A.activation(out=lbc[:, 0:4], in_=lbc[:, 0:4], func=AF.Sigmoid), ["lbc"], ["lbc"])
        vop(lambda: V.tensor_scalar(out=lbc[:, 4:8], in0=lbc[:, 0:4], scalar1=-1.0, scalar2=1.0, op0=ALU.mult, op1=ALU.add), ["lbc"], ["lbc"])
        vop(lambda: V.memset(Sf[:], 0.0), [], ["Sf"])
        dm = big[0][:, 0:128]
        t_a = big[0][:, 128:256]
        t_b = big[0][:, 256:384]
        t_c = big[0][:, 384:512]
        t_d = big[0][:, 512:640]
        vop(lambda: V.tensor_scalar(out=dm, in0=iot[:], scalar1=pidx[:, 0:1], scalar2=None, op0=ALU.subtract), ["iot", "pidx"], HK)
        for gi, win in enumerate((2, 4, 8, 16)):
            vop(lambda: V.tensor_scalar(out=t_a, in0=dm, scalar1=0.0, scalar2=None, op0=ALU.is_ge), HK, HK)
            vop(lambda: V.tensor_scalar(out=t_b, in0=dm, scalar1=float(win), scalar2=None, op0=ALU.is_lt), HK, HK)
            vop(lambda: V.tensor_tensor(out=t_a, in0=t_a, in1=t_b, op=ALU.mult), HK, HK)
            vop(lambda: V.scalar_tensor_tensor(out=Mt[:, 3 * gi, :], in0=t_a, scalar=1.0 / win, in1=ident[:], op0=ALU.mult, op1=ALU.subtract), HK + ["ident"], ["Mt"])
            vop(lambda: V.tensor_scalar(out=t_c, in0=iot[:], scalar1=1.0, scalar2=float(win), op0=ALU.add, op1=ALU.min), ["iot"], HK)
            vop(lambda: V.reciprocal(out=t_c, in_=t_c), HK, HK)
            vop(lambda: V.tensor_tensor(out=t_c, in0=t_c, in1=t_a, op=ALU.mult), HK, HK)
            vop(lambda: V.tensor_tensor(out=Mt[:, 3 * gi + 2, :], in0=t_c, in1=ident[:], op=ALU.subtract), HK + ["ident"], ["Mt"])
            vop(lambda: V.tensor_scalar(out=t_d, in0=dm, scalar1=128.0, scalar2=float(win), op0=ALU.add, op1=ALU.is_lt), HK, HK)
            vop(lambda: V.tensor_scalar(out=Mt[:, 3 * gi + 1, :], in0=t_d, scalar1=1.0 / win, scalar2=None, op0=ALU.mult), HK, ["Mt"])
        colch = big[1][:, 0:128]
        rowch = big[1][:, 128:256]
        kb.op(pool, lambda: G.iota(colch, pattern=[[1, 8], [0, 16]], base=0, channel_multiplier=0, allow_small_or_imprecise_dtypes=True), [], HK)
        kb.op(pe, lambda: T.transpose(bank[6][:, 0:128], colch, ident[:]), HK + ["ident"], ["b6"])
        vop(lambda: V.tensor_copy(out=rowch, in_=bank[6][:, 0:128]), ["b6"], HK)
        vop(lambda: V.tensor_tensor(out=t_b, in0=colch, in1=rowch, op=ALU.is_equal), HK, HK)
        vop(lambda: V.tensor_scalar(out=t_a, in0=dm, scalar1=0.0, scalar2=None, op0=ALU.is_ge), HK, HK)
        vop(lambda: V.tensor_tensor(out=BD16[:], in0=t_a, in1=t_b, op=ALU.mult), HK, ["BD16"])
        vop(lambda: V.tensor_scalar(out=cm8[:], in0=iot[:, 0:8], scalar1=rowch[:, 0:1], scalar2=None, op0=ALU.is_equal), ["iot"] + HK, ["cm8"])
        jidx = big[2][:, 0:512]
        kb.op(pool, lambda: G.iota(jidx, pattern=[[0, 32], [1, 16]], base=0, channel_multiplier=0, allow_small_or_imprecise_dtypes=True), [], HK)
        vop(lambda: V.tensor_scalar(out=rmask512[:], in0=jidx, scalar1=0.0, scalar2=None, op0=ALU.is_gt), HK, ["rmask512"])
        kb.selfsync_all = False
        kb.release(["poolw"])
        WSL.append(ablk_all[:].rearrange("p a b c -> p (a b) c"))
        WSK.append("ablkW")
        kb.alias["ablkW"] = ["ablk0", "ablk1"]
        pscol = lambda gi: pcol[:, 100 + gi:101 + gi]
        ongcol = pcol[:, 112:113]
        l1 = {"blk": 0}

        import os as _os2
        _os_ss = int(_os2.environ.get("HGRN_SS", "0"))

        def mixer1(s):
            prenorm(1, 2)
            load_win(owin_d, 5)
            load_wout(owout_d)
            for t in range(NT):
                ts = slice(t * TL, (t + 1) * TL)
                S_ = lambda i: slab(t, i)
                qk, qkk = S_(2)
                Qt = qk.bitcast(BF16)[:, 0:512]
                Kt = qk.bitcast(BF16)[:, 512:1024]
                v34a, v3k = S_(3)
                v34b, v4k = S_(4)
                vtm = [v34a.bitcast(BF16)[:, 0:512], v34a.bitcast(BF16)[:, 512:1024],
                       v34b.bitcast(BF16)[:, 0:512], v34b.bitcast(BF16)[:, 512:1024]]
                vtk = [v3k, v3k, v4k, v4k]
                vx, vxk = S_(5)
                Vexp = vx.bitcast(BF16).rearrange("p (c v) -> p c v", v=128)
                sbs, sbk = S_(6)
                Sb = sbs.bitcast(BF16).rearrange("p (c v) -> p c v", v=128)
                m7, m7k = S_(7)
                m7b = m7.bitcast(BF16)
                scTm = m7b[:, 0:128]
                Khtm = m7b[:, 128:256]
                dbf = m7b[:, 256:768].rearrange("p (g t) -> p g t", t=128)
                Tg, Tgk = S_(0)
                Kh32, Kh32k = S_(1)
                T1, T1k = sgt[0][:], "sgt0"
                T2, T2k = sgt[1][:], "sgt1"
                T3, T3k = sqt[0][:], "sqt0"
                T4, T4k = sqt[1][:], "sqt1"
                for tb in range(4):
                    tok = slice(t * TL + tb * 128, t * TL + (tb + 1) * 128)
                    first = (s == 0 and t == 0 and tb == 0)
                    cur = l1["blk"] % 2
                    l1["blk"] += 1
                    zc, zck = zc_buf[cur], f"zc{cur}"
                    zp, zpk = zc_buf[1 - cur], f"zc{1 - cur}"
                    for k in range(KD):
                        kb.mm(bank[2][:, :], hn[:, k, tok], WSL[0][:, k, :], k == 0, k == KD - 1, [WSK[0], f"hn{t}"], ["b2"])
                    kb.op(act, lambda: A.copy(out=zc[:], in_=bank[2][:, :]), ["b2"], [zck])
                    for gi in range(4):
                        gsl = slice(gi * 128, (gi + 1) * 128)
                        kb.mm(bank[3][:, gsl], zc[:, gsl], Mt[:, 3 * gi + (2 if first else 0), :], True, first, [zck, "Mt"], ["b3"], inc=first)
                        if not first:
                            kb.mm(bank[3][:, gsl], zp[:, gsl], Mt[:, 3 * gi + 1, :], False, True, [zpk, "Mt"], ["b3"], inc=True)
                    vop(lambda: V.tensor_copy(out=dbf, in_=bank[3][:, :].rearrange("p (g t) -> p g t", t=128)), ["b3"], [m7k + "_d"])
                    for gi in range(4):
                        gsl = slice(gi * 128, (gi + 1) * 128)
                        kb.mm(bank[4][:, gsl], poolW[:, gi, :], dbf[:, gi, :], True, True, ["poolW", m7k + "_d"], ["b4"])
                    for gi in range(4):
                        gsl = slice(gi * 128, (gi + 1) * 128)
                        vop(lambda: V.tensor_scalar(out=ymix[:, gi, tb * 128:(tb + 1) * 128], in0=bank[4][:, gsl], scalar1=pscol(gi), scalar2=None, op0=ALU.mult),
                            ["b4", "pcol"], ["ymix"])
                    for k in range(KD):
                        kb.mm(bank[2][:, :], hn[:, k, tok], WSL[3][:, k, :], k == 0, k == KD - 1, [WSK[3], f"hn{t}"], ["b2"])
                    kb.op(act, lambda: A.copy(out=vtm[tb], in_=bank[2][:, :]), ["b2"], [vtk[tb]])
                for h in range(4):
                    zfm(1, h, t, bank[0], "b0")
                    kb.op(act, lambda: A.activation(out=T4, in_=bank[0][:, :], func=AF.Silu), ["b0"], [T4k])
                    zfm(2, h, t, bank[1], "b1")
                    kb.op(act, lambda: A.activation(out=T1, in_=bank[1][:, :], func=AF.Sigmoid), ["b1"], [T1k])
                    vop(lambda: V.tensor_scalar(out=T1, in0=T1, scalar1=lbc[:, 4 + h:5 + h], scalar2=lbc[:, h:h + 1], op0=ALU.mult, op1=ALU.add), [T1k, "lbc"], [T1k])
                    vop(lambda: V.tensor_scalar(out=T2, in0=T1, scalar1=-1.0, scalar2=1.0, op0=ALU.mult, op1=ALU.add), [T1k], [T2k])
                    kb.op(act, lambda: A.activation(out=T1, in_=T1, func=AF.Ln), [T1k], [T1k])
                    vop(lambda: V.tensor_tensor_scan(out=T3, data0=rmask512[:], data1=T1, initial=0.0, op0=ALU.mult, op1=ALU.add), [T1k, "rmask512"], [T3k])
                    kb.op(act, lambda: A.activation(out=T1, in_=T3, func=AF.Exp), [T3k], [T1k])
                    kb.op(act, lambda: A.activation(out=T3, in_=T3, func=AF.Exp, scale=-1.0), [T3k], [T3k])
                    vop(lambda: V.tensor_tensor(out=Qt, in0=T4, in1=T1, op=ALU.mult), [T4k, T1k], [qkk])
                    vop(lambda: V.tensor_tensor(out=T2, in0=T2, in1=T3, op=ALU.mult), [T2k, T3k], [T2k])
                    vop(lambda: V.tensor_copy(out=Kt, in_=T2), [T2k], [qkk])
                    eb3 = T1.rearrange("p (c j) -> p c j", j=16)
                    vop(lambda: V.tensor_tensor(out=Kh32.rearrange("p (c j) -> p c j", j=16), in0=T2.rearrange("p (c j) -> p c j", j=16),
                                                in1=eb3[:, :, 15:16].broadcast_to([128, 32, 16]), op=ALU.mult), [T2k, T1k], [Kh32k])
                    zfm(4, h, t, bank[0], "b0")
                    kb.op(act, lambda: A.activation(out=Tg, in_=bank[0][:, :], func=AF.Silu), ["b0"], [Tgk])
                    ob, obk = bank[7], "b7"

                    def Sc(c):
                        if c == 0 or c == 8:
                            return Sf[:, h, :], "Sf"
                        if c <= 4:
                            return T4[:, (c - 1) * 128:c * 128], T4k
                        return T2[:, (c - 5) * 128:(c - 4) * 128], T2k

                    def ubank(tb):
                        return ((bank[5], "b5"), (bank[6], "b6")) if tb % 2 == 0 else ((bank[0], "b0"), (bank[1], "b1"))

                    def hfront(tb):
                        cs = slice(tb * 128, (tb + 1) * 128)
                        p_ = tb % 2
                        scT_, Kht_ = m7b[:, 768 * p_:768 * p_ + 128], m7b[:, 768 * p_ + 128:768 * p_ + 256]
                        pk_ = m7k + f"_p{p_}"
                        kb.mm(bank[3][:, 0:128], Kt[:, cs], Qt[:, cs], True, True, [qkk], ["b3"])
                        vop(lambda: V.tensor_tensor(out=scT_, in0=bank[3][:, 0:128], in1=BD16[:], op=ALU.mult), ["b3", "BD16"], [pk_])
                        kb.op(pe, lambda: T.transpose(bank[4][:, 0:128], Kh32[:, cs], ident[:]), [Kh32k, "ident"], ["b4"])
                        kb.op(act, lambda: A.copy(out=Kht_, in_=bank[4][:, 0:128]), ["b4"], [pk_])
                        vop(lambda: V.tensor_tensor(out=Vexp, in0=vtm[tb][:, None, h * 128:(h + 1) * 128].broadcast_to([128, 8, 128]),
                                                    in1=cm8[:, :, None].broadcast_to([128, 8, 128]), op=ALU.mult), [vtk[tb], "cm8"], [vxk])
                        Vf = Vexp.rearrange("p c v -> p (c v)")
                        (u0, u0k), (u1, u1k) = ubank(tb)
                        kb.mm(u0[:, :], Kht_, Vf[:, 0:512], True, True, [pk_, vxk], [u0k])
                        kb.mm(u1[:, :], Kht_, Vf[:, 512:1024], True, True, [pk_, vxk], [u1k])

                    def hmid(tb):
                        ub2 = ubank(tb)
                        kb.op(act, lambda: A.copy(out=Sb[:, 0, :], in_=Sf[:, h, :]), ["Sf"], [sbk])
                        for c in range(8):
                            ub, ubk = ub2[0] if c < 4 else ub2[1]
                            col = tb * 128 + c * 16 + 15
                            src, srck = Sc(c)
                            dst, dstk = Sc(c + 1)
                            vop(lambda: V.scalar_tensor_tensor(out=dst, in0=src, scalar=T1[:, col:col + 1], in1=ub[:, (c % 4) * 128:(c % 4 + 1) * 128],
                                                               op0=ALU.mult, op1=ALU.add), [srck, T1k, ubk], [dstk], ss=bool(_os_ss))
                        kb.op(act, lambda: A.copy(out=Sb[:, 1:5, :].rearrange("p c v -> p (c v)"), in_=T4), [T4k], [sbk])
                        kb.op(act, lambda: A.copy(out=Sb[:, 5:8, :].rearrange("p c v -> p (c v)"), in_=T2[:, 0:384]), [T2k], [sbk])

                    def hback(tb):
                        cs = slice(tb * 128, (tb + 1) * 128)
                        p_ = tb % 2
                        scT_ = m7b[:, 768 * p_:768 * p_ + 128]
                        pk_ = m7k + f"_p{p_}"
                        kb.mm(ob[:, cs], vtm[tb][:, h * 128:(h + 1) * 128], scT_, True, False, [vtk[tb], pk_], [obk], inc=True)
                        for c in range(8):
                            kb.mm(ob[:, tb * 128 + c * 16:tb * 128 + (c + 1) * 16], Sb[:, c, :], Qt[:, tb * 128 + c * 16:tb * 128 + (c + 1) * 16],
                                  False, c == 7, [sbk, qkk], [obk], inc=True)

                    hfront(0)
                    for tb in range(4):
                        if tb + 1 < 4:
                            hfront(tb + 1)
                        hmid(tb)
                        hback(tb)
                    kb.op(act, lambda: A.activation(out=T2, in_=ob[:, :], func=AF.Square), [obk], [T2k])
                    kb.mm(bank[2][:, :], ones[:], T2, True, True, ["ones", T2k], ["b2"])
                    kb.op(act, lambda: A.activation(out=T3, in_=bank[2][:, :], func=AF.Ln, scale=1.0 / 128.0, bias=epsc[:, 0:1]), ["b2", "epsc"], [T3k])
                    kb.op(act, lambda: A.activation(out=T3, in_=T3, func=AF.Exp, scale=-0.5), [T3k], [T3k])
                    vop(lambda: V.tensor_tensor(out=T2, in0=T3, in1=ob[:, :], op=ALU.mult), [T3k, obk], [T2k])
                    vop(lambda: V.scalar_tensor_tensor(out=ymix[:, 4 + h, :], in0=T2, scalar=ongcol, in1=Tg, op0=ALU.mult, op1=ALU.mult),
                        [T2k, "pcol", Tgk], ["ymix"])
                wout_apply(t)
                postnorm_add(1, 3, False, [t])

        for s in range(NSUP):
            if s > 0:
                load_x(s)
            for l in range(2):
                if 'ffn' in stages:
                    prenorm(l, 0)
                    ffn(l * 2 + 0)
                    postnorm_add(l, 1, True)
                if 'mix0' in stages and l == 0:
                    mixer0(s)
                if 'mix1' in stages and l == 1:
                    mixer1(s)
                if 'ffn' in stages:
                    prenorm(l, 4)
                    ffn(l * 2 + 1)
                    postnorm_add(l, 5, True)
            store_x(s)
        kb.finish(["xout0", "xout1"] )
        stuck = kb.simulate()
        if stuck:
            raise RuntimeError(f"semaphore deadlock: {stuck}")
    return nc


_CACHE = {}


def make_common(inputs):
    f = lambda k, shp: np.ascontiguousarray(inputs[k], dtype=np.float32).reshape(shp)
    return {
        "norm_g": f("norm_g", (96, 128)),
        "ffn_wg": f("ffn_wg", (4, D, DFF)), "ffn_wu": f("ffn_wu", (4, D, DFF)), "ffn_wd": f("ffn_wd", (4, DFF, D)),
        "even_w_in": f("even_w_in", (D, 1536)), "even_w_out": f("even_w_out", (D, D)),
        "s5_lam_re": f("s5_lam_re", (32, 64)), "s5_lam_im": f("s5_lam_im", (32, 64)), "s5_log_dt": f("s5_log_dt", (1, 32)),
        "s5_b_re": f("s5_b_re", (32, 64, 16)), "s5_b_im": f("s5_b_im", (32, 64, 16)),
        "s5_c_re": f("s5_c_re", (512, 64)), "s5_c_im": f("s5_c_im", (512, 64)),
        "s5_d": f("s5_d", (4, 128)), "s5_w_glu": f("s5_w_glu", (512, 512)),
        "sgu_norm_g": f("sgu_norm_g", (1, 512)), "sgu_w": f("sgu_w", (4, 128, 128)), "sgu_b": f("sgu_b", (1, 512)),
        "odd_w_in": f("odd_w_in", (D, 2560)), "odd_w_out": f("odd_w_out", (D, D)),
        "pool_w": f("pool_w", (4, 128, 128)), "pool_scale": f("pool_scale", (4, 128)),
        "hgrn_lb": f("hgrn_lb", (8, 128)), "hgrn_onorm_g": f("hgrn_onorm_g", (1, 128)),
    }


def kernel(**inputs):
    x = np.ascontiguousarray(inputs["x"], dtype=np.float32)
    B = x.shape[0]
    if "nc" not in _CACHE:
        _CACHE["nc"] = build(stages=("ffn", "mix0", "mix1"))
    nc = _CACHE["nc"]
    common = make_common(inputs)
    active = [0, 1, 4, 5]
    zeros = {k: np.zeros_like(v) for k, v in common.items()}
    zx = np.zeros_like(x[0])
    in_maps = []
    for c in range(8):
        if c in active:
            m = dict(common)
            m["x"] = x[active.index(c)]
        else:
            m = dict(zeros)
            m["x"] = zx
        in_maps.append(m)
    res = run_bass_kernel_spmd(nc, in_maps, core_ids=list(range(8)))
    out = np.stack([res.results[active[b]]["out"] for b in range(B)], axis=0)
    return out.astype(np.float32)
```
